# Optimizing a Trainium2 kernel written in Bass

```python
import jax, jax.numpy as jnp
from jax import lax
import numpy as np

D_MODEL = 2048
BATCH = 1
SEQ = 8192
DEPTH = 1

CHUNK = 128
A_HEADS = 8
A_HEAD_DIM = D_MODEL // A_HEADS
A_WIDTH = A_HEADS * A_HEAD_DIM
POOL_WINDOWS = (2, 4, 8, 16)
B_GROUPS = len(POOL_WINDOWS)
B_GROUP_DIM = D_MODEL // 8
B_WIDTH = B_GROUPS * B_GROUP_DIM
IN_COLS = 2 * A_WIDTH + B_WIDTH + 2 * D_MODEL
FFN_DIM = 5632
CONV_WIDTH = 3
N_MOD = 6
EPS = 1e-6

kernel_name = "hybrid_sgu_pool_convffn_block"


def rms_norm(x, g):
    xf = x.astype(jnp.float32)
    y = xf * lax.rsqrt(jnp.mean(xf * xf, axis=-1, keepdims=True) + EPS)
    return (y * g.astype(jnp.float32)).astype(x.dtype)


def layer_norm(x, g, b):
    xf = x.astype(jnp.float32)
    mu = jnp.mean(xf, axis=-1, keepdims=True)
    xc = xf - mu
    y = xc * lax.rsqrt(jnp.mean(xc * xc, axis=-1, keepdims=True) + EPS)
    return (y * g.astype(jnp.float32) + b.astype(jnp.float32)).astype(x.dtype)


def spatial_gating(u, v, ln_g, ln_b, w_s, b_s):
    B, S, _ = v.shape
    n_chunks = S // CHUNK
    vn = layer_norm(v, ln_g, ln_b).reshape(B, n_chunks, CHUNK, A_HEADS, A_HEAD_DIM)
    mask = jnp.tril(jnp.ones((CHUNK, CHUNK), dtype=bool))
    w = jnp.where(mask[None], w_s, jnp.zeros_like(w_s)).astype(vn.dtype)
    mixed = jnp.einsum('hij,bnjhd->bnihd', w, vn) + b_s.T[None, None, :, :, None]
    return u * mixed.reshape(B, S, A_WIDTH)


def multiscale_pool(p, w_pool, pool_scale):
    B, S, _ = p.shape
    groups = p.reshape(B, S, B_GROUPS, B_GROUP_DIM)
    pos = jnp.arange(1, S + 1, dtype=jnp.float32)[None, :, None]
    outs = []
    for gi, win in enumerate(POOL_WINDOWS):
        pg = groups[:, :, gi].astype(jnp.float32)
        cs = jnp.cumsum(pg, axis=1)
        lagged = jnp.pad(cs, ((0, 0), (win, 0), (0, 0)))[:, :S]
        mean = (cs - lagged) / jnp.minimum(pos, float(win))
        pooled = (mean - pg).astype(p.dtype)
        outs.append(jnp.einsum('bsc,ce->bse', pooled, w_pool[gi]))
    return jnp.concatenate(outs, axis=-1) * pool_scale


def causal_depthwise_conv(h, w, b):
    S = h.shape[1]
    hp = jnp.pad(h, ((0, 0), (CONV_WIDTH - 1, 0), (0, 0)))
    out = b
    for k in range(CONV_WIDTH):
        out = out + hp[:, k:k + S] * w[k]
    return out


def hybrid_layer(x, c, w_ada, b_ada, g_pre_mix, g_post_mix, w_in, ln_v_g, ln_v_b,
                 w_spatial, b_spatial, w_pool, pool_scale, w_branch_a, w_branch_b,
                 w_out, g_pre_ffn, g_post_ffn, w_up, conv_w, conv_b, w_down):
    mod = jax.nn.silu(c) @ w_ada + b_ada
    shift_m, scale_m, gate_m, shift_f, scale_f, gate_f = [
        m[:, None, :] for m in jnp.split(mod, N_MOD, axis=-1)]

    h = rms_norm(x, g_pre_mix) * (1.0 + scale_m) + shift_m
    proj = h @ w_in
    u, v, p, ga, gb = jnp.split(
        proj, [A_WIDTH, 2 * A_WIDTH, 2 * A_WIDTH + B_WIDTH, 2 * A_WIDTH + B_WIDTH + D_MODEL], axis=-1)
    u = jax.nn.gelu(u)
    v = jax.nn.gelu(v)
    y_a = spatial_gating(u, v, ln_v_g, ln_v_b, w_spatial, b_spatial) @ w_branch_a
    y_b = multiscale_pool(p, w_pool, pool_scale) @ w_branch_b
    merged = jax.nn.sigmoid(ga) * y_a + jax.nn.sigmoid(gb) * y_b
    mix = merged @ w_out
    x = x + gate_m * rms_norm(mix, g_post_mix)

    h = rms_norm(x, g_pre_ffn) * (1.0 + scale_f) + shift_f
    up = causal_depthwise_conv(h @ w_up, conv_w, conv_b)
    a, g = jnp.split(up, 2, axis=-1)
    y = (jax.nn.gelu(a) * g) @ w_down
    x = x + gate_f * rms_norm(y, g_post_ffn)
    return x


def setup_inputs(seed: int = 0) -> dict:
    key = jax.random.key(seed)
    ks = jax.random.split(key, 24)
    f32 = jnp.float32
    L, D = DEPTH, D_MODEL

    def nrm(k, shape, scale):
        return jax.random.normal(k, shape, f32) * scale

    def gain(k, shape):
        return 1.0 + 0.02 * jax.random.normal(k, shape, f32)

    return {
        "x": jax.random.normal(ks[0], (BATCH, SEQ, D), f32),
        "c": jax.random.normal(ks[1], (BATCH, D), f32),
        "w_ada": nrm(ks[2], (L, D, N_MOD * D), 0.5 * D ** -0.5),
        "b_ada": nrm(ks[3], (L, N_MOD * D), 0.01),
        "g_pre_mix": gain(ks[4], (L, D)),
        "g_post_mix": gain(ks[5], (L, D)),
        "w_in": nrm(ks[6], (L, D, IN_COLS), D ** -0.5),
        "ln_v_g": gain(ks[7], (L, A_WIDTH)),
        "ln_v_b": nrm(ks[8], (L, A_WIDTH), 0.01),
        "w_spatial": nrm(ks[9], (L, A_HEADS, CHUNK, CHUNK), CHUNK ** -0.5),
        "b_spatial": gain(ks[10], (L, A_HEADS, CHUNK)),
        "w_pool": nrm(ks[11], (L, B_GROUPS, B_GROUP_DIM, B_GROUP_DIM), B_GROUP_DIM ** -0.5),
        "pool_scale": gain(ks[12], (L, B_WIDTH)),
        "w_branch_a": nrm(ks[13], (L, A_WIDTH, D), A_WIDTH ** -0.5),
        "w_branch_b": nrm(ks[14], (L, B_WIDTH, D), B_WIDTH ** -0.5),
        "w_out": nrm(ks[15], (L, D, D), D ** -0.5),
        "g_pre_ffn": gain(ks[16], (L, D)),
        "g_post_ffn": gain(ks[17], (L, D)),
        "w_up": nrm(ks[18], (L, D, 2 * FFN_DIM), D ** -0.5),
        "conv_w": nrm(ks[19], (L, CONV_WIDTH, 2 * FFN_DIM), CONV_WIDTH ** -0.5),
        "conv_b": nrm(ks[20], (L, 2 * FFN_DIM), 0.01),
        "w_down": nrm(ks[21], (L, FFN_DIM, D), FFN_DIM ** -0.5),
    }


def reference(x, c, w_ada, b_ada, g_pre_mix, g_post_mix, w_in, ln_v_g, ln_v_b,
              w_spatial, b_spatial, w_pool, pool_scale, w_branch_a, w_branch_b,
              w_out, g_pre_ffn, g_post_ffn, w_up, conv_w, conv_b, w_down):
    for l in range(DEPTH):
        x = hybrid_layer(x, c, w_ada[l], b_ada[l], g_pre_mix[l], g_post_mix[l], w_in[l],
                         ln_v_g[l], ln_v_b[l], w_spatial[l], b_spatial[l], w_pool[l],
                         pool_scale[l], w_branch_a[l], w_branch_b[l], w_out[l],
                         g_pre_ffn[l], g_post_ffn[l], w_up[l], conv_w[l], conv_b[l], w_down[l])
    return x
```

```python
import contextlib
import numpy as np
import concourse.bass as bass
import concourse.mybir as mybir
from concourse.bass_utils import run_bass_kernel_spmd

F32 = mybir.dt.float32
BF16 = mybir.dt.bfloat16
AF = mybir.ActivationFunctionType
ALU = mybir.AluOpType
AX = mybir.AxisListType

NCORE = 8
D = 2048
KC = 16
TOK = 1024
TM = 1152
NCH = 9
NE = 1026
FFN = 5632
NG = 4
GB = 11
EPS = 1e-6
POOL_WINDOWS = (2, 4, 8, 16)

V_GPM, V_GQM, V_GPF, V_GQF, V_PS, V_CW, V_CB = 0, 16, 32, 48, 64, 72, 72 + 264
NV = 72 + 264 + 88


class Plan:
    ENGS = ("pe", "act", "dve", "pool", "sp")

    def __init__(self):
        self.q = {e: [] for e in self.ENGS}
        self.cnt = {e: 0 for e in self.ENGS}
        self.waited = {e: {} for e in self.ENGS}
        self.semnames = list(self.ENGS)

    def new_sem(self, name):
        self.cnt[name] = 0
        self.semnames.append(name)
        return name

    def _waits(self, eng, deps):
        for d in deps:
            if d is None:
                continue
            s, v = d
            if v <= 0 or self.waited[eng].get(s, 0) >= v:
                continue
            self.waited[eng][s] = v
            self.q[eng].append(("w", s, v))

    def op(self, eng, fn, deps=(), signal=True):
        self._waits(eng, deps)
        if signal:
            self.cnt[eng] += 1
            self.q[eng].append(("o", fn, eng, 1))
            return (eng, self.cnt[eng])
        self.q[eng].append(("o", fn, None, 0))
        return None

    def dma(self, eng, fn, sem, deps=()):
        self._waits(eng, deps)
        self.cnt[sem] += 16
        self.q[eng].append(("o", fn, sem, 16))
        return (sem, self.cnt[sem])

    def wait(self, eng, deps):
        self._waits(eng, deps)

    def emit(self, block, sems):
        def run(engname, e):
            for it in self.q[engname]:
                if it[0] == "w":
                    e.wait_ge(sems[it[1]], it[2])
                else:
                    ins = it[1](e)
                    if it[2] is not None:
                        ins.then_inc(sems[it[2]], it[3])
        if self.q["pe"]:
            block.tensor(lambda e: run("pe", e))
        if self.q["act"]:
            block.scalar(lambda e: run("act", e))
        if self.q["dve"]:
            block.vector(lambda e: run("dve", e))
        if self.q["pool"]:
            block.gpsimd(lambda e: run("pool", e))
        if self.q["sp"]:
            block.sync(lambda e: run("sp", e))


DEBUG = False


def build_program():
    nc = bass.Bass("TRN2", target_bir_lowering=False)
    dbg = {}
    if DEBUG:
        for nm, n in (("dbg_h", 16 * 1152), ("dbg_vn", 9 * 2048), ("dbg_sg", 16 * 1026), ("dbg_mg", 16 * 1026), ("dbg_q", 8 * 1026)):
            dbg[nm] = nc.dram_tensor(nm, [128, n], BF16, kind="ExternalOutput").ap()

    def din(name, shape, dt=F32):
        return nc.dram_tensor(name, list(shape), dt, kind="ExternalInput").ap()

    xT = din("xT", [D, TM])
    c_col = din("c_col", [128, KC])
    w_ada = din("w_ada", [D, 6 * D])
    b_ada_fm = din("b_ada_fm", [128, 96])
    vec_fm_d = din("vec_fm", [128, NV])
    ln_rows = din("ln_rows", [2, D])
    bs_exp_d = din("bs_exp", [1, 2048])
    wsT_d = din("wsT", [128, 8 * 128])
    tri_d = din("tri", [128, 128])
    poolA_d = din("poolA", [128, 16 * 128])
    wpool_d = din("wpool", [128, 4 * 2 * 256])
    hmask_d = din("hmask", [128, 1])
    w_in = din("w_in", [D, 9216])
    w_ba = din("w_branch_a", [D, D])
    w_bb = din("w_branch_b", [1024, D])
    w_out = din("w_out", [D, D])
    w_up = din("w_up", [D, 2 * FFN])
    w_down = din("w_down", [FFN, D])
    outT = nc.dram_tensor("outT", [D, TOK], F32, kind="ExternalOutput").ap()

    P = Plan()
    P.new_sem("dbg")

    def dump(nm, src2d, deps):
        if DEBUG:
            P.dma("sp", lambda e: e.dma_start(out=dbg[nm][:, :], in_=src2d), "dbg", deps=deps)
    for s in ("ring0", "ring1", "ring2", "ld", "ld2", "ldc", "xin", "st", "fin", "cx0", "cx1", "cx2", "cx3"):
        P.new_sem(s)

    with contextlib.ExitStack() as es:
        def sb(name, shape, dt):
            return es.enter_context(nc.sbuf_tensor(name, list(shape), dt))

        R1 = sb("R1", [128, 34848], BF16)
        RC = sb("RC", [128, 16416], BF16)
        RD = sb("RD", [128, 18432], BF16)
        RING = sb("RING", [128, 3 * 8192], BF16)
        SC = sb("SC", [128, 4104], F32)
        ps = es.enter_context(nc.psum_tensor("ps", [128, 4096], F32))

        def bview(reg, off_b, nbytes, dt):
            a = reg[:, off_b // 2:(off_b + nbytes) // 2]
            return a if dt == BF16 else a.bitcast(F32)

        def v3(ap2, k):
            return ap2.rearrange("p (k n) -> p k n", k=k)

        gu = v3(bview(R1, 0, 32832, BF16), 16)
        gv = v3(bview(R1, 32832, 36864, BF16), 9)
        yb = gu
        poolA = v3(bview(R1, 0, 4096, BF16), 16)
        wpool = bview(R1, 4096, 4096, BF16).rearrange("p (g k e) -> p g k e", g=4, k=2)
        p_tm = v3(bview(R1, 32832, 18432, BF16), 9)
        qT = v3(bview(R1, 32832, 16416, BF16), 8)
        pooledT = v3(bview(R1, 51264, 16416, BF16), 8)
        mixT = v3(bview(R1, 0, 65664, F32), 16)
        yT = v3(bview(R1, 0, 65536, F32), 16)
        ya = v3(bview(RC, 0, 32832, BF16), 16)
        ln_g_bc = bview(RC, 0, 8192, F32)
        ln_b_bc = bview(RC, 8192, 8192, F32)
        WmT = v3(bview(RC, 16384, 2048, BF16), 8)
        bs2 = bview(RC, 18432, 4096, BF16)
        wsT_f = v3(bview(RC, 22528, 4096, F32), 8)
        tri = bview(RC, 26624, 512, F32)
        act = v3(bview(RC, 0, 22528, BF16), 11)
        cxin = [bview(RC, i * 4104, 4104, F32) for i in range(2)]
        hT = v3(bview(RD, 0, 36864, BF16), 16)
        h2T = v3(bview(RD, 0, 32832, BF16), 16)

        ring = [v3(RING[:, i * 8192:(i + 1) * 8192], 16) for i in range(3)]
        ring_flat = [RING[:, i * 8192:(i + 1) * 8192] for i in range(3)]

        mod_fm = sb("mod_fm", [128, 96], F32)
        bada = sb("bada", [128, 96], F32)
        vec = sb("vec", [128, NV], F32)
        derived = sb("derived", [128, 64], F32)
        gs_m, ggm, gsf, ggf = (derived[:, 0:16], derived[:, 16:32], derived[:, 32:48], derived[:, 48:64])
        ccol = sb("ccol", [128, KC], F32)
        scb = sb("scb", [128, KC], BF16)
        epst = sb("epst", [128, 1], F32)
        hmask = sb("hmask_sb", [128, 1], F32)
        ones_bf = sb("ones_bf", [128, 128], BF16)
        onesrow = sb("onesrow", [1, 128], F32)
        ones2 = sb("ones2", [2, 128], BF16)
        modrow = sb("modrow", [1, 512], F32)
        vsum = sb("vsum", [128, 36], F32)
        vstat = sb("vstat", [128, 5 * 9], F32)
        vsq, vmean, vmsq, vvar, vrstd = (vstat[:, i * 9:(i + 1) * 9] for i in range(5))

        def scv(off, n, dt=F32):
            a = SC[:, off:off + n]
            return a if dt == F32 else a.bitcast(BF16)

        sems = {n: es.enter_context(nc.semaphore(n)) for n in P.semnames}
        block = es.enter_context(nc.Block())

        ring_free = [[], [], []]
        ring_next = [0]
        prefetched = []

        def _load(spec):
            w_ap, r0, nk, c0, ncols = spec
            s = ring_next[0] % 3
            ring_next[0] += 1
            dst = ring_flat[s][:, 0:nk * ncols].rearrange("p (k n) -> p k n", k=nk)
            src = w_ap[r0 * 128:(r0 + nk) * 128, c0:c0 + ncols].rearrange("(k p) n -> p k n", p=128)
            tok = P.dma("pool", lambda e: e.dma_start(out=dst, in_=src), "ring%d" % s, deps=ring_free[s])
            ring_free[s] = []
            return s, dst, tok

        def prefetch(w_ap, r0, nk, c0, ncols):
            spec = (id(w_ap.tensor) if hasattr(w_ap, "tensor") else id(w_ap), r0, nk, c0, ncols)
            prefetched.append((spec, _load((w_ap, r0, nk, c0, ncols))))

        def load_piece(w_ap, r0, nk, c0, ncols):
            spec = (id(w_ap.tensor) if hasattr(w_ap, "tensor") else id(w_ap), r0, nk, c0, ncols)
            if prefetched and prefetched[0][0] == spec:
                return prefetched.pop(0)[1]
            return _load((w_ap, r0, nk, c0, ncols))

        bank_free = [[] for _ in range(8)]
        blk_next = [0]
        tm_next = [0]

        def bank(b):
            return ps[:, b * 512:(b + 1) * 512]

        def alloc_blk():
            i = blk_next[0] % 2
            blk_next[0] += 1
            deps = bank_free[3 * i] + bank_free[3 * i + 1] + bank_free[3 * i + 2]
            return i, deps

        def blk_view(i, n):
            return ps[:, 3 * i * 512:(3 * i + 3) * 512].rearrange("p (k n) -> p k n", k=3)[:, :, 0:n]

        def set_blk_free(i, toks):
            for b in range(3):
                bank_free[3 * i + b] = list(toks)

        def alloc_tm():
            b = tm_next[0] % 6
            tm_next[0] += 1
            return b, bank_free[b]

        pe_hooks = []

        def fm_block(lhsTs, rhs_fn, pieces, deps):
            i, bdeps = alloc_blk()
            nk = len(lhsTs)
            tok = None
            first = True
            for kc in range(nk):
                for pi, (c0, n) in enumerate(pieces):
                    out_ap = bank(3 * i + pi)[:, 0:n]
                    last = (kc == nk - 1) and (pi == len(pieces) - 1)
                    tok = P.op("pe", lambda e, o=out_ap, l=lhsTs[kc], r=rhs_fn(kc, c0, n), st=(kc == 0), sp=(kc == nk - 1):
                               e.matmul(o, lhsT=l, rhs=r, start=st, stop=sp),
                               deps=(list(deps) + bdeps) if first else (), signal=last)
                    first = False
            hooks = list(pe_hooks)
            del pe_hooks[:]
            for h in hooks:
                h()
            return i, tok

        PCS_H = [(126, 342), (468, 342), (810, 342)]
        PCS_E = [(0, 342), (342, 342), (684, 342)]
        PCS_T = [(0, 512), (512, 512)]

        def all_ad():
            return [("act", P.cnt["act"]), ("dve", P.cnt["dve"]), ("pool", P.cnt["pool"])]

        t_c = P.dma("sp", lambda e: e.dma_start(out=ccol[:], in_=c_col[:, :]), "ldc")
        P.dma("sp", lambda e: e.dma_start(out=bada[:], in_=b_ada_fm[:, :]), "ldc")
        P.dma("sp", lambda e: e.dma_start(out=vec[:], in_=vec_fm_d[:, :]), "ldc")
        P.dma("sp", lambda e: e.dma_start(out=hmask[:], in_=hmask_d[:, :]), "ldc")
        ldc_all = ("ldc", P.cnt["ldc"])
        t_ms = P.op("dve", lambda e: e.memset(epst[:], EPS))
        P.op("dve", lambda e: e.memset(ones_bf[:], 1.0))
        P.op("dve", lambda e: e.memset(ones2[:], 1.0))
        t_ones = P.op("dve", lambda e: e.memset(onesrow[:], 1.0))
        P.op("dve", lambda e: e.memset(vsum[:], 0.0))
        t_vz = P.op("dve", lambda e: e.memset(vstat[:], 0.0))
        t_sc = P.op("act", lambda e: e.activation(out=scb[:], in_=ccol[:], func=AF.Silu), deps=[ldc_all])

        xres = [bview(R1, kc * 4608, 4608, F32) for kc in range(15)] + [bview(RC, 0, 4608, F32)]
        rstd0 = bview(RC, 4608, 4608, F32)
        sqb = [SC[:, i * 576:(i + 1) * 576].bitcast(BF16) for i in range(2)]
        tfx = [SC[:, 1152 + i * 1152:1152 + (i + 1) * 1152] for i in range(2)]
        for kc in range(KC):
            P.dma("sp", lambda e, kc=kc: e.dma_start(out=xres[kc], in_=xT[kc * 128:(kc + 1) * 128, :]), "xin")
        t_xl = [("xin", P.cnt["xin"])] * KC
        S0_PCS = [(0, 384), (384, 384), (768, 384)]
        st_i, st_deps = alloc_blk()
        stat_tok = None
        sq_free = [[], []]
        for kc in range(KC):
            b = kc % 2
            t_sq = P.op("act", lambda e, b=b, kc=kc: e.activation(out=sqb[b], in_=xres[kc], func=AF.Square),
                        deps=[t_xl[kc]] + sq_free[b])
            for pi, (c0, n) in enumerate(S0_PCS):
                stat_tok = P.op("pe", lambda e, b=b, pi=pi, c0=c0, n=n, kc=kc:
                                e.matmul(bank(3 * st_i + pi)[:, 0:n], lhsT=ones_bf[:], rhs=sqb[b][:, c0:c0 + n],
                                         start=(kc == 0), stop=(kc == KC - 1)),
                                deps=[t_sq] + (st_deps if kc == 0 else []), signal=(pi == 2))
            sq_free[b] = [stat_tok]
        t_r0 = P.op("act", lambda e: e.activation(out=v3(rstd0, 3), in_=blk_view(st_i, 384), func=AF.Sqrt,
                                                  bias=epst[:], scale=1.0 / D), deps=[stat_tok, t_ms])
        set_blk_free(st_i, [t_r0])
        t_r0 = P.op("dve", lambda e: e.reciprocal(out=rstd0, in_=rstd0), deps=[t_r0])

        mod_tok = [None] * 24
        modrow_free = [[]]
        pm_free = [[]]

        def mod_piece(j):
            s, wt, t_w = load_piece(w_ada, 0, KC, j * 512, 512)
            tok = None
            for kc in range(KC):
                tok = P.op("pe", lambda e, kc=kc, wt=wt: e.matmul(ps[0:1, 3584:4096], lhsT=scb[:, kc:kc + 1], rhs=wt[:, kc, :],
                                                                  start=(kc == 0), stop=(kc == KC - 1)),
                           deps=[t_w, t_sc] + pm_free[0] + bank_free[7] if kc == 0 else (), signal=(kc == KC - 1))
            ring_free[s].append(tok)
            mr = modrow[0:1, 0:512]
            t_cp = P.op("dve", lambda e, mr=mr: e.tensor_copy(out=mr, in_=ps[0:1, 3584:4096]), deps=[tok] + modrow_free[0])
            pm_free[0] = [t_cp]
            bank_free[7] = [t_cp]
            t4 = None
            for q in range(4):
                t4 = P.op("pe", lambda e, q=q, mr=mr: e.matmul(ps[:, 3072 + q:3072 + q + 1], lhsT=mr[0:1, q * 128:(q + 1) * 128],
                                                               rhs=onesrow[0:1, 0:1], start=True, stop=True),
                          deps=[t_cp, t_ones] + bank_free[6] if q == 0 else (), signal=(q == 3))
            modrow_free[0] = [t4]
            t_m = P.op("dve", lambda e, j=j: e.tensor_tensor(out=mod_fm[:, 4 * j:4 * j + 4], in0=ps[:, 3072:3076],
                                                            in1=bada[:, 4 * j:4 * j + 4], op=ALU.add), deps=[t4, ldc_all])
            bank_free[6] = [t_m]
            mod_tok[j] = t_m

        for j in range(8):
            mod_piece(j)
        t_gsm = P.op("dve", lambda e: e.scalar_tensor_tensor(out=gs_m, in0=mod_fm[:, 16:32], scalar=1.0, in1=vec[:, V_GPM:V_GPM + 16],
                                                             op0=ALU.add, op1=ALU.mult), deps=[mod_tok[7], ldc_all])
        hT_ready = None
        tfx_free = [[], []]
        for kc in range(KC):
            b = kc % 2
            t_a = P.op("dve", lambda e, kc=kc, b=b: e.scalar_tensor_tensor(out=tfx[b], in0=xres[kc], scalar=gs_m[:, kc:kc + 1], in1=rstd0,
                                                                           op0=ALU.mult, op1=ALU.mult), deps=[t_xl[kc], t_gsm, t_r0] + tfx_free[b])
            hT_ready = P.op("act", lambda e, kc=kc, b=b: e.activation(out=hT[:, kc, :], in_=tfx[b], func=AF.Identity,
                                                                      bias=mod_fm[:, kc:kc + 1], scale=1.0), deps=[t_a, mod_tok[3]])
            tfx_free[b] = [hT_ready]
        s0_done = all_ad()
        dump("dbg_h", RD[:, 0:16 * 1152], [hT_ready])

        P.dma("sp", lambda e: e.dma_start(out=ln_g_bc, in_=ln_rows[0:1, :].broadcast_to([128, D])), "ld", deps=s0_done)
        P.dma("sp", lambda e: e.dma_start(out=ln_b_bc, in_=ln_rows[1:2, :].broadcast_to([128, D])), "ld")
        P.dma("sp", lambda e: e.dma_start(out=wsT_f.rearrange("p k n -> p (k n)"), in_=wsT_d[:, :]), "ld")
        P.dma("sp", lambda e: e.dma_start(out=tri, in_=tri_d[:, :]), "ld")
        bsf = SC[0:1, 0:2048]
        bslo = SC[0:1, 2048:3072].bitcast(BF16)
        P.dma("sp", lambda e: e.dma_start(out=bsf, in_=bs_exp_d[:, :]), "ld")
        ld_s3 = ("ld", P.cnt["ld"])
        t_wm = None
        for h in range(8):
            t_wm = P.op("dve", lambda e, h=h: e.tensor_tensor(out=WmT[:, h, :], in0=wsT_f[:, h, :], in1=tri, op=ALU.mult), deps=[ld_s3])

        mod_queue = list(range(8, 24))
        mod_ctr = [0]

        def maybe_mod(every=2, limit=20):
            mod_ctr[0] += 1
            if mod_queue and mod_queue[0] < limit and mod_ctr[0] % every == 0:
                mod_piece(mod_queue.pop(0))

        gv_ready = None
        for g in range(4):
            s, wt, t_w = load_piece(w_in, 0, KC, 2048 + g * 512, 512)
            if g == 0:
                t_bh = P.dma("pool", lambda e: e.dma_start(out=bs2[0:1, :], in_=bs_exp_d[:, :]), "ld2", deps=s0_done)
            for c in range(NCH):
                b, bdeps = alloc_tm()
                tok = None
                for kc in range(KC):
                    tok = P.op("pe", lambda e, b=b, kc=kc, c=c, wt=wt: e.matmul(bank(b), lhsT=hT[:, kc, c * 128:(c + 1) * 128], rhs=wt[:, kc, :],
                                                                                start=(kc == 0), stop=(kc == KC - 1)),
                               deps=[t_w, hT_ready] + bdeps if kc == 0 else (), signal=(kc == KC - 1))
                gv_ready = P.op("act", lambda e, b=b, c=c, g=g: e.activation(out=gv[:, c, g * 512:(g + 1) * 512], in_=bank(b), func=AF.Gelu_apprx_tanh,
                                                                             accum_out=vsum[:, c * 4 + g:c * 4 + g + 1]), deps=[tok, t_vz] + s0_done)
                bank_free[b] = [gv_ready]
            ring_free[s].append(tok)
            maybe_mod()
        t_lo = P.op("dve", lambda e: e.tensor_tensor(out=bslo, in0=bsf, in1=bs2[0:1, :], op=ALU.subtract), deps=[ld_s3, t_bh])
        t_bl = P.dma("sp", lambda e: e.dma_start(out=bs2[1:2, :], in_=bslo), "ld2", deps=[t_lo])
        ld2_all = ("ld2", P.cnt["ld2"])

        junk = SC[:, 1024:2048].bitcast(BF16)
        t_q = None
        for c in range(NCH):
            t_q = P.op("act", lambda e, c=c: e.activation(out=junk, in_=gv[:, c, :], func=AF.Square, accum_out=vsq[:, c:c + 1]),
                       deps=[gv_ready, hT_ready, t_lo, t_bl])
        t = P.op("dve", lambda e: e.tensor_reduce(out=vmean, in_=vsum[:].rearrange("p (c g) -> p c g", g=4), axis=AX.X, op=ALU.add), deps=[gv_ready])
        t = P.op("dve", lambda e: e.tensor_scalar(out=vmean, in0=vmean, scalar1=1.0 / D, scalar2=None, op0=ALU.mult), deps=[t])
        t = P.op("dve", lambda e: e.tensor_tensor(out=vmsq, in0=vmean, in1=vmean, op=ALU.mult), deps=[t])
        t = P.op("dve", lambda e: e.scalar_tensor_tensor(out=vvar, in0=vsq, scalar=1.0 / D, in1=vmsq, op0=ALU.mult, op1=ALU.subtract), deps=[t, t_q])
        t = P.op("act", lambda e: e.activation(out=vvar, in_=vvar, func=AF.Sqrt, bias=epst[:], scale=1.0), deps=[t])
        t_vr = P.op("dve", lambda e: e.reciprocal(out=vrstd, in_=vvar), deps=[t])
        vn_ready = [None] * NCH
        for c in range(NCH):
            t1 = P.op("dve", lambda e, c=c: e.tensor_scalar(out=gv[:, c, :], in0=gv[:, c, :], scalar1=vmean[:, c:c + 1], scalar2=vrstd[:, c:c + 1],
                                                            op0=ALU.subtract, op1=ALU.mult), deps=[t_vr, t_q])
            t2 = P.op("dve", lambda e, c=c: e.tensor_tensor(out=gv[:, c, :], in0=gv[:, c, :], in1=ln_g_bc, op=ALU.mult), deps=[t1, ld_s3])
            vn_ready[c] = P.op("dve", lambda e, c=c: e.tensor_tensor(out=gv[:, c, :], in0=gv[:, c, :], in1=ln_b_bc, op=ALU.add), deps=[t2])

        dump("dbg_vn", R1[:, 16416:16416 + 9 * 2048], [vn_ready[8]])
        evac_rr = [0]
        gu_ready = None
        for g in range(4):
            s, wt, t_w = load_piece(w_in, 0, KC, g * 512, 512)
            for bl in range(4):
                blk = g * 4 + bl
                i, tok = fm_block([wt[:, kc, bl * 128:(bl + 1) * 128] for kc in range(KC)],
                                  lambda kc, c0, n: hT[:, kc, c0:c0 + n], PCS_H, [t_w, hT_ready])
                gu_ready = P.op("act", lambda e, i=i, blk=blk: e.activation(out=v3(gu[:, blk, :], 3), in_=blk_view(i, 342), func=AF.Gelu_apprx_tanh),
                                deps=[tok] + s0_done)
                set_blk_free(i, [gu_ready])
            ring_free[s].append(tok)
            maybe_mod()

        sg_ready = None
        for c in range(NCH):
            for q in range(4):
                b, bdeps = alloc_tm()
                P.op("pe", lambda e, b=b, q=q: e.matmul(bank(b), lhsT=ones2[0:2, :], rhs=bs2[0:2, q * 512:(q + 1) * 512], start=True, stop=False),
                     deps=[ld2_all, vn_ready[c], t_wm] + bdeps, signal=False)
                tok = None
                for f in range(4):
                    fb = q * 4 + f
                    h = fb // 2
                    tok = P.op("pe", lambda e, b=b, f=f, fb=fb, c=c, h=h: e.matmul(bank(b)[:, f * 128:(f + 1) * 128], lhsT=gv[:, c, fb * 128:(fb + 1) * 128],
                                                                                  rhs=WmT[:, h, :], start=False, stop=(f == 3)), signal=(f == 3))
                pv = bank(b).rearrange("p (k n) -> p k n", k=4)
                if c == 0:
                    o = gu[:, q * 4:q * 4 + 4, 0:2]
                    i0 = pv[:, :, 126:128]
                else:
                    o = gu[:, q * 4:q * 4 + 4, 2 + (c - 1) * 128:2 + c * 128]
                    i0 = pv
                sg_ready = P.op("dve", lambda e, o=o, i0=i0: e.tensor_tensor(out=o, in0=i0, in1=o, op=ALU.mult), deps=[tok, gu_ready])
                bank_free[b] = [sg_ready]

        dump("dbg_sg", R1[:, 0:16 * 1026], [sg_ready])
        def evac_copy(i, out3, deps):
            k = evac_rr[0] % 2
            evac_rr[0] += 1
            if k == 0:
                t = P.op("act", lambda e: e.activation(out=out3, in_=blk_view(i, 342), func=AF.Copy), deps=deps)
            else:
                t = P.op("dve", lambda e: e.tensor_copy(out=out3, in_=blk_view(i, 342)), deps=deps)
            set_blk_free(i, [t])
            return t

        ya_ready = None
        for g in range(4):
            s, wt, t_w = load_piece(w_ba, 0, KC, g * 512, 512)
            for bl in range(4):
                blk = g * 4 + bl
                i, tok = fm_block([wt[:, kc, bl * 128:(bl + 1) * 128] for kc in range(KC)],
                                  lambda kc, c0, n: gu[:, kc, c0:c0 + n], PCS_E, [t_w, sg_ready])
                ya_ready = evac_copy(i, v3(ya[:, blk, :], 3), [tok, sg_ready])
            ring_free[s].append(tok)
            maybe_mod()
        s4_last_pe = tok
        ya_all = all_ad()

        sgt = [SC[:, i * 1026:(i + 1) * 1026] for i in range(2)]
        tmpm = [SC[:, 2052 + i * 1026:2052 + (i + 1) * 1026] for i in range(2)]
        sgt_free = [[], []]
        tmp_free = [[], []]
        n9 = [0]
        st9 = {"merged": None, "last_pe": None, "yb_all": []}

        def gate_piece(stage, g):
            s, wt, t_w = load_piece(w_in, 0, KC, 5120 + stage * 2048 + g * 512, 512)
            for bl in range(4):
                blk = g * 4 + bl
                i, tok = fm_block([wt[:, kc, bl * 128:(bl + 1) * 128] for kc in range(KC)],
                                  lambda kc, c0, n: hT[:, kc, c0:c0 + n], PCS_H, [t_w])
                k = n9[0] % 2
                n9[0] += 1
                t_s = P.op("act", lambda e, i=i, k=k: e.activation(out=v3(sgt[k], 3), in_=blk_view(i, 342), func=AF.Sigmoid), deps=[tok, t_q] + sgt_free[k])
                set_blk_free(i, [t_s])
                if stage == 0:
                    st9["merged"] = P.op("dve", lambda e, blk=blk, k=k: e.tensor_tensor(out=ya[:, blk, :], in0=ya[:, blk, :], in1=sgt[k], op=ALU.mult),
                                         deps=[t_s] + ya_all)
                    sgt_free[k] = [st9["merged"]]
                else:
                    t_m = P.op("dve", lambda e, blk=blk, k=k: e.tensor_tensor(out=tmpm[k], in0=yb[:, blk, :], in1=sgt[k], op=ALU.mult),
                               deps=[t_s] + st9["yb_all"] + tmp_free[k])
                    sgt_free[k] = [t_m]
                    st9["merged"] = P.op("dve", lambda e, blk=blk, k=k: e.tensor_tensor(out=ya[:, blk, :], in0=ya[:, blk, :], in1=tmpm[k], op=ALU.add),
                                         deps=[t_m])
                    tmp_free[k] = [st9["merged"]]
                st9["last_pe"] = tok
            ring_free[s].append(tok)
            maybe_mod()

        for g in range(2):
            s, wt, t_w = load_piece(w_in, 0, KC, 4096 + g * 512, 512)
            for c in range(NCH):
                b, bdeps = alloc_tm()
                tok = None
                for kc in range(KC):
                    tok = P.op("pe", lambda e, b=b, kc=kc, c=c, wt=wt: e.matmul(bank(b), lhsT=hT[:, kc, c * 128:(c + 1) * 128], rhs=wt[:, kc, :],
                                                                                start=(kc == 0), stop=(kc == KC - 1)),
                               deps=[t_w, sg_ready] + bdeps if kc == 0 else (), signal=(kc == KC - 1))
                k = evac_rr[0] % 2
                evac_rr[0] += 1
                if k == 0:
                    t_p = P.op("act", lambda e, b=b, c=c, g=g: e.activation(out=p_tm[:, c, g * 512:(g + 1) * 512], in_=bank(b), func=AF.Copy), deps=[tok, sg_ready])
                else:
                    t_p = P.op("dve", lambda e, b=b, c=c, g=g: e.tensor_copy(out=p_tm[:, c, g * 512:(g + 1) * 512], in_=bank(b)), deps=[tok, sg_ready])
                bank_free[b] = [t_p]
            ring_free[s].append(tok)
            maybe_mod()
        p_all = all_ad()
        P.dma("pool", lambda e: e.dma_start(out=poolA.rearrange("p k n -> p (k n)"), in_=poolA_d[:, :]), "ld", deps=[s4_last_pe])
        P.dma("pool", lambda e: e.dma_start(out=wpool.rearrange("p g k e -> p (g k e)"), in_=wpool_d[:, :]), "ld", deps=[s4_last_pe])
        ld_s6 = ("ld", P.cnt["ld"])
        gate_piece(0, 0)
        pooled_ready = None
        for c in range(NCH):
            kind = 0 if c == 1 else 2
            for q in range(2):
                b, bdeps = alloc_tm()
                tok = None
                for f in range(4):
                    fb = q * 4 + f
                    gi = fb // 2
                    two = c >= 1
                    tok = P.op("pe", lambda e, b=b, f=f, fb=fb, c=c, gi=gi, kind=kind, two=two:
                               e.matmul(bank(b)[:, f * 128:(f + 1) * 128], lhsT=p_tm[:, c, fb * 128:(fb + 1) * 128],
                                        rhs=poolA[:, kind * 4 + gi, :], start=True, stop=(not two)),
                               deps=p_all + [ld_s6] + bdeps if f == 0 else (), signal=(f == 3 and not two))
                    if two:
                        tok = P.op("pe", lambda e, b=b, f=f, fb=fb, c=c, gi=gi, kind=kind:
                                   e.matmul(bank(b)[:, f * 128:(f + 1) * 128], lhsT=p_tm[:, c - 1, fb * 128:(fb + 1) * 128],
                                            rhs=poolA[:, (kind + 1) * 4 + gi, :], start=False, stop=True), signal=(f == 3))
                pv = bank(b).rearrange("p (k n) -> p k n", k=4)
                if c == 0:
                    o = pooledT[:, q * 4:q * 4 + 4, 0:2]
                    i0 = pv[:, :, 126:128]
                else:
                    o = pooledT[:, q * 4:q * 4 + 4, 2 + (c - 1) * 128:2 + c * 128]
                    i0 = pv
                pooled_ready = P.op("act", lambda e, o=o, i0=i0: e.activation(out=o, in_=i0, func=AF.Copy), deps=[tok])
                bank_free[b] = [pooled_ready]
        gate_piece(0, 1)
        q_ready = None
        for gi in range(4):
            for eb in range(2):
                i, tok = fm_block([wpool[:, gi, k2, eb * 128:(eb + 1) * 128] for k2 in range(2)],
                                  lambda k2, c0, n, gi=gi: pooledT[:, gi * 2 + k2, c0:c0 + n], PCS_E, [pooled_ready, ld_s6])
                col = V_PS + gi * 2 + eb
                q_ready = P.op("act", lambda e, i=i, gi=gi, eb=eb, col=col: e.activation(out=v3(qT[:, gi * 2 + eb, :], 3), in_=blk_view(i, 342), func=AF.Identity,
                                                                                         scale=vec[:, col:col + 1]), deps=[tok, ldc_all, pooled_ready])
                set_blk_free(i, [q_ready])
        dump("dbg_q", R1[:, 16416:16416 + 8 * 1026], [q_ready])
        gate_piece(0, 2)
        for g in range(4):
            s, wt, t_w = load_piece(w_bb, 0, 8, g * 512, 512)
            for bl in range(4):
                blk = g * 4 + bl
                i, tok = fm_block([wt[:, kc, bl * 128:(bl + 1) * 128] for kc in range(8)],
                                  lambda kc, c0, n: qT[:, kc, c0:c0 + n], PCS_E, [t_w, q_ready])
                evac_copy(i, v3(yb[:, blk, :], 3), [tok, q_ready])
            ring_free[s].append(tok)
            maybe_mod()
        st9["yb_all"] = all_ad()
        gate_piece(0, 3)
        for g in range(4):
            gate_piece(1, g)
        while mod_queue and mod_queue[0] < 20:
            mod_piece(mod_queue.pop(0))
        merged_ready = st9["merged"]
        dump("dbg_mg", RC[:, 0:16 * 1026], [merged_ready])
        hT_dead = st9["last_pe"]

        sq_all = v3(bview(RD, 0, 32832, BF16), 16)
        rstd_e = SC[:, 0:1026]
        cxs = [SC[:, 1026 * (1 + i):1026 * (2 + i)] for i in range(3)]
        cx_free = [[], [], []]
        s10_done = all_ad()
        t_cx = [None] * KC
        for blk in range(3):
            t_cx[blk] = P.dma("sp", lambda e, blk=blk: e.dma_start(out=cxs[blk % 3], in_=xT[blk * 128:(blk + 1) * 128, 126:TM]), "cx%d" % (blk % 3), deps=s10_done)

        def square_to(eng, dst, src, deps):
            if eng == "act":
                return P.op("act", lambda e: e.activation(out=dst, in_=src, func=AF.Square), deps=deps)
            return P.op(eng, lambda e: e.tensor_tensor(out=dst, in0=src, in1=src, op=ALU.mult), deps=deps)

        sq_tok = [None] * KC
        for g in range(4):
            s, wt, t_w = load_piece(w_out, 0, KC, g * 512, 512)
            for bl in range(4):
                blk = g * 4 + bl
                i, tok = fm_block([wt[:, kc, bl * 128:(bl + 1) * 128] for kc in range(KC)],
                                  lambda kc, c0, n: ya[:, kc, c0:c0 + n], PCS_E, [t_w, merged_ready])
                eng = "act" if evac_rr[0] % 2 == 0 else "dve"
                t_e = evac_copy(i, v3(mixT[:, blk, :], 3), [tok, merged_ready])
                sq_tok[blk] = square_to(eng, sq_all[:, blk, :], mixT[:, blk, :], [t_e, hT_dead])
            ring_free[s].append(tok)
        merged_dead = tok

        def ffn_up_specs(k, half):
            col_base = half * FFN + k * GB * 128
            return [(w_up, 0, KC, col_base + pc0, pn) for (pc0, pn) in ((0, 512), (512, 512), (1024, 384))]
        for sp_ in ffn_up_specs(0, 0):
            prefetch(*sp_)

        def stats_matmuls(n, pieces, dep_list):
            i, bdeps = alloc_blk()
            tok = None
            for kc in range(KC):
                for pi, (c0, m) in enumerate(pieces):
                    tok = P.op("pe", lambda e, pi=pi, c0=c0, m=m, kc=kc: e.matmul(bank(3 * i + pi)[:, 0:m], lhsT=ones_bf[:], rhs=sq_all[:, kc, c0:c0 + m],
                                                                                  start=(kc == 0), stop=(kc == KC - 1)),
                               deps=[dep_list[kc]] + (bdeps if kc == 0 else []), signal=(kc == KC - 1 and pi == len(pieces) - 1))
            m = pieces[0][1]
            np_ = len(pieces)
            src = ps[:, 3 * i * 512:(3 * i + np_) * 512].rearrange("p (k n) -> p k n", k=np_)[:, :, 0:m]
            t_r = P.op("act", lambda e: e.activation(out=v3(rstd_e[:, 0:n], np_), in_=src, func=AF.Sqrt, bias=epst[:], scale=1.0 / D), deps=[tok] + s10_done)
            set_blk_free(i, [t_r])
            return P.op("dve", lambda e: e.reciprocal(out=rstd_e[:, 0:n], in_=rstd_e[:, 0:n]), deps=[t_r])

        t_rm = stats_matmuls(NE, PCS_E, sq_tok)
        t_ggm = P.op("dve", lambda e: e.tensor_tensor(out=ggm, in0=mod_fm[:, 32:48], in1=vec[:, V_GQM:V_GQM + 16], op=ALU.mult),
                     deps=[mod_tok[11], ldc_all])
        xmid_tok = [None] * KC
        sq2_tok = [None] * KC
        for blk in range(KC):
            b = blk % 3
            if t_cx[blk] is None:
                t_cx[blk] = P.dma("sp", lambda e, blk=blk, b=b: e.dma_start(out=cxs[b], in_=xT[blk * 128:(blk + 1) * 128, 126:TM]), "cx%d" % b, deps=cx_free[b])
            t_a = P.op("dve", lambda e, blk=blk: e.scalar_tensor_tensor(out=mixT[:, blk, :], in0=mixT[:, blk, :], scalar=ggm[:, blk:blk + 1], in1=rstd_e,
                                                                        op0=ALU.mult, op1=ALU.mult), deps=[t_rm, t_ggm])
            xmid_tok[blk] = P.op("dve", lambda e, blk=blk, b=b: e.tensor_tensor(out=mixT[:, blk, :], in0=mixT[:, blk, :], in1=cxs[b], op=ALU.add), deps=[t_a, t_cx[blk]])
            cx_free[b] = [xmid_tok[blk]]
            sq2_tok[blk] = square_to("act", sq_all[:, blk, :], mixT[:, blk, :], [xmid_tok[blk], t_rm])
        xmid_ready = xmid_tok[KC - 1]
        spill = None
        for blk in range(KC):
            spill = P.dma("sp", lambda e, blk=blk: e.dma_start(out=outT[blk * 128:(blk + 1) * 128, :], in_=mixT[:, blk, 2:NE]), "st", deps=[xmid_tok[blk]])
        spill_all = ("st", P.cnt["st"])
        t_r2 = stats_matmuls(NE, PCS_E, sq2_tok)
        t_gsf = P.op("dve", lambda e: e.scalar_tensor_tensor(out=gsf, in0=mod_fm[:, 64:80], scalar=1.0, in1=vec[:, V_GPF:V_GPF + 16],
                                                             op0=ALU.add, op1=ALU.mult), deps=[mod_tok[19], ldc_all])
        tf = [bview(RC, i * 4104, 4104, F32) for i in range(4)]
        tf_free = [[], [], [], []]
        h2_tok = None
        for blk in range(KC):
            b = blk % 4
            eng = "dve"
            if eng == "dve":
                t_a = P.op("dve", lambda e, blk=blk, b=b: e.scalar_tensor_tensor(out=tf[b], in0=mixT[:, blk, :], scalar=gsf[:, blk:blk + 1], in1=rstd_e,
                                                                                 op0=ALU.mult, op1=ALU.mult), deps=[t_r2, t_gsf, merged_dead] + tf_free[b])
                h2_tok = P.op("act", lambda e, blk=blk, b=b: e.activation(out=h2T[:, blk, :], in_=tf[b], func=AF.Identity,
                                                                          bias=mod_fm[:, 48 + blk:49 + blk], scale=1.0),
                              deps=[t_a, mod_tok[15], t_r2])
            else:
                t_a = P.op("pool", lambda e, blk=blk, b=b: e.tensor_tensor(out=tf[b], in0=mixT[:, blk, :], in1=rstd_e, op=ALU.mult),
                           deps=[t_r2, merged_dead] + tf_free[b])
                h2_tok = P.op("act", lambda e, blk=blk, b=b: e.activation(out=h2T[:, blk, :], in_=tf[b], func=AF.Identity,
                                                                          bias=mod_fm[:, 48 + blk:49 + blk], scale=gsf[:, blk:blk + 1]),
                              deps=[t_a, mod_tok[15], t_r2, t_gsf])
            tf_free[b] = [h2_tok]
        h2_ready = P.op("dve", lambda e: e.tensor_scalar(out=h2T[:, :, 0:2], in0=h2T[:, :, 0:2], scalar1=hmask[:, 0:1], scalar2=None, op0=ALU.mult),
                        deps=[h2_tok, ldc_all])
        mid_done = all_ad()

        stg = [SC[:, i * 1026:(i + 1) * 1026] for i in range(2)]
        cb = [SC[:, 2052 + i * 1024:2052 + (i + 1) * 1024] for i in range(2)]
        sqy = [SC[:, i * 512:(i + 1) * 512].bitcast(BF16) for i in range(2)]
        stg_free = [[], []]
        cb_free = [[], []]
        sqy_free = [[], []]
        nblk = [0]
        y_ready = [None] * KC
        act_free = []
        conv_last = [None]
        ystat_tok = [None]
        for k in range(NG):
            act_ready = [None] * GB
            for half in (0, 1):
                done = 0
                for sp_ in ffn_up_specs(k, half):
                    s, wt, t_w = load_piece(*sp_)
                    pn = sp_[4]
                    for bl in range(pn // 128):
                        m = done
                        done += 1
                        mg = k * GB + m
                        i, tok = fm_block([wt[:, kc, bl * 128:(bl + 1) * 128] for kc in range(KC)],
                                          lambda kc, c0, n: h2T[:, kc, c0:c0 + n], PCS_E, [t_w, h2_ready])
                        b = nblk[0] % 2
                        nblk[0] += 1
                        cwc = V_CW + (half * 44 + mg) * 3
                        cbc = V_CB + half * 44 + mg
                        t_st = P.op("act", lambda e, i=i, b=b: e.activation(out=v3(stg[b], 3), in_=blk_view(i, 342), func=AF.Copy),
                                    deps=[tok] + stg_free[b] + mid_done)
                        set_blk_free(i, [t_st])
                        t_c0 = P.op("act", lambda e, b=b, cwc=cwc, cbc=cbc: e.activation(out=cb[b], in_=stg[b][:, 2:NE], func=AF.Identity,
                                                                                        bias=vec[:, cbc:cbc + 1], scale=vec[:, cwc + 2:cwc + 3]),
                                    deps=[t_st, ldc_all] + cb_free[b])
                        t_c1 = P.op("dve", lambda e, b=b, cwc=cwc: e.scalar_tensor_tensor(out=cb[b], in0=stg[b][:, 1:NE - 1], scalar=vec[:, cwc + 1:cwc + 2], in1=cb[b],
                                                                                          op0=ALU.mult, op1=ALU.add), deps=[t_c0])
                        t_c2 = P.op("dve", lambda e, b=b, cwc=cwc: e.scalar_tensor_tensor(out=cb[b], in0=stg[b][:, 0:NE - 2], scalar=vec[:, cwc:cwc + 1], in1=cb[b],
                                                                                          op0=ALU.mult, op1=ALU.add), deps=[t_c1])
                        stg_free[b] = [t_c2]
                        if half == 0:
                            t_o = P.op("act", lambda e, b=b, m=m: e.activation(out=act[:, m, :], in_=cb[b], func=AF.Gelu_apprx_tanh),
                                       deps=[t_c2] + act_free + mid_done)
                        else:
                            t_o = P.op("dve", lambda e, b=b, m=m: e.tensor_tensor(out=act[:, m, :], in0=act[:, m, :], in1=cb[b], op=ALU.mult),
                                       deps=[t_c2, act_ready[m]])
                        cb_free[b] = [t_o]
                        act_ready[m] = t_o
                        conv_last[0] = t_o
                    ring_free[s].append(tok)
                    if k == 0 and mod_queue:
                        mod_piece(mod_queue.pop(0))
            act_all = list(act_ready)
            last_pe = None
            last_group = (k == NG - 1)
            if last_group:
                ys_first = [True]
            for q in range(4):
                s, wt, t_w = load_piece(w_down, k * GB, GB, q * 512, 512)
                for ob in range(4):
                    o = q * 4 + ob
                    i, tok = fm_block([wt[:, kc, ob * 128:(ob + 1) * 128] for kc in range(GB)],
                                      lambda kc, c0, n: act[:, kc, c0:c0 + n], PCS_T, [t_w] + act_all)
                    src = ps[:, 3 * i * 512:(3 * i + 2) * 512]
                    if k == 0:
                        if o % 2:
                            t_y = P.op("dve", lambda e, o=o, src=src: e.tensor_copy(out=yT[:, o, :], in_=src), deps=[tok, spill_all] + mid_done)
                        else:
                            t_y = P.op("act", lambda e, o=o, src=src: e.activation(out=yT[:, o, :], in_=src, func=AF.Copy), deps=[tok, spill_all] + mid_done)
                    else:
                        t_y = P.op("dve", lambda e, o=o, src=src: e.tensor_tensor(out=yT[:, o, :], in0=src, in1=yT[:, o, :], op=ALU.add),
                                   deps=[tok, y_ready[o]])
                    set_blk_free(i, [t_y])
                    y_ready[o] = t_y
                    last_pe = tok
                    if last_group:
                        b2 = o % 2
                        t_sq = P.op("act", lambda e, o=o, b2=b2: e.activation(out=sqy[b2], in_=yT[:, o, :], func=AF.Square),
                                    deps=[t_y, conv_last[0]] + sqy_free[b2])

                        def hook(o=o, b2=b2, t_sq=t_sq):
                            tk = None
                            for pi in range(2):
                                tk = P.op("pe", lambda e, pi=pi, b2=b2, o=o: e.matmul(bank(6 + pi), lhsT=ones_bf[:], rhs=sqy[b2][:, pi * 512:(pi + 1) * 512],
                                                                                      start=(o == 0), stop=(o == KC - 1)),
                                          deps=[t_sq] + (bank_free[6] + bank_free[7] if o == 0 else []), signal=(pi == 1))
                            sqy_free[b2] = [tk]
                            ystat_tok[0] = tk
                        pe_hooks.append(hook)
                ring_free[s].append(tok)
            act_free = [last_pe]
        for h in list(pe_hooks):
            h()
        del pe_hooks[:]

        rstd_y = SC[:, 2052:2052 + 1024]
        t_r3 = P.op("act", lambda e: e.activation(out=rstd_y, in_=ps[:, 3072:4096], func=AF.Sqrt, bias=epst[:], scale=1.0 / D),
                    deps=[ystat_tok[0], conv_last[0]])
        t_r3 = P.op("dve", lambda e: e.reciprocal(out=rstd_y, in_=rstd_y), deps=[t_r3])
        t_ggf = P.op("dve", lambda e: e.tensor_tensor(out=ggf, in0=mod_fm[:, 80:96], in1=vec[:, V_GQF:V_GQF + 16], op=ALU.mult),
                     deps=[mod_tok[23], ldc_all])
        cxf = [bview(RC, i * 4096, 4096, F32) for i in range(4)]
        cxf_free = [[last_pe] for _ in range(4)]
        fin = None
        for blk in range(KC):
            b = blk % 4
            t_x = P.dma("sp", lambda e, blk=blk, b=b: e.dma_start(out=cxf[b], in_=outT[blk * 128:(blk + 1) * 128, :]), "cx%d" % b,
                        deps=cxf_free[b] + [spill_all])
            t_a = P.op("dve", lambda e, blk=blk: e.scalar_tensor_tensor(out=yT[:, blk, :], in0=yT[:, blk, :], scalar=ggf[:, blk:blk + 1], in1=rstd_y,
                                                                        op0=ALU.mult, op1=ALU.mult), deps=[t_r3, t_ggf])
            t_o = P.op("dve", lambda e, blk=blk, b=b: e.tensor_tensor(out=yT[:, blk, :], in0=yT[:, blk, :], in1=cxf[b], op=ALU.add), deps=[t_a, t_x])
            cxf_free[b] = [t_o]
            fin = P.dma("sp", lambda e, blk=blk: e.dma_start(out=outT[blk * 128:(blk + 1) * 128, :], in_=yT[:, blk, :]), "fin", deps=[t_o, t_x])
        P.wait("sp", [("fin", P.cnt["fin"])])
        if DEBUG:
            P.wait("sp", [("dbg", P.cnt["dbg"])])
        P.emit(block, sems)
    return nc


_NC_CACHE = {}


def _pool_mats(first_core):
    A = np.zeros((4, 4, 128, 128), np.float32)
    for gi, win in enumerate(POOL_WINDOWS):
        for t in range(128):
            for j in range(t - win + 1, t + 1):
                if j >= 0:
                    A[2, gi, j, t] += 1.0 / win
                else:
                    A[3, gi, 128 + j, t] += 1.0 / win
            A[2, gi, t, t] -= 1.0
            cnt = min(t + 1, win)
            for j in range(max(0, t - win + 1), t + 1):
                A[0, gi, j, t] += 1.0 / cnt
            A[0, gi, t, t] -= 1.0
    if not first_core:
        A[0] = A[2]
        A[1] = A[3]
    return np.ascontiguousarray(A.transpose(2, 0, 1, 3).reshape(128, 16 * 128))


def kernel(x, c, w_ada, b_ada, g_pre_mix, g_post_mix, w_in, ln_v_g, ln_v_b, w_spatial, b_spatial,
           w_pool, pool_scale, w_branch_a, w_branch_b, w_out, g_pre_ffn, g_post_ffn, w_up, conv_w,
           conv_b, w_down):
    f = np.float32
    x = np.asarray(x, f)
    S = x.shape[1]
    assert S == NCORE * TOK

    def fm(v, nblk):
        return np.asarray(v, f).reshape(nblk, 128).T

    xs = x[0]
    xpad = np.concatenate([np.zeros((128, D), f), xs], axis=0)
    vec_fm = np.concatenate([
        fm(g_pre_mix[0], 16), fm(g_post_mix[0], 16), fm(g_pre_ffn[0], 16), fm(g_post_ffn[0], 16),
        fm(pool_scale[0], 8),
        np.asarray(conv_w[0], f).T.reshape(88, 128, 3).transpose(1, 0, 2).reshape(128, 264),
        fm(conv_b[0], 88)], axis=1)
    vec_fm = np.ascontiguousarray(vec_fm, f)
    assert vec_fm.shape == (128, NV)
    common = {
        "c_col": np.ascontiguousarray(fm(c[0], 16)),
        "w_ada": np.ascontiguousarray(w_ada[0], f),
        "b_ada_fm": np.ascontiguousarray(fm(b_ada[0], 96)),
        "vec_fm": vec_fm,
        "ln_rows": np.ascontiguousarray(np.stack([ln_v_g[0], ln_v_b[0]]), f),
        "bs_exp": np.ascontiguousarray(np.repeat(np.asarray(b_spatial[0], f), 2, axis=0).reshape(1, 2048)),
        "wsT": np.ascontiguousarray(np.asarray(w_spatial[0], f).transpose(2, 0, 1).reshape(128, 1024)),
        "tri": np.ascontiguousarray(np.triu(np.ones((128, 128), f))),
        "wpool": np.ascontiguousarray(np.asarray(w_pool[0], f).reshape(4, 2, 128, 256).transpose(2, 0, 1, 3).reshape(128, 2048)),
        "w_in": np.ascontiguousarray(w_in[0], f),
        "w_branch_a": np.ascontiguousarray(w_branch_a[0], f),
        "w_branch_b": np.ascontiguousarray(w_branch_b[0], f),
        "w_out": np.ascontiguousarray(w_out[0], f),
        "w_up": np.ascontiguousarray(w_up[0], f),
        "w_down": np.ascontiguousarray(w_down[0], f),
    }
    in_maps = []
    for core in range(NCORE):
        m = dict(common)
        m["xT"] = np.ascontiguousarray(xpad[core * TOK:core * TOK + TM].T)
        m["poolA"] = _pool_mats(core == 0)
        m["hmask"] = np.full((128, 1), 0.0 if core == 0 else 1.0, f)
        in_maps.append(m)
    if "nc" not in _NC_CACHE:
        _NC_CACHE["nc"] = build_program()
    res = run_bass_kernel_spmd(_NC_CACHE["nc"], in_maps, core_ids=list(range(NCORE)))
    if DEBUG:
        _NC_CACHE["dbg"] = [{k: np.asarray(v) for k, v in r.items() if k.startswith("dbg_")} for r in res.results]
    outs = [np.asarray(r["outT"], f) for r in res.results]
    full = np.concatenate(outs, axis=1).T
    return np.ascontiguousarray(full[None], f)
```

```python
import contextlib
import numpy as np
import concourse.bass as bass
import concourse.mybir as mybir
from concourse.bass_utils import run_bass_kernel_spmd

F32 = mybir.dt.float32
BF16 = mybir.dt.bfloat16
AF = mybir.ActivationFunctionType
ALU = mybir.AluOpType
AX = mybir.AxisListType

NCORE = 8
D = 2048
KC = 16
TOK = 1024
TM = 1152
NCH = 9
NE = 1026
FFN = 5632
NG = 4
GB = 11
EPS = 1e-6
POOL_WINDOWS = (2, 4, 8, 16)

V_GPM, V_GQM, V_GPF, V_GQF, V_PS, V_CW, V_CB = 0, 16, 32, 48, 64, 72, 72 + 264
NV = 72 + 264 + 88


class Plan:
    ENGS = ("pe", "act", "dve", "pool", "sp")

    def __init__(self):
        self.q = {e: [] for e in self.ENGS}
        self.cnt = {e: 0 for e in self.ENGS}
        self.waited = {e: {} for e in self.ENGS}
        self.semnames = list(self.ENGS)

    def new_sem(self, name):
        self.cnt[name] = 0
        self.semnames.append(name)
        return name

    def _waits(self, eng, deps):
        for d in deps:
            if d is None:
                continue
            s, v = d
            if v <= 0 or self.waited[eng].get(s, 0) >= v:
                continue
            self.waited[eng][s] = v
            self.q[eng].append(("w", s, v))

    def op(self, eng, fn, deps=(), signal=True):
        self._waits(eng, deps)
        if signal:
            self.cnt[eng] += 1
            self.q[eng].append(("o", fn, eng, 1))
            return (eng, self.cnt[eng])
        self.q[eng].append(("o", fn, None, 0))
        return None

    def dma(self, eng, fn, sem, deps=()):
        self._waits(eng, deps)
        self.cnt[sem] += 16
        self.q[eng].append(("o", fn, sem, 16))
        return (sem, self.cnt[sem])

    def wait(self, eng, deps):
        self._waits(eng, deps)

    def emit(self, block, sems):
        def run(engname, e):
            for it in self.q[engname]:
                if it[0] == "w":
                    e.wait_ge(sems[it[1]], it[2])
                else:
                    ins = it[1](e)
                    if it[2] is not None:
                        ins.then_inc(sems[it[2]], it[3])
        if self.q["pe"]:
            block.tensor(lambda e: run("pe", e))
        if self.q["act"]:
            block.scalar(lambda e: run("act", e))
        if self.q["dve"]:
            block.vector(lambda e: run("dve", e))
        if self.q["pool"]:
            block.gpsimd(lambda e: run("pool", e))
        if self.q["sp"]:
            block.sync(lambda e: run("sp", e))


DEBUG = False


def build_program():
    nc = bass.Bass("TRN2", target_bir_lowering=False)
    dbg = {}
    if DEBUG:
        for nm, n in (("dbg_h", 16 * 1152), ("dbg_vn", 9 * 2048), ("dbg_sg", 16 * 1026), ("dbg_mg", 16 * 1026), ("dbg_q", 8 * 1026)):
            dbg[nm] = nc.dram_tensor(nm, [128, n], BF16, kind="ExternalOutput").ap()

    def din(name, shape, dt=F32):
        return nc.dram_tensor(name, list(shape), dt, kind="ExternalInput").ap()

    xT = din("xT", [D, TM])
    c_col = din("c_col", [128, KC])
    w_ada = din("w_ada", [D, 6 * D])
    b_ada_fm = din("b_ada_fm", [128, 96])
    vec_fm_d = din("vec_fm", [128, NV])
    ln_rows = din("ln_rows", [2, D])
    bs_exp_d = din("bs_exp", [1, 2048])
    wsT_d = din("wsT", [128, 8 * 128])
    tri_d = din("tri", [128, 128])
    poolA_d = din("poolA", [128, 16 * 128])
    wpool_d = din("wpool", [128, 4 * 2 * 256])
    hmask_d = din("hmask", [128, 1])
    w_in = din("w_in", [D, 9216])
    w_ba = din("w_branch_a", [D, D])
    w_bb = din("w_branch_b", [1024, D])
    w_out = din("w_out", [D, D])
    w_up = din("w_up", [D, 2 * FFN])
    w_down = din("w_down", [FFN, D])
    outT = nc.dram_tensor("outT", [D, TOK], F32, kind="ExternalOutput").ap()

    P = Plan()
    P.new_sem("dbg")

    def dump(nm, src2d, deps):
        if DEBUG:
            P.dma("sp", lambda e: e.dma_start(out=dbg[nm][:, :], in_=src2d), "dbg", deps=deps)
    for s in ("ring0", "ring1", "ring2", "ld", "ld2", "ldc", "xin", "st", "fin", "cx0", "cx1", "cx2", "cx3"):
        P.new_sem(s)

    with contextlib.ExitStack() as es:
        def sb(name, shape, dt):
            return es.enter_context(nc.sbuf_tensor(name, list(shape), dt))

        R1 = sb("R1", [128, 34848], BF16)
        RC = sb("RC", [128, 16416], BF16)
        RD = sb("RD", [128, 18432], BF16)
        RING = sb("RING", [128, 3 * 8192], BF16)
        SC = sb("SC", [128, 4104], F32)
        ps = es.enter_context(nc.psum_tensor("ps", [128, 4096], F32))

        def bview(reg, off_b, nbytes, dt):
            a = reg[:, off_b // 2:(off_b + nbytes) // 2]
            return a if dt == BF16 else a.bitcast(F32)

        def v3(ap2, k):
            return ap2.rearrange("p (k n) -> p k n", k=k)

        gu = v3(bview(R1, 0, 32832, BF16), 16)
        gv = v3(bview(R1, 32832, 36864, BF16), 9)
        yb = gu
        poolA = v3(bview(R1, 0, 4096, BF16), 16)
        wpool = bview(R1, 4096, 4096, BF16).rearrange("p (g k e) -> p g k e", g=4, k=2)
        p_tm = v3(bview(R1, 32832, 18432, BF16), 9)
        qT = v3(bview(R1, 32832, 16416, BF16), 8)
        pooledT = v3(bview(R1, 51264, 16416, BF16), 8)
        mixT = v3(bview(R1, 0, 65664, F32), 16)
        yT = v3(bview(R1, 0, 65536, F32), 16)
        ya = v3(bview(RC, 0, 32832, BF16), 16)
        ln_g_bc = bview(RC, 0, 8192, F32)
        ln_b_bc = bview(RC, 8192, 8192, F32)
        WmT = v3(bview(RC, 16384, 2048, BF16), 8)
        bs2 = bview(RC, 18432, 4096, BF16)
        wsT_f = v3(bview(RC, 22528, 4096, F32), 8)
        tri = bview(RC, 26624, 512, F32)
        act = v3(bview(RC, 0, 22528, BF16), 11)
        cxin = [bview(RC, i * 4104, 4104, F32) for i in range(2)]
        hT = v3(bview(RD, 0, 36864, BF16), 16)
        h2T = v3(bview(RD, 0, 32832, BF16), 16)

        ring = [v3(RING[:, i * 8192:(i + 1) * 8192], 16) for i in range(3)]
        ring_flat = [RING[:, i * 8192:(i + 1) * 8192] for i in range(3)]

        mod_fm = sb("mod_fm", [128, 96], F32)
        bada = sb("bada", [128, 96], F32)
        vec = sb("vec", [128, NV], F32)
        derived = sb("derived", [128, 64], F32)
        gs_m, ggm, gsf, ggf = (derived[:, 0:16], derived[:, 16:32], derived[:, 32:48], derived[:, 48:64])
        ccol = sb("ccol", [128, KC], F32)
        scb = sb("scb", [128, KC], BF16)
        epst = sb("epst", [128, 1], F32)
        hmask = sb("hmask_sb", [128, 1], F32)
        ones_bf = sb("ones_bf", [128, 128], BF16)
        onesrow = sb("onesrow", [1, 128], F32)
        ones2 = sb("ones2", [2, 128], BF16)
        modrow = sb("modrow", [1, 512], F32)
        vsum = sb("vsum", [128, 36], F32)
        vstat = sb("vstat", [128, 5 * 9], F32)
        vsq, vmean, vmsq, vvar, vrstd = (vstat[:, i * 9:(i + 1) * 9] for i in range(5))

        def scv(off, n, dt=F32):
            a = SC[:, off:off + n]
            return a if dt == F32 else a.bitcast(BF16)

        sems = {n: es.enter_context(nc.semaphore(n)) for n in P.semnames}
        block = es.enter_context(nc.Block())

        ring_free = [[], [], []]
        ring_next = [0]
        prefetched = []

        def _load(spec):
            w_ap, r0, nk, c0, ncols = spec
            s = ring_next[0] % 3
            ring_next[0] += 1
            dst = ring_flat[s][:, 0:nk * ncols].rearrange("p (k n) -> p k n", k=nk)
            src = w_ap[r0 * 128:(r0 + nk) * 128, c0:c0 + ncols].rearrange("(k p) n -> p k n", p=128)
            tok = P.dma("pool", lambda e: e.dma_start(out=dst, in_=src), "ring%d" % s, deps=ring_free[s])
            ring_free[s] = []
            return s, dst, tok

        def prefetch(w_ap, r0, nk, c0, ncols):
            spec = (id(w_ap.tensor) if hasattr(w_ap, "tensor") else id(w_ap), r0, nk, c0, ncols)
            prefetched.append((spec, _load((w_ap, r0, nk, c0, ncols))))

        def load_piece(w_ap, r0, nk, c0, ncols):
            spec = (id(w_ap.tensor) if hasattr(w_ap, "tensor") else id(w_ap), r0, nk, c0, ncols)
            if prefetched and prefetched[0][0] == spec:
                return prefetched.pop(0)[1]
            return _load((w_ap, r0, nk, c0, ncols))

        bank_free = [[] for _ in range(8)]
        blk_next = [0]
        tm_next = [0]

        def bank(b):
            return ps[:, b * 512:(b + 1) * 512]

        def alloc_blk():
            i = blk_next[0] % 2
            blk_next[0] += 1
            deps = bank_free[3 * i] + bank_free[3 * i + 1] + bank_free[3 * i + 2]
            return i, deps

        def blk_view(i, n):
            return ps[:, 3 * i * 512:(3 * i + 3) * 512].rearrange("p (k n) -> p k n", k=3)[:, :, 0:n]

        def set_blk_free(i, toks):
            for b in range(3):
                bank_free[3 * i + b] = list(toks)

        def alloc_tm():
            b = tm_next[0] % 6
            tm_next[0] += 1
            return b, bank_free[b]

        pe_hooks = []

        def fm_block(lhsTs, rhs_fn, pieces, deps):
            i, bdeps = alloc_blk()
            nk = len(lhsTs)
            tok = None
            first = True
            for kc in range(nk):
                for pi, (c0, n) in enumerate(pieces):
                    out_ap = bank(3 * i + pi)[:, 0:n]
                    last = (kc == nk - 1) and (pi == len(pieces) - 1)
                    tok = P.op("pe", lambda e, o=out_ap, l=lhsTs[kc], r=rhs_fn(kc, c0, n), st=(kc == 0), sp=(kc == nk - 1):
                               e.matmul(o, lhsT=l, rhs=r, start=st, stop=sp),
                               deps=(list(deps) + bdeps) if first else (), signal=last)
                    first = False
            hooks = list(pe_hooks)
            del pe_hooks[:]
            for h in hooks:
                h()
            return i, tok

        PCS_H = [(126, 342), (468, 342), (810, 342)]
        PCS_E = [(0, 342), (342, 342), (684, 342)]
        PCS_T = [(0, 512), (512, 512)]

        def all_ad():
            return [("act", P.cnt["act"]), ("dve", P.cnt["dve"]), ("pool", P.cnt["pool"])]

        t_c = P.dma("sp", lambda e: e.dma_start(out=ccol[:], in_=c_col[:, :]), "ldc")
        P.dma("sp", lambda e: e.dma_start(out=bada[:], in_=b_ada_fm[:, :]), "ldc")
        P.dma("sp", lambda e: e.dma_start(out=vec[:], in_=vec_fm_d[:, :]), "ldc")
        P.dma("sp", lambda e: e.dma_start(out=hmask[:], in_=hmask_d[:, :]), "ldc")
        ldc_all = ("ldc", P.cnt["ldc"])
        t_ms = P.op("dve", lambda e: e.memset(epst[:], EPS))
        P.op("dve", lambda e: e.memset(ones_bf[:], 1.0))
        P.op("dve", lambda e: e.memset(ones2[:], 1.0))
        t_ones = P.op("dve", lambda e: e.memset(onesrow[:], 1.0))
        P.op("dve", lambda e: e.memset(vsum[:], 0.0))
        t_vz = P.op("dve", lambda e: e.memset(vstat[:], 0.0))
        t_sc = P.op("act", lambda e: e.activation(out=scb[:], in_=ccol[:], func=AF.Silu), deps=[ldc_all])

        xres = [bview(R1, kc * 4608, 4608, F32) for kc in range(15)] + [bview(RC, 0, 4608, F32)]
        rstd0 = bview(RC, 4608, 4608, F32)
        sqb = [SC[:, i * 576:(i + 1) * 576].bitcast(BF16) for i in range(2)]
        tfx = [SC[:, 1152 + i * 1152:1152 + (i + 1) * 1152] for i in range(2)]
        for kc in range(KC):
            P.dma("sp", lambda e, kc=kc: e.dma_start(out=xres[kc], in_=xT[kc * 128:(kc + 1) * 128, :]), "xin")
        t_xl = [("xin", P.cnt["xin"])] * KC
        S0_PCS = [(0, 384), (384, 384), (768, 384)]
        st_i, st_deps = alloc_blk()
        stat_tok = None
        sq_free = [[], []]
        for kc in range(KC):
            b = kc % 2
            t_sq = P.op("act", lambda e, b=b, kc=kc: e.activation(out=sqb[b], in_=xres[kc], func=AF.Square),
                        deps=[t_xl[kc]] + sq_free[b])
            for pi, (c0, n) in enumerate(S0_PCS):
                stat_tok = P.op("pe", lambda e, b=b, pi=pi, c0=c0, n=n, kc=kc:
                                e.matmul(bank(3 * st_i + pi)[:, 0:n], lhsT=ones_bf[:], rhs=sqb[b][:, c0:c0 + n],
                                         start=(kc == 0), stop=(kc == KC - 1)),
                                deps=[t_sq] + (st_deps if kc == 0 else []), signal=(pi == 2))
            sq_free[b] = [stat_tok]
        t_r0 = P.op("act", lambda e: e.activation(out=v3(rstd0, 3), in_=blk_view(st_i, 384), func=AF.Sqrt,
                                                  bias=epst[:], scale=1.0 / D), deps=[stat_tok, t_ms])
        set_blk_free(st_i, [t_r0])
        t_r0 = P.op("dve", lambda e: e.reciprocal(out=rstd0, in_=rstd0), deps=[t_r0])

        mod_tok = [None] * 24
        modrow_free = [[]]
        pm_free = [[]]

        def mod_piece(j):
            s, wt, t_w = load_piece(w_ada, 0, KC, j * 512, 512)
            tok = None
            for kc in range(KC):
                tok = P.op("pe", lambda e, kc=kc, wt=wt: e.matmul(ps[0:1, 3584:4096], lhsT=scb[:, kc:kc + 1], rhs=wt[:, kc, :],
                                                                  start=(kc == 0), stop=(kc == KC - 1)),
                           deps=[t_w, t_sc] + pm_free[0] + bank_free[7] if kc == 0 else (), signal=(kc == KC - 1))
            ring_free[s].append(tok)
            mr = modrow[0:1, 0:512]
            t_cp = P.op("dve", lambda e, mr=mr: e.tensor_copy(out=mr, in_=ps[0:1, 3584:4096]), deps=[tok] + modrow_free[0])
            pm_free[0] = [t_cp]
            bank_free[7] = [t_cp]
            t4 = None
            for q in range(4):
                t4 = P.op("pe", lambda e, q=q, mr=mr: e.matmul(ps[:, 3072 + q:3072 + q + 1], lhsT=mr[0:1, q * 128:(q + 1) * 128],
                                                               rhs=onesrow[0:1, 0:1], start=True, stop=True),
                          deps=[t_cp, t_ones] + bank_free[6] if q == 0 else (), signal=(q == 3))
            modrow_free[0] = [t4]
            t_m = P.op("dve", lambda e, j=j: e.tensor_tensor(out=mod_fm[:, 4 * j:4 * j + 4], in0=ps[:, 3072:3076],
                                                            in1=bada[:, 4 * j:4 * j + 4], op=ALU.add), deps=[t4, ldc_all])
            bank_free[6] = [t_m]
            mod_tok[j] = t_m

        for j in range(8):
            mod_piece(j)
        t_gsm = P.op("dve", lambda e: e.scalar_tensor_tensor(out=gs_m, in0=mod_fm[:, 16:32], scalar=1.0, in1=vec[:, V_GPM:V_GPM + 16],
                                                             op0=ALU.add, op1=ALU.mult), deps=[mod_tok[7], ldc_all])
        hT_ready = None
        tfx_free = [[], []]
        for kc in range(KC):
            b = kc % 2
            t_a = P.op("dve", lambda e, kc=kc, b=b: e.scalar_tensor_tensor(out=tfx[b], in0=xres[kc], scalar=gs_m[:, kc:kc + 1], in1=rstd0,
                                                                           op0=ALU.mult, op1=ALU.mult), deps=[t_xl[kc], t_gsm, t_r0] + tfx_free[b])
            hT_ready = P.op("act", lambda e, kc=kc, b=b: e.activation(out=hT[:, kc, :], in_=tfx[b], func=AF.Identity,
                                                                      bias=mod_fm[:, kc:kc + 1], scale=1.0), deps=[t_a, mod_tok[3]])
            tfx_free[b] = [hT_ready]
        s0_done = all_ad()
        dump("dbg_h", RD[:, 0:16 * 1152], [hT_ready])

        P.dma("sp", lambda e: e.dma_start(out=ln_g_bc, in_=ln_rows[0:1, :].broadcast_to([128, D])), "ld", deps=s0_done)
        P.dma("sp", lambda e: e.dma_start(out=ln_b_bc, in_=ln_rows[1:2, :].broadcast_to([128, D])), "ld")
        P.dma("sp", lambda e: e.dma_start(out=wsT_f.rearrange("p k n -> p (k n)"), in_=wsT_d[:, :]), "ld")
        P.dma("sp", lambda e: e.dma_start(out=tri, in_=tri_d[:, :]), "ld")
        bsf = SC[0:1, 0:2048]
        bslo = SC[0:1, 2048:3072].bitcast(BF16)
        P.dma("sp", lambda e: e.dma_start(out=bsf, in_=bs_exp_d[:, :]), "ld")
        ld_s3 = ("ld", P.cnt["ld"])
        t_wm = None
        for h in range(8):
            t_wm = P.op("dve", lambda e, h=h: e.tensor_tensor(out=WmT[:, h, :], in0=wsT_f[:, h, :], in1=tri, op=ALU.mult), deps=[ld_s3])

        mod_queue = list(range(8, 24))
        mod_ctr = [0]

        def maybe_mod(every=2, limit=20):
            mod_ctr[0] += 1
            if mod_queue and mod_queue[0] < limit and mod_ctr[0] % every == 0:
                mod_piece(mod_queue.pop(0))

        gv_ready = None
        for g in range(4):
            s, wt, t_w = load_piece(w_in, 0, KC, 2048 + g * 512, 512)
            if g == 0:
                t_bh = P.dma("pool", lambda e: e.dma_start(out=bs2[0:1, :], in_=bs_exp_d[:, :]), "ld2", deps=s0_done)
            for c in range(NCH):
                b, bdeps = alloc_tm()
                tok = None
                for kc in range(KC):
                    tok = P.op("pe", lambda e, b=b, kc=kc, c=c, wt=wt: e.matmul(bank(b), lhsT=hT[:, kc, c * 128:(c + 1) * 128], rhs=wt[:, kc, :],
                                                                                start=(kc == 0), stop=(kc == KC - 1)),
                               deps=[t_w, hT_ready] + bdeps if kc == 0 else (), signal=(kc == KC - 1))
                gv_ready = P.op("act", lambda e, b=b, c=c, g=g: e.activation(out=gv[:, c, g * 512:(g + 1) * 512], in_=bank(b), func=AF.Gelu_apprx_tanh,
                                                                             accum_out=vsum[:, c * 4 + g:c * 4 + g + 1]), deps=[tok, t_vz] + s0_done)
                bank_free[b] = [gv_ready]
            ring_free[s].append(tok)
            maybe_mod()
        t_lo = P.op("dve", lambda e: e.tensor_tensor(out=bslo, in0=bsf, in1=bs2[0:1, :], op=ALU.subtract), deps=[ld_s3, t_bh])
        t_bl = P.dma("sp", lambda e: e.dma_start(out=bs2[1:2, :], in_=bslo), "ld2", deps=[t_lo])
        ld2_all = ("ld2", P.cnt["ld2"])

        junk = SC[:, 1024:2048].bitcast(BF16)
        t_q = None
        for c in range(NCH):
            t_q = P.op("act", lambda e, c=c: e.activation(out=junk, in_=gv[:, c, :], func=AF.Square, accum_out=vsq[:, c:c + 1]),
                       deps=[gv_ready, hT_ready, t_lo, t_bl])
        t = P.op("dve", lambda e: e.tensor_reduce(out=vmean, in_=vsum[:].rearrange("p (c g) -> p c g", g=4), axis=AX.X, op=ALU.add), deps=[gv_ready])
        t = P.op("dve", lambda e: e.tensor_scalar(out=vmean, in0=vmean, scalar1=1.0 / D, scalar2=None, op0=ALU.mult), deps=[t])
        t = P.op("dve", lambda e: e.tensor_tensor(out=vmsq, in0=vmean, in1=vmean, op=ALU.mult), deps=[t])
        t = P.op("dve", lambda e: e.scalar_tensor_tensor(out=vvar, in0=vsq, scalar=1.0 / D, in1=vmsq, op0=ALU.mult, op1=ALU.subtract), deps=[t, t_q])
        t = P.op("act", lambda e: e.activation(out=vvar, in_=vvar, func=AF.Sqrt, bias=epst[:], scale=1.0), deps=[t])
        t_vr = P.op("dve", lambda e: e.reciprocal(out=vrstd, in_=vvar), deps=[t])
        vn_ready = [None] * NCH
        for c in range(NCH):
            t1 = P.op("dve", lambda e, c=c: e.tensor_scalar(out=gv[:, c, :], in0=gv[:, c, :], scalar1=vmean[:, c:c + 1], scalar2=vrstd[:, c:c + 1],
                                                            op0=ALU.subtract, op1=ALU.mult), deps=[t_vr, t_q])
            t2 = P.op("dve", lambda e, c=c: e.tensor_tensor(out=gv[:, c, :], in0=gv[:, c, :], in1=ln_g_bc, op=ALU.mult), deps=[t1, ld_s3])
            vn_ready[c] = P.op("dve", lambda e, c=c: e.tensor_tensor(out=gv[:, c, :], in0=gv[:, c, :], in1=ln_b_bc, op=ALU.add), deps=[t2])

        dump("dbg_vn", R1[:, 16416:16416 + 9 * 2048], [vn_ready[8]])
        evac_rr = [0]
        gu_ready = None
        for g in range(4):
            s, wt, t_w = load_piece(w_in, 0, KC, g * 512, 512)
            for bl in range(4):
                blk = g * 4 + bl
                i, tok = fm_block([wt[:, kc, bl * 128:(bl + 1) * 128] for kc in range(KC)],
                                  lambda kc, c0, n: hT[:, kc, c0:c0 + n], PCS_H, [t_w, hT_ready])
                gu_ready = P.op("act", lambda e, i=i, blk=blk: e.activation(out=v3(gu[:, blk, :], 3), in_=blk_view(i, 342), func=AF.Gelu_apprx_tanh),
                                deps=[tok] + s0_done)
                set_blk_free(i, [gu_ready])
            ring_free[s].append(tok)
            maybe_mod()

        sg_ready = None
        for c in range(NCH):
            for q in range(4):
                b, bdeps = alloc_tm()
                P.op("pe", lambda e, b=b, q=q: e.matmul(bank(b), lhsT=ones2[0:2, :], rhs=bs2[0:2, q * 512:(q + 1) * 512], start=True, stop=False),
                     deps=[ld2_all, vn_ready[c], t_wm] + bdeps, signal=False)
                tok = None
                for f in range(4):
                    fb = q * 4 + f
                    h = fb // 2
                    tok = P.op("pe", lambda e, b=b, f=f, fb=fb, c=c, h=h: e.matmul(bank(b)[:, f * 128:(f + 1) * 128], lhsT=gv[:, c, fb * 128:(fb + 1) * 128],
                                                                                  rhs=WmT[:, h, :], start=False, stop=(f == 3)), signal=(f == 3))
                pv = bank(b).rearrange("p (k n) -> p k n", k=4)
                if c == 0:
                    o = gu[:, q * 4:q * 4 + 4, 0:2]
                    i0 = pv[:, :, 126:128]
                else:
                    o = gu[:, q * 4:q * 4 + 4, 2 + (c - 1) * 128:2 + c * 128]
                    i0 = pv
                sg_ready = P.op("dve", lambda e, o=o, i0=i0: e.tensor_tensor(out=o, in0=i0, in1=o, op=ALU.mult), deps=[tok, gu_ready])
                bank_free[b] = [sg_ready]

        dump("dbg_sg", R1[:, 0:16 * 1026], [sg_ready])
        def evac_copy(i, out3, deps):
            k = evac_rr[0] % 2
            evac_rr[0] += 1
            if k == 0:
                t = P.op("act", lambda e: e.activation(out=out3, in_=blk_view(i, 342), func=AF.Copy), deps=deps)
            else:
                t = P.op("dve", lambda e: e.tensor_copy(out=out3, in_=blk_view(i, 342)), deps=deps)
            set_blk_free(i, [t])
            return t

        ya_ready = None
        for g in range(4):
            s, wt, t_w = load_piece(w_ba, 0, KC, g * 512, 512)
            for bl in range(4):
                blk = g * 4 + bl
                i, tok = fm_block([wt[:, kc, bl * 128:(bl + 1) * 128] for kc in range(KC)],
                                  lambda kc, c0, n: gu[:, kc, c0:c0 + n], PCS_E, [t_w, sg_ready])
                ya_ready = evac_copy(i, v3(ya[:, blk, :], 3), [tok, sg_ready])
            ring_free[s].append(tok)
            maybe_mod()
        s4_last_pe = tok
        ya_all = all_ad()

        sgt = [SC[:, i * 1026:(i + 1) * 1026] for i in range(2)]
        tmpm = [SC[:, 2052 + i * 1026:2052 + (i + 1) * 1026] for i in range(2)]
        sgt_free = [[], []]
        tmp_free = [[], []]
        n9 = [0]
        st9 = {"merged": None, "last_pe": None, "yb_all": []}

        def gate_piece(stage, g):
            s, wt, t_w = load_piece(w_in, 0, KC, 5120 + stage * 2048 + g * 512, 512)
            for bl in range(4):
                blk = g * 4 + bl
                i, tok = fm_block([wt[:, kc, bl * 128:(bl + 1) * 128] for kc in range(KC)],
                                  lambda kc, c0, n: hT[:, kc, c0:c0 + n], PCS_H, [t_w])
                k = n9[0] % 2
                n9[0] += 1
                t_s = P.op("act", lambda e, i=i, k=k: e.activation(out=v3(sgt[k], 3), in_=blk_view(i, 342), func=AF.Sigmoid), deps=[tok, t_q] + sgt_free[k])
                set_blk_free(i, [t_s])
                if stage == 0:
                    st9["merged"] = P.op("dve", lambda e, blk=blk, k=k: e.tensor_tensor(out=ya[:, blk, :], in0=ya[:, blk, :], in1=sgt[k], op=ALU.mult),
                                         deps=[t_s] + ya_all)
                    sgt_free[k] = [st9["merged"]]
                else:
                    t_m = P.op("dve", lambda e, blk=blk, k=k: e.tensor_tensor(out=tmpm[k], in0=yb[:, blk, :], in1=sgt[k], op=ALU.mult),
                               deps=[t_s] + st9["yb_all"] + tmp_free[k])
                    sgt_free[k] = [t_m]
                    st9["merged"] = P.op("dve", lambda e, blk=blk, k=k: e.tensor_tensor(out=ya[:, blk, :], in0=ya[:, blk, :], in1=tmpm[k], op=ALU.add),
                                         deps=[t_m])
                    tmp_free[k] = [st9["merged"]]
                st9["last_pe"] = tok
            ring_free[s].append(tok)
            maybe_mod()

        for g in range(2):
            s, wt, t_w = load_piece(w_in, 0, KC, 4096 + g * 512, 512)
            for c in range(NCH):
                b, bdeps = alloc_tm()
                tok = None
                for kc in range(KC):
                    tok = P.op("pe", lambda e, b=b, kc=kc, c=c, wt=wt: e.matmul(bank(b), lhsT=hT[:, kc, c * 128:(c + 1) * 128], rhs=wt[:, kc, :],
                                                                                start=(kc == 0), stop=(kc == KC - 1)),
                               deps=[t_w, sg_ready] + bdeps if kc == 0 else (), signal=(kc == KC - 1))
                k = evac_rr[0] % 2
                evac_rr[0] += 1
                if k == 0:
                    t_p = P.op("act", lambda e, b=b, c=c, g=g: e.activation(out=p_tm[:, c, g * 512:(g + 1) * 512], in_=bank(b), func=AF.Copy), deps=[tok, sg_ready])
                else:
                    t_p = P.op("dve", lambda e, b=b, c=c, g=g: e.tensor_copy(out=p_tm[:, c, g * 512:(g + 1) * 512], in_=bank(b)), deps=[tok, sg_ready])
                bank_free[b] = [t_p]
            ring_free[s].append(tok)
            maybe_mod()
        p_all = all_ad()
        P.dma("pool", lambda e: e.dma_start(out=poolA.rearrange("p k n -> p (k n)"), in_=poolA_d[:, :]), "ld", deps=[s4_last_pe])
        P.dma("pool", lambda e: e.dma_start(out=wpool.rearrange("p g k e -> p (g k e)"), in_=wpool_d[:, :]), "ld", deps=[s4_last_pe])
        ld_s6 = ("ld", P.cnt["ld"])
        gate_piece(0, 0)
        pooled_ready = None
        for c in range(NCH):
            kind = 0 if c == 1 else 2
            for q in range(2):
                b, bdeps = alloc_tm()
                tok = None
                for f in range(4):
                    fb = q * 4 + f
                    gi = fb // 2
                    two = c >= 1
                    tok = P.op("pe", lambda e, b=b, f=f, fb=fb, c=c, gi=gi, kind=kind, two=two:
                               e.matmul(bank(b)[:, f * 128:(f + 1) * 128], lhsT=p_tm[:, c, fb * 128:(fb + 1) * 128],
                                        rhs=poolA[:, kind * 4 + gi, :], start=True, stop=(not two)),
                               deps=p_all + [ld_s6] + bdeps if f == 0 else (), signal=(f == 3 and not two))
                    if two:
                        tok = P.op("pe", lambda e, b=b, f=f, fb=fb, c=c, gi=gi, kind=kind:
                                   e.matmul(bank(b)[:, f * 128:(f + 1) * 128], lhsT=p_tm[:, c - 1, fb * 128:(fb + 1) * 128],
                                            rhs=poolA[:, (kind + 1) * 4 + gi, :], start=False, stop=True), signal=(f == 3))
                pv = bank(b).rearrange("p (k n) -> p k n", k=4)
                if c == 0:
                    o = pooledT[:, q * 4:q * 4 + 4, 0:2]
                    i0 = pv[:, :, 126:128]
                else:
                    o = pooledT[:, q * 4:q * 4 + 4, 2 + (c - 1) * 128:2 + c * 128]
                    i0 = pv
                pooled_ready = P.op("act", lambda e, o=o, i0=i0: e.activation(out=o, in_=i0, func=AF.Copy), deps=[tok])
                bank_free[b] = [pooled_ready]
        gate_piece(0, 1)
        q_ready = None
        for gi in range(4):
            for eb in range(2):
                i, tok = fm_block([wpool[:, gi, k2, eb * 128:(eb + 1) * 128] for k2 in range(2)],
                                  lambda k2, c0, n, gi=gi: pooledT[:, gi * 2 + k2, c0:c0 + n], PCS_E, [pooled_ready, ld_s6])
                col = V_PS + gi * 2 + eb
                q_ready = P.op("act", lambda e, i=i, gi=gi, eb=eb, col=col: e.activation(out=v3(qT[:, gi * 2 + eb, :], 3), in_=blk_view(i, 342), func=AF.Identity,
                                                                                         scale=vec[:, col:col + 1]), deps=[tok, ldc_all, pooled_ready])
                set_blk_free(i, [q_ready])
        dump("dbg_q", R1[:, 16416:16416 + 8 * 1026], [q_ready])
        gate_piece(0, 2)
        for g in range(4):
            s, wt, t_w = load_piece(w_bb, 0, 8, g * 512, 512)
            for bl in range(4):
                blk = g * 4 + bl
                i, tok = fm_block([wt[:, kc, bl * 128:(bl + 1) * 128] for kc in range(8)],
                                  lambda kc, c0, n: qT[:, kc, c0:c0 + n], PCS_E, [t_w, q_ready])
                evac_copy(i, v3(yb[:, blk, :], 3), [tok, q_ready])
            ring_free[s].append(tok)
            maybe_mod()
        st9["yb_all"] = all_ad()
        gate_piece(0, 3)
        for g in range(4):
            gate_piece(1, g)
        while mod_queue and mod_queue[0] < 20:
            mod_piece(mod_queue.pop(0))
        merged_ready = st9["merged"]
        dump("dbg_mg", RC[:, 0:16 * 1026], [merged_ready])
        hT_dead = st9["last_pe"]

        sq_all = v3(bview(RD, 0, 32832, BF16), 16)
        rstd_e = SC[:, 0:1026]
        cxs = [SC[:, 1026 * (1 + i):1026 * (2 + i)] for i in range(3)]
        cx_free = [[], [], []]
        s10_done = all_ad()
        t_cx = [None] * KC
        for blk in range(3):
            t_cx[blk] = P.dma("sp", lambda e, blk=blk: e.dma_start(out=cxs[blk % 3], in_=xT[blk * 128:(blk + 1) * 128, 126:TM]), "cx%d" % (blk % 3), deps=s10_done)

        def square_to(eng, dst, src, deps):
            if eng == "act":
                return P.op("act", lambda e: e.activation(out=dst, in_=src, func=AF.Square), deps=deps)
            return P.op(eng, lambda e: e.tensor_tensor(out=dst, in0=src, in1=src, op=ALU.mult), deps=deps)

        sq_tok = [None] * KC
        for g in range(4):
            s, wt, t_w = load_piece(w_out, 0, KC, g * 512, 512)
            for bl in range(4):
                blk = g * 4 + bl
                i, tok = fm_block([wt[:, kc, bl * 128:(bl + 1) * 128] for kc in range(KC)],
                                  lambda kc, c0, n: ya[:, kc, c0:c0 + n], PCS_E, [t_w, merged_ready])
                eng = "act" if evac_rr[0] % 2 == 0 else "dve"
                t_e = evac_copy(i, v3(mixT[:, blk, :], 3), [tok, merged_ready])
                sq_tok[blk] = square_to(eng, sq_all[:, blk, :], mixT[:, blk, :], [t_e, hT_dead])
            ring_free[s].append(tok)
        merged_dead = tok

        def ffn_up_specs(k, half):
            col_base = half * FFN + k * GB * 128
            return [(w_up, 0, KC, col_base + pc0, pn) for (pc0, pn) in ((0, 512), (512, 512), (1024, 384))]
        for sp_ in ffn_up_specs(0, 0):
            prefetch(*sp_)

        def stats_matmuls(n, pieces, dep_list):
            i, bdeps = alloc_blk()
            tok = None
            for kc in range(KC):
                for pi, (c0, m) in enumerate(pieces):
                    tok = P.op("pe", lambda e, pi=pi, c0=c0, m=m, kc=kc: e.matmul(bank(3 * i + pi)[:, 0:m], lhsT=ones_bf[:], rhs=sq_all[:, kc, c0:c0 + m],
                                                                                  start=(kc == 0), stop=(kc == KC - 1)),
                               deps=[dep_list[kc]] + (bdeps if kc == 0 else []), signal=(kc == KC - 1 and pi == len(pieces) - 1))
            m = pieces[0][1]
            np_ = len(pieces)
            src = ps[:, 3 * i * 512:(3 * i + np_) * 512].rearrange("p (k n) -> p k n", k=np_)[:, :, 0:m]
            t_r = P.op("act", lambda e: e.activation(out=v3(rstd_e[:, 0:n], np_), in_=src, func=AF.Sqrt, bias=epst[:], scale=1.0 / D), deps=[tok] + s10_done)
            set_blk_free(i, [t_r])
            return P.op("dve", lambda e: e.reciprocal(out=rstd_e[:, 0:n], in_=rstd_e[:, 0:n]), deps=[t_r])

        t_rm = stats_matmuls(NE, PCS_E, sq_tok)
        t_ggm = P.op("dve", lambda e: e.tensor_tensor(out=ggm, in0=mod_fm[:, 32:48], in1=vec[:, V_GQM:V_GQM + 16], op=ALU.mult),
                     deps=[mod_tok[11], ldc_all])
        xmid_tok = [None] * KC
        sq2_tok = [None] * KC
        for blk in range(KC):
            b = blk % 3
            if t_cx[blk] is None:
                t_cx[blk] = P.dma("sp", lambda e, blk=blk, b=b: e.dma_start(out=cxs[b], in_=xT[blk * 128:(blk + 1) * 128, 126:TM]), "cx%d" % b, deps=cx_free[b])
            t_a = P.op("dve", lambda e, blk=blk: e.scalar_tensor_tensor(out=mixT[:, blk, :], in0=mixT[:, blk, :], scalar=ggm[:, blk:blk + 1], in1=rstd_e,
                                                                        op0=ALU.mult, op1=ALU.mult), deps=[t_rm, t_ggm])
            xmid_tok[blk] = P.op("pool", lambda e, blk=blk, b=b: e.tensor_tensor(out=mixT[:, blk, :], in0=mixT[:, blk, :], in1=cxs[b], op=ALU.add), deps=[t_a, t_cx[blk]])
            cx_free[b] = [xmid_tok[blk]]
            sq2_tok[blk] = square_to("act", sq_all[:, blk, :], mixT[:, blk, :], [xmid_tok[blk], t_rm])
        xmid_ready = xmid_tok[KC - 1]
        spill = None
        for blk in range(KC):
            spill = P.dma("sp", lambda e, blk=blk: e.dma_start(out=outT[blk * 128:(blk + 1) * 128, :], in_=mixT[:, blk, 2:NE]), "st", deps=[xmid_tok[blk]])
        spill_all = ("st", P.cnt["st"])
        t_r2 = stats_matmuls(NE, PCS_E, sq2_tok)
        t_gsf = P.op("dve", lambda e: e.scalar_tensor_tensor(out=gsf, in0=mod_fm[:, 64:80], scalar=1.0, in1=vec[:, V_GPF:V_GPF + 16],
                                                             op0=ALU.add, op1=ALU.mult), deps=[mod_tok[19], ldc_all])
        tf = [bview(RC, i * 4104, 4104, F32) for i in range(4)]
        tf_free = [[], [], [], []]
        h2_tok = None
        for blk in range(KC):
            b = blk % 4
            eng = "dve" if blk % 2 == 0 else "pool"
            if eng == "dve":
                t_a = P.op("dve", lambda e, blk=blk, b=b: e.scalar_tensor_tensor(out=tf[b], in0=mixT[:, blk, :], scalar=gsf[:, blk:blk + 1], in1=rstd_e,
                                                                                 op0=ALU.mult, op1=ALU.mult), deps=[t_r2, t_gsf, merged_dead] + tf_free[b])
                h2_tok = P.op("act", lambda e, blk=blk, b=b: e.activation(out=h2T[:, blk, :], in_=tf[b], func=AF.Identity,
                                                                          bias=mod_fm[:, 48 + blk:49 + blk], scale=1.0),
                              deps=[t_a, mod_tok[15], t_r2])
            else:
                t_a = P.op("pool", lambda e, blk=blk, b=b: e.tensor_tensor(out=tf[b], in0=mixT[:, blk, :], in1=rstd_e, op=ALU.mult),
                           deps=[t_r2, merged_dead] + tf_free[b])
                h2_tok = P.op("act", lambda e, blk=blk, b=b: e.activation(out=h2T[:, blk, :], in_=tf[b], func=AF.Identity,
                                                                          bias=mod_fm[:, 48 + blk:49 + blk], scale=gsf[:, blk:blk + 1]),
                              deps=[t_a, mod_tok[15], t_r2, t_gsf])
            tf_free[b] = [h2_tok]
        h2_ready = P.op("dve", lambda e: e.tensor_scalar(out=h2T[:, :, 0:2], in0=h2T[:, :, 0:2], scalar1=hmask[:, 0:1], scalar2=None, op0=ALU.mult),
                        deps=[h2_tok, ldc_all])
        mid_done = all_ad()

        stg = [SC[:, i * 1026:(i + 1) * 1026] for i in range(2)]
        cb = [SC[:, 2052 + i * 1024:2052 + (i + 1) * 1024] for i in range(2)]
        sqy = [SC[:, i * 512:(i + 1) * 512].bitcast(BF16) for i in range(2)]
        stg_free = [[], []]
        cb_free = [[], []]
        sqy_free = [[], []]
        nblk = [0]
        y_ready = [None] * KC
        act_free = []
        conv_last = [None]
        ystat_tok = [None]
        for k in range(NG):
            act_ready = [None] * GB
            for half in (0, 1):
                done = 0
                for sp_ in ffn_up_specs(k, half):
                    s, wt, t_w = load_piece(*sp_)
                    pn = sp_[4]
                    for bl in range(pn // 128):
                        m = done
                        done += 1
                        mg = k * GB + m
                        i, tok = fm_block([wt[:, kc, bl * 128:(bl + 1) * 128] for kc in range(KC)],
                                          lambda kc, c0, n: h2T[:, kc, c0:c0 + n], PCS_E, [t_w, h2_ready])
                        b = nblk[0] % 2
                        nblk[0] += 1
                        cwc = V_CW + (half * 44 + mg) * 3
                        cbc = V_CB + half * 44 + mg
                        t_st = P.op("act", lambda e, i=i, b=b: e.activation(out=v3(stg[b], 3), in_=blk_view(i, 342), func=AF.Copy),
                                    deps=[tok] + stg_free[b] + mid_done)
                        set_blk_free(i, [t_st])
                        t_c0 = P.op("act", lambda e, b=b, cwc=cwc, cbc=cbc: e.activation(out=cb[b], in_=stg[b][:, 2:NE], func=AF.Identity,
                                                                                        bias=vec[:, cbc:cbc + 1], scale=vec[:, cwc + 2:cwc + 3]),
                                    deps=[t_st, ldc_all] + cb_free[b])
                        t_c1 = P.op("dve", lambda e, b=b, cwc=cwc: e.scalar_tensor_tensor(out=cb[b], in0=stg[b][:, 1:NE - 1], scalar=vec[:, cwc + 1:cwc + 2], in1=cb[b],
                                                                                          op0=ALU.mult, op1=ALU.add), deps=[t_c0])
                        t_c2 = P.op("dve", lambda e, b=b, cwc=cwc: e.scalar_tensor_tensor(out=cb[b], in0=stg[b][:, 0:NE - 2], scalar=vec[:, cwc:cwc + 1], in1=cb[b],
                                                                                          op0=ALU.mult, op1=ALU.add), deps=[t_c1])
                        stg_free[b] = [t_c2]
                        if half == 0:
                            t_o = P.op("act", lambda e, b=b, m=m: e.activation(out=act[:, m, :], in_=cb[b], func=AF.Gelu_apprx_tanh),
                                       deps=[t_c2] + act_free + mid_done)
                        else:
                            t_o = P.op("dve", lambda e, b=b, m=m: e.tensor_tensor(out=act[:, m, :], in0=act[:, m, :], in1=cb[b], op=ALU.mult),
                                       deps=[t_c2, act_ready[m]])
                        cb_free[b] = [t_o]
                        act_ready[m] = t_o
                        conv_last[0] = t_o
                    ring_free[s].append(tok)
            act_all = list(act_ready)
            last_pe = None
            last_group = (k == NG - 1)
            if last_group:
                ys_first = [True]
            for q in range(4):
                s, wt, t_w = load_piece(w_down, k * GB, GB, q * 512, 512)
                for ob in range(4):
                    o = q * 4 + ob
                    i, tok = fm_block([wt[:, kc, ob * 128:(ob + 1) * 128] for kc in range(GB)],
                                      lambda kc, c0, n: act[:, kc, c0:c0 + n], PCS_T, [t_w] + act_all)
                    src = ps[:, 3 * i * 512:(3 * i + 2) * 512]
                    if k == 0:
                        if o % 2:
                            t_y = P.op("dve", lambda e, o=o, src=src: e.tensor_copy(out=yT[:, o, :], in_=src), deps=[tok, spill_all] + mid_done)
                        else:
                            t_y = P.op("act", lambda e, o=o, src=src: e.activation(out=yT[:, o, :], in_=src, func=AF.Copy), deps=[tok, spill_all] + mid_done)
                    else:
                        t_y = P.op("dve", lambda e, o=o, src=src: e.tensor_tensor(out=yT[:, o, :], in0=src, in1=yT[:, o, :], op=ALU.add),
                                   deps=[tok, y_ready[o]])
                    set_blk_free(i, [t_y])
                    y_ready[o] = t_y
                    last_pe = tok
                    if last_group:
                        b2 = o % 2
                        t_sq = P.op("act", lambda e, o=o, b2=b2: e.activation(out=sqy[b2], in_=yT[:, o, :], func=AF.Square),
                                    deps=[t_y, conv_last[0]] + sqy_free[b2])

                        def hook(o=o, b2=b2, t_sq=t_sq):
                            tk = None
                            for pi in range(2):
                                tk = P.op("pe", lambda e, pi=pi, b2=b2, o=o: e.matmul(bank(6 + pi), lhsT=ones_bf[:], rhs=sqy[b2][:, pi * 512:(pi + 1) * 512],
                                                                                      start=(o == 0), stop=(o == KC - 1)),
                                          deps=[t_sq] + (bank_free[6] + bank_free[7] if o == 0 else []), signal=(pi == 1))
                            sqy_free[b2] = [tk]
                            ystat_tok[0] = tk
                        pe_hooks.append(hook)
                ring_free[s].append(tok)
                if mod_queue and ((q == 0 and k < 3) or (k == 0 and q == 2)):
                    mod_piece(mod_queue.pop(0))
            act_free = [last_pe]
        for h in list(pe_hooks):
            h()
        del pe_hooks[:]

        rstd_y = SC[:, 2052:2052 + 1024]
        t_r3 = P.op("act", lambda e: e.activation(out=rstd_y, in_=ps[:, 3072:4096], func=AF.Sqrt, bias=epst[:], scale=1.0 / D),
                    deps=[ystat_tok[0], conv_last[0]])
        t_r3 = P.op("dve", lambda e: e.reciprocal(out=rstd_y, in_=rstd_y), deps=[t_r3])
        t_ggf = P.op("dve", lambda e: e.tensor_tensor(out=ggf, in0=mod_fm[:, 80:96], in1=vec[:, V_GQF:V_GQF + 16], op=ALU.mult),
                     deps=[mod_tok[23], ldc_all])
        cxf = [bview(RC, i * 4096, 4096, F32) for i in range(4)]
        t_xf = [None] * KC
        for blk in range(4):
            t_xf[blk] = P.dma("sp", lambda e, blk=blk: e.dma_start(out=cxf[blk], in_=outT[blk * 128:(blk + 1) * 128, :]), "cx%d" % blk,
                              deps=[last_pe, spill_all])
        fin = None
        for blk in range(KC):
            b = blk % 4
            t_a = P.op("dve", lambda e, blk=blk: e.scalar_tensor_tensor(out=yT[:, blk, :], in0=yT[:, blk, :], scalar=ggf[:, blk:blk + 1], in1=rstd_y,
                                                                        op0=ALU.mult, op1=ALU.mult), deps=[t_r3, t_ggf])
            t_o = P.op("pool", lambda e, blk=blk, b=b: e.tensor_tensor(out=yT[:, blk, :], in0=yT[:, blk, :], in1=cxf[b], op=ALU.add), deps=[t_a, t_xf[blk]])
            fin = P.dma("act", lambda e, blk=blk: e.dma_start(out=outT[blk * 128:(blk + 1) * 128, :], in_=yT[:, blk, :]), "fin", deps=[t_o, t_xf[blk]])
            if blk + 4 < KC:
                t_xf[blk + 4] = P.dma("sp", lambda e, blk=blk, b=b: e.dma_start(out=cxf[b], in_=outT[(blk + 4) * 128:(blk + 5) * 128, :]), "cx%d" % b,
                                      deps=[t_o])
        P.wait("sp", [("fin", P.cnt["fin"])])
        if DEBUG:
            P.wait("sp", [("dbg", P.cnt["dbg"])])
        P.emit(block, sems)
    return nc


_NC_CACHE = {}


def _pool_mats(first_core):
    A = np.zeros((4, 4, 128, 128), np.float32)
    for gi, win in enumerate(POOL_WINDOWS):
        for t in range(128):
            for j in range(t - win + 1, t + 1):
                if j >= 0:
                    A[2, gi, j, t] += 1.0 / win
                else:
                    A[3, gi, 128 + j, t] += 1.0 / win
            A[2, gi, t, t] -= 1.0
            cnt = min(t + 1, win)
            for j in range(max(0, t - win + 1), t + 1):
                A[0, gi, j, t] += 1.0 / cnt
            A[0, gi, t, t] -= 1.0
    if not first_core:
        A[0] = A[2]
        A[1] = A[3]
    return np.ascontiguousarray(A.transpose(2, 0, 1, 3).reshape(128, 16 * 128))


def kernel(x, c, w_ada, b_ada, g_pre_mix, g_post_mix, w_in, ln_v_g, ln_v_b, w_spatial, b_spatial,
           w_pool, pool_scale, w_branch_a, w_branch_b, w_out, g_pre_ffn, g_post_ffn, w_up, conv_w,
           conv_b, w_down):
    f = np.float32
    x = np.asarray(x, f)
    S = x.shape[1]
    assert S == NCORE * TOK

    def fm(v, nblk):
        return np.asarray(v, f).reshape(nblk, 128).T

    xs = x[0]
    xpad = np.concatenate([np.zeros((128, D), f), xs], axis=0)
    vec_fm = np.concatenate([
        fm(g_pre_mix[0], 16), fm(g_post_mix[0], 16), fm(g_pre_ffn[0], 16), fm(g_post_ffn[0], 16),
        fm(pool_scale[0], 8),
        np.asarray(conv_w[0], f).T.reshape(88, 128, 3).transpose(1, 0, 2).reshape(128, 264),
        fm(conv_b[0], 88)], axis=1)
    vec_fm = np.ascontiguousarray(vec_fm, f)
    assert vec_fm.shape == (128, NV)
    common = {
        "c_col": np.ascontiguousarray(fm(c[0], 16)),
        "w_ada": np.ascontiguousarray(w_ada[0], f),
        "b_ada_fm": np.ascontiguousarray(fm(b_ada[0], 96)),
        "vec_fm": vec_fm,
        "ln_rows": np.ascontiguousarray(np.stack([ln_v_g[0], ln_v_b[0]]), f),
        "bs_exp": np.ascontiguousarray(np.repeat(np.asarray(b_spatial[0], f), 2, axis=0).reshape(1, 2048)),
        "wsT": np.ascontiguousarray(np.asarray(w_spatial[0], f).transpose(2, 0, 1).reshape(128, 1024)),
        "tri": np.ascontiguousarray(np.triu(np.ones((128, 128), f))),
        "wpool": np.ascontiguousarray(np.asarray(w_pool[0], f).reshape(4, 2, 128, 256).transpose(2, 0, 1, 3).reshape(128, 2048)),
        "w_in": np.ascontiguousarray(w_in[0], f),
        "w_branch_a": np.ascontiguousarray(w_branch_a[0], f),
        "w_branch_b": np.ascontiguousarray(w_branch_b[0], f),
        "w_out": np.ascontiguousarray(w_out[0], f),
        "w_up": np.ascontiguousarray(w_up[0], f),
        "w_down": np.ascontiguousarray(w_down[0], f),
    }
    in_maps = []
    for core in range(NCORE):
        m = dict(common)
        m["xT"] = np.ascontiguousarray(xpad[core * TOK:core * TOK + TM].T)
        m["poolA"] = _pool_mats(core == 0)
        m["hmask"] = np.full((128, 1), 0.0 if core == 0 else 1.0, f)
        in_maps.append(m)
    if "nc" not in _NC_CACHE:
        _NC_CACHE["nc"] = build_program()
    res = run_bass_kernel_spmd(_NC_CACHE["nc"], in_maps, core_ids=list(range(NCORE)))
    if DEBUG:
        _NC_CACHE["dbg"] = [{k: np.asarray(v) for k, v in r.items() if k.startswith("dbg_")} for r in res.results]
    outs = [np.asarray(r["outT"], f) for r in res.results]
    full = np.concatenate(outs, axis=1).T
    return np.ascontiguousarray(full[None], f)
```

```python
import contextlib
import numpy as np
import concourse.bass as bass
import concourse.mybir as mybir
from concourse.bass_utils import run_bass_kernel_spmd

F32 = mybir.dt.float32
BF16 = mybir.dt.bfloat16
AF = mybir.ActivationFunctionType
ALU = mybir.AluOpType
AX = mybir.AxisListType

NCORE = 8
D = 2048
KC = 16
TOK = 1024
TM = 1152
NCH = 9
NE = 1026
FFN = 5632
NG = 4
GB = 11
EPS = 1e-6
POOL_WINDOWS = (2, 4, 8, 16)

V_GPM, V_GQM, V_GPF, V_GQF, V_PS, V_CW, V_CB = 0, 16, 32, 48, 64, 72, 72 + 264
NV = 72 + 264 + 88


class Plan:
    ENGS = ("pe", "act", "dve", "pool", "sp")

    def __init__(self):
        self.q = {e: [] for e in self.ENGS}
        self.cnt = {e: 0 for e in self.ENGS}
        self.waited = {e: {} for e in self.ENGS}
        self.semnames = list(self.ENGS)

    def new_sem(self, name):
        self.cnt[name] = 0
        self.semnames.append(name)
        return name

    def _waits(self, eng, deps):
        for d in deps:
            if d is None:
                continue
            s, v = d
            if v <= 0 or self.waited[eng].get(s, 0) >= v:
                continue
            self.waited[eng][s] = v
            self.q[eng].append(("w", s, v))

    def op(self, eng, fn, deps=(), signal=True):
        self._waits(eng, deps)
        if signal:
            self.cnt[eng] += 1
            self.q[eng].append(("o", fn, eng, 1))
            return (eng, self.cnt[eng])
        self.q[eng].append(("o", fn, None, 0))
        return None

    def dma(self, eng, fn, sem, deps=()):
        self._waits(eng, deps)
        self.cnt[sem] += 16
        self.q[eng].append(("o", fn, sem, 16))
        return (sem, self.cnt[sem])

    def wait(self, eng, deps):
        self._waits(eng, deps)

    def emit(self, block, sems):
        def run(engname, e):
            for it in self.q[engname]:
                if it[0] == "w":
                    e.wait_ge(sems[it[1]], it[2])
                else:
                    ins = it[1](e)
                    if it[2] is not None:
                        ins.then_inc(sems[it[2]], it[3])
        if self.q["pe"]:
            block.tensor(lambda e: run("pe", e))
        if self.q["act"]:
            block.scalar(lambda e: run("act", e))
        if self.q["dve"]:
            block.vector(lambda e: run("dve", e))
        if self.q["pool"]:
            block.gpsimd(lambda e: run("pool", e))
        if self.q["sp"]:
            block.sync(lambda e: run("sp", e))


DEBUG = False


def build_program():
    nc = bass.Bass("TRN2", target_bir_lowering=False)
    dbg = {}
    if DEBUG:
        for nm, n in (("dbg_h", 16 * 1152), ("dbg_vn", 9 * 2048), ("dbg_sg", 16 * 1026), ("dbg_mg", 16 * 1026), ("dbg_q", 8 * 1026)):
            dbg[nm] = nc.dram_tensor(nm, [128, n], BF16, kind="ExternalOutput").ap()

    def din(name, shape, dt=F32):
        return nc.dram_tensor(name, list(shape), dt, kind="ExternalInput").ap()

    xT = din("xT", [D, TM])
    c_col = din("c_col", [128, KC])
    w_ada = din("w_ada", [D, 6 * D])
    b_ada_fm = din("b_ada_fm", [128, 96])
    vec_fm_d = din("vec_fm", [128, NV])
    ln_rows = din("ln_rows", [2, D])
    bs_exp_d = din("bs_exp", [1, 2048])
    wsT_d = din("wsT", [128, 8 * 128])
    tri_d = din("tri", [128, 128])
    poolA_d = din("poolA", [128, 16 * 128])
    wpool_d = din("wpool", [128, 4 * 2 * 256])
    hmask_d = din("hmask", [128, 1])
    w_in = din("w_in", [D, 9216])
    w_ba = din("w_branch_a", [D, D])
    w_bb = din("w_branch_b", [1024, D])
    w_out = din("w_out", [D, D])
    w_up = din("w_up", [D, 2 * FFN])
    w_down = din("w_down", [FFN, D])
    outT = nc.dram_tensor("outT", [D, TOK], F32, kind="ExternalOutput").ap()

    P = Plan()
    P.new_sem("dbg")

    def dump(nm, src2d, deps):
        if DEBUG:
            P.dma("sp", lambda e: e.dma_start(out=dbg[nm][:, :], in_=src2d), "dbg", deps=deps)
    for s in ("ring0", "ring1", "ring2", "ld", "ld2", "ldc", "xin", "st", "fin", "cx0", "cx1", "cx2", "cx3"):
        P.new_sem(s)

    with contextlib.ExitStack() as es:
        def sb(name, shape, dt):
            return es.enter_context(nc.sbuf_tensor(name, list(shape), dt))

        R1 = sb("R1", [128, 34848], BF16)
        RC = sb("RC", [128, 16416], BF16)
        RD = sb("RD", [128, 18432], BF16)
        RING = sb("RING", [128, 3 * 8192], BF16)
        SC = sb("SC", [128, 4104], F32)
        ps = es.enter_context(nc.psum_tensor("ps", [128, 4096], F32))

        def bview(reg, off_b, nbytes, dt):
            a = reg[:, off_b // 2:(off_b + nbytes) // 2]
            return a if dt == BF16 else a.bitcast(F32)

        def v3(ap2, k):
            return ap2.rearrange("p (k n) -> p k n", k=k)

        gu = v3(bview(R1, 0, 32832, BF16), 16)
        gv = v3(bview(R1, 32832, 36864, BF16), 9)
        yb = gu
        poolA = v3(bview(R1, 0, 4096, BF16), 16)
        wpool = bview(R1, 4096, 4096, BF16).rearrange("p (g k e) -> p g k e", g=4, k=2)
        p_tm = v3(bview(R1, 32832, 18432, BF16), 9)
        qT = v3(bview(R1, 32832, 16416, BF16), 8)
        pooledT = v3(bview(R1, 51264, 16416, BF16), 8)
        mixT = v3(bview(R1, 0, 65664, F32), 16)
        yT = v3(bview(R1, 0, 65536, F32), 16)
        ya = v3(bview(RC, 0, 32832, BF16), 16)
        ln_g_bc = bview(RC, 0, 8192, F32)
        ln_b_bc = bview(RC, 8192, 8192, F32)
        WmT = v3(bview(RC, 16384, 2048, BF16), 8)
        bs2 = bview(RC, 18432, 4096, BF16)
        wsT_f = v3(bview(RC, 22528, 4096, F32), 8)
        tri = bview(RC, 26624, 512, F32)
        act = v3(bview(RC, 0, 22528, BF16), 11)
        cxin = [bview(RC, i * 4104, 4104, F32) for i in range(2)]
        hT = v3(bview(RD, 0, 36864, BF16), 16)
        h2T = v3(bview(RD, 0, 32832, BF16), 16)

        ring = [v3(RING[:, i * 8192:(i + 1) * 8192], 16) for i in range(3)]
        ring_flat = [RING[:, i * 8192:(i + 1) * 8192] for i in range(3)]

        mod_fm = sb("mod_fm", [128, 96], F32)
        bada = sb("bada", [128, 96], F32)
        vec = sb("vec", [128, NV], F32)
        derived = sb("derived", [128, 64], F32)
        gs_m, ggm, gsf, ggf = (derived[:, 0:16], derived[:, 16:32], derived[:, 32:48], derived[:, 48:64])
        ccol = sb("ccol", [128, KC], F32)
        scb = sb("scb", [128, KC], BF16)
        epst = sb("epst", [128, 1], F32)
        hmask = sb("hmask_sb", [128, 1], F32)
        ones_bf = sb("ones_bf", [128, 128], BF16)
        onesrow = sb("onesrow", [1, 128], F32)
        ones2 = sb("ones2", [2, 128], BF16)
        modrow = sb("modrow", [1, 512], F32)
        vsum = sb("vsum", [128, 36], F32)
        vstat = sb("vstat", [128, 5 * 9], F32)
        vsq, vmean, vmsq, vvar, vrstd = (vstat[:, i * 9:(i + 1) * 9] for i in range(5))

        def scv(off, n, dt=F32):
            a = SC[:, off:off + n]
            return a if dt == F32 else a.bitcast(BF16)

        sems = {n: es.enter_context(nc.semaphore(n)) for n in P.semnames}
        block = es.enter_context(nc.Block())

        ring_free = [[], [], []]
        ring_next = [0]
        prefetched = []

        def _load(spec):
            w_ap, r0, nk, c0, ncols = spec
            s = ring_next[0] % 3
            ring_next[0] += 1
            dst = ring_flat[s][:, 0:nk * ncols].rearrange("p (k n) -> p k n", k=nk)
            src = w_ap[r0 * 128:(r0 + nk) * 128, c0:c0 + ncols].rearrange("(k p) n -> p k n", p=128)
            tok = P.dma("pool", lambda e: e.dma_start(out=dst, in_=src), "ring%d" % s, deps=ring_free[s])
            ring_free[s] = []
            return s, dst, tok

        def prefetch(w_ap, r0, nk, c0, ncols):
            spec = (id(w_ap.tensor) if hasattr(w_ap, "tensor") else id(w_ap), r0, nk, c0, ncols)
            prefetched.append((spec, _load((w_ap, r0, nk, c0, ncols))))

        def load_piece(w_ap, r0, nk, c0, ncols):
            spec = (id(w_ap.tensor) if hasattr(w_ap, "tensor") else id(w_ap), r0, nk, c0, ncols)
            if prefetched and prefetched[0][0] == spec:
                return prefetched.pop(0)[1]
            return _load((w_ap, r0, nk, c0, ncols))

        bank_free = [[] for _ in range(8)]
        blk_next = [0]
        tm_next = [0]

        def bank(b):
            return ps[:, b * 512:(b + 1) * 512]

        def alloc_blk():
            i = blk_next[0] % 2
            blk_next[0] += 1
            deps = bank_free[3 * i] + bank_free[3 * i + 1] + bank_free[3 * i + 2]
            return i, deps

        def blk_view(i, n):
            return ps[:, 3 * i * 512:(3 * i + 3) * 512].rearrange("p (k n) -> p k n", k=3)[:, :, 0:n]

        def set_blk_free(i, toks):
            for b in range(3):
                bank_free[3 * i + b] = list(toks)

        def alloc_tm():
            b = tm_next[0] % 6
            tm_next[0] += 1
            return b, bank_free[b]

        pe_hooks = []

        def fm_block(lhsTs, rhs_fn, pieces, deps, kdeps=None):
            i, bdeps = alloc_blk()
            nk = len(lhsTs)
            tok = None
            first = True
            for kc in range(nk):
                for pi, (c0, n) in enumerate(pieces):
                    out_ap = bank(3 * i + pi)[:, 0:n]
                    last = (kc == nk - 1) and (pi == len(pieces) - 1)
                    tok = P.op("pe", lambda e, o=out_ap, l=lhsTs[kc], r=rhs_fn(kc, c0, n), st=(kc == 0), sp=(kc == nk - 1):
                               e.matmul(o, lhsT=l, rhs=r, start=st, stop=sp),
                               deps=((list(deps) + bdeps) if first else []) + ([kdeps[kc]] if (kdeps and pi == 0) else []), signal=last)
                    first = False
            hooks = list(pe_hooks)
            del pe_hooks[:]
            for h in hooks:
                h()
            return i, tok

        PCS_H = [(126, 342), (468, 342), (810, 342)]
        PCS_E = [(0, 342), (342, 342), (684, 342)]
        PCS_T = [(0, 512), (512, 512)]

        def all_ad():
            return [("act", P.cnt["act"]), ("dve", P.cnt["dve"]), ("pool", P.cnt["pool"])]

        t_c = P.dma("sp", lambda e: e.dma_start(out=ccol[:], in_=c_col[:, :]), "ldc")
        P.dma("sp", lambda e: e.dma_start(out=bada[:], in_=b_ada_fm[:, :]), "ldc")
        P.dma("sp", lambda e: e.dma_start(out=vec[:], in_=vec_fm_d[:, :]), "ldc")
        P.dma("sp", lambda e: e.dma_start(out=hmask[:], in_=hmask_d[:, :]), "ldc")
        ldc_all = ("ldc", P.cnt["ldc"])
        t_ms = P.op("dve", lambda e: e.memset(epst[:], EPS))
        P.op("dve", lambda e: e.memset(ones_bf[:], 1.0))
        P.op("dve", lambda e: e.memset(ones2[:], 1.0))
        t_ones = P.op("dve", lambda e: e.memset(onesrow[:], 1.0))
        P.op("dve", lambda e: e.memset(vsum[:], 0.0))
        t_vz = P.op("dve", lambda e: e.memset(vstat[:], 0.0))
        t_init = t_vz
        t_sc = P.op("act", lambda e: e.activation(out=scb[:], in_=ccol[:], func=AF.Silu), deps=[ldc_all, t_init])

        xres = [bview(R1, kc * 4608, 4608, F32) for kc in range(15)] + [bview(RC, 0, 4608, F32)]
        rstd0 = bview(RC, 4608, 4608, F32)
        sqb = [SC[:, i * 576:(i + 1) * 576].bitcast(BF16) for i in range(2)]
        tfx = [SC[:, 1152 + i * 1152:1152 + (i + 1) * 1152] for i in range(2)]
        for kc in range(KC):
            P.dma("sp", lambda e, kc=kc: e.dma_start(out=xres[kc], in_=xT[kc * 128:(kc + 1) * 128, :]), "xin")
        t_xl = [("xin", P.cnt["xin"])] * KC
        S0_PCS = [(0, 384), (384, 384), (768, 384)]
        st_i, st_deps = alloc_blk()
        stat_tok = None
        sq_free = [[], []]
        for kc in range(KC):
            b = kc % 2
            t_sq = P.op("act", lambda e, b=b, kc=kc: e.activation(out=sqb[b], in_=xres[kc], func=AF.Square),
                        deps=[t_xl[kc]] + sq_free[b])
            for pi, (c0, n) in enumerate(S0_PCS):
                stat_tok = P.op("pe", lambda e, b=b, pi=pi, c0=c0, n=n, kc=kc:
                                e.matmul(bank(3 * st_i + pi)[:, 0:n], lhsT=ones_bf[:], rhs=sqb[b][:, c0:c0 + n],
                                         start=(kc == 0), stop=(kc == KC - 1)),
                                deps=[t_sq, t_init] + (st_deps if kc == 0 else []), signal=(pi == 2))
            sq_free[b] = [stat_tok]
        t_r0 = P.op("act", lambda e: e.activation(out=v3(rstd0, 3), in_=blk_view(st_i, 384), func=AF.Sqrt,
                                                  bias=epst[:], scale=1.0 / D), deps=[stat_tok, t_ms])
        set_blk_free(st_i, [t_r0])
        t_r0 = P.op("dve", lambda e: e.reciprocal(out=rstd0, in_=rstd0), deps=[t_r0])

        mod_tok = [None] * 24
        modrow_free = [[]]
        pm_free = [[]]

        def mod_piece(j):
            s, wt, t_w = load_piece(w_ada, 0, KC, j * 512, 512)
            tok = None
            for kc in range(KC):
                tok = P.op("pe", lambda e, kc=kc, wt=wt: e.matmul(ps[0:1, 3584:4096], lhsT=scb[:, kc:kc + 1], rhs=wt[:, kc, :],
                                                                  start=(kc == 0), stop=(kc == KC - 1)),
                           deps=[t_w, t_sc] + pm_free[0] + bank_free[7] if kc == 0 else (), signal=(kc == KC - 1))
            ring_free[s].append(tok)
            mr = modrow[0:1, 0:512]
            t_cp = P.op("dve", lambda e, mr=mr: e.tensor_copy(out=mr, in_=ps[0:1, 3584:4096]), deps=[tok] + modrow_free[0])
            pm_free[0] = [t_cp]
            bank_free[7] = [t_cp]
            t4 = None
            for q in range(4):
                t4 = P.op("pe", lambda e, q=q, mr=mr: e.matmul(ps[:, 3072 + q:3072 + q + 1], lhsT=mr[0:1, q * 128:(q + 1) * 128],
                                                               rhs=onesrow[0:1, 0:1], start=True, stop=True),
                          deps=[t_cp, t_ones] + bank_free[6] if q == 0 else (), signal=(q == 3))
            modrow_free[0] = [t4]
            t_m = P.op("dve", lambda e, j=j: e.tensor_tensor(out=mod_fm[:, 4 * j:4 * j + 4], in0=ps[:, 3072:3076],
                                                            in1=bada[:, 4 * j:4 * j + 4], op=ALU.add), deps=[t4, ldc_all])
            bank_free[6] = [t_m]
            mod_tok[j] = t_m

        for j in range(8):
            mod_piece(j)
        t_gsm = P.op("dve", lambda e: e.scalar_tensor_tensor(out=gs_m, in0=mod_fm[:, 16:32], scalar=1.0, in1=vec[:, V_GPM:V_GPM + 16],
                                                             op0=ALU.add, op1=ALU.mult), deps=[mod_tok[7], ldc_all])
        hT_ready = None
        tfx_free = [[], []]
        for kc in range(KC):
            b = kc % 2
            t_a = P.op("dve", lambda e, kc=kc, b=b: e.scalar_tensor_tensor(out=tfx[b], in0=xres[kc], scalar=gs_m[:, kc:kc + 1], in1=rstd0,
                                                                           op0=ALU.mult, op1=ALU.mult), deps=[t_xl[kc], t_gsm, t_r0] + tfx_free[b])
            hT_ready = P.op("act", lambda e, kc=kc, b=b: e.activation(out=hT[:, kc, :], in_=tfx[b], func=AF.Identity,
                                                                      bias=mod_fm[:, kc:kc + 1], scale=1.0), deps=[t_a, mod_tok[3]])
            tfx_free[b] = [hT_ready]
        s0_done = all_ad()
        dump("dbg_h", RD[:, 0:16 * 1152], [hT_ready])

        P.dma("sp", lambda e: e.dma_start(out=ln_g_bc, in_=ln_rows[0:1, :].broadcast_to([128, D])), "ld", deps=s0_done)
        P.dma("sp", lambda e: e.dma_start(out=ln_b_bc, in_=ln_rows[1:2, :].broadcast_to([128, D])), "ld")
        P.dma("sp", lambda e: e.dma_start(out=wsT_f.rearrange("p k n -> p (k n)"), in_=wsT_d[:, :]), "ld")
        P.dma("sp", lambda e: e.dma_start(out=tri, in_=tri_d[:, :]), "ld")
        bsf = SC[0:1, 0:2048]
        bslo = SC[0:1, 2048:3072].bitcast(BF16)
        P.dma("sp", lambda e: e.dma_start(out=bsf, in_=bs_exp_d[:, :]), "ld")
        ld_s3 = ("ld", P.cnt["ld"])
        t_wm = None
        for h in range(8):
            t_wm = P.op("dve", lambda e, h=h: e.tensor_tensor(out=WmT[:, h, :], in0=wsT_f[:, h, :], in1=tri, op=ALU.mult), deps=[ld_s3])

        mod_queue = list(range(8, 24))
        mod_ctr = [0]

        def maybe_mod(every=2, limit=20):
            mod_ctr[0] += 1
            if mod_queue and mod_queue[0] < limit and mod_ctr[0] % every == 0:
                mod_piece(mod_queue.pop(0))

        gv_ready = None
        for g in range(4):
            s, wt, t_w = load_piece(w_in, 0, KC, 2048 + g * 512, 512)
            if g == 0:
                t_bh = P.dma("pool", lambda e: e.dma_start(out=bs2[0:1, :], in_=bs_exp_d[:, :]), "ld2", deps=s0_done)
            for c in range(NCH):
                b, bdeps = alloc_tm()
                tok = None
                for kc in range(KC):
                    tok = P.op("pe", lambda e, b=b, kc=kc, c=c, wt=wt: e.matmul(bank(b), lhsT=hT[:, kc, c * 128:(c + 1) * 128], rhs=wt[:, kc, :],
                                                                                start=(kc == 0), stop=(kc == KC - 1)),
                               deps=[t_w, hT_ready] + bdeps if kc == 0 else (), signal=(kc == KC - 1))
                gv_ready = P.op("act", lambda e, b=b, c=c, g=g: e.activation(out=gv[:, c, g * 512:(g + 1) * 512], in_=bank(b), func=AF.Gelu_apprx_tanh,
                                                                             accum_out=vsum[:, c * 4 + g:c * 4 + g + 1]), deps=[tok, t_vz] + s0_done)
                bank_free[b] = [gv_ready]
            ring_free[s].append(tok)
            maybe_mod()
        t_lo = P.op("dve", lambda e: e.tensor_tensor(out=bslo, in0=bsf, in1=bs2[0:1, :], op=ALU.subtract), deps=[ld_s3, t_bh])
        t_bl = P.dma("sp", lambda e: e.dma_start(out=bs2[1:2, :], in_=bslo), "ld2", deps=[t_lo])
        ld2_all = ("ld2", P.cnt["ld2"])

        junk = SC[:, 1024:2048].bitcast(BF16)
        t_q = None
        for c in range(NCH):
            t_q = P.op("act", lambda e, c=c: e.activation(out=junk, in_=gv[:, c, :], func=AF.Square, accum_out=vsq[:, c:c + 1]),
                       deps=[gv_ready, hT_ready, t_lo, t_bl])
        t = P.op("dve", lambda e: e.tensor_reduce(out=vmean, in_=vsum[:].rearrange("p (c g) -> p c g", g=4), axis=AX.X, op=ALU.add), deps=[gv_ready])
        t = P.op("dve", lambda e: e.tensor_scalar(out=vmean, in0=vmean, scalar1=1.0 / D, scalar2=None, op0=ALU.mult), deps=[t])
        t = P.op("dve", lambda e: e.tensor_tensor(out=vmsq, in0=vmean, in1=vmean, op=ALU.mult), deps=[t])
        t = P.op("dve", lambda e: e.scalar_tensor_tensor(out=vvar, in0=vsq, scalar=1.0 / D, in1=vmsq, op0=ALU.mult, op1=ALU.subtract), deps=[t, t_q])
        t = P.op("act", lambda e: e.activation(out=vvar, in_=vvar, func=AF.Sqrt, bias=epst[:], scale=1.0), deps=[t])
        t_vr = P.op("dve", lambda e: e.reciprocal(out=vrstd, in_=vvar), deps=[t])
        vn_ready = [None] * NCH
        for c in range(NCH):
            t1 = P.op("dve", lambda e, c=c: e.tensor_scalar(out=gv[:, c, :], in0=gv[:, c, :], scalar1=vmean[:, c:c + 1], scalar2=vrstd[:, c:c + 1],
                                                            op0=ALU.subtract, op1=ALU.mult), deps=[t_vr, t_q])
            t2 = P.op("dve", lambda e, c=c: e.tensor_tensor(out=gv[:, c, :], in0=gv[:, c, :], in1=ln_g_bc, op=ALU.mult), deps=[t1, ld_s3])
            vn_ready[c] = P.op("dve", lambda e, c=c: e.tensor_tensor(out=gv[:, c, :], in0=gv[:, c, :], in1=ln_b_bc, op=ALU.add), deps=[t2])

        dump("dbg_vn", R1[:, 16416:16416 + 9 * 2048], [vn_ready[8]])
        evac_rr = [0]
        gu_ready = None
        for g in range(4):
            s, wt, t_w = load_piece(w_in, 0, KC, g * 512, 512)
            for bl in range(4):
                blk = g * 4 + bl
                i, tok = fm_block([wt[:, kc, bl * 128:(bl + 1) * 128] for kc in range(KC)],
                                  lambda kc, c0, n: hT[:, kc, c0:c0 + n], PCS_H, [t_w, hT_ready])
                gu_ready = P.op("act", lambda e, i=i, blk=blk: e.activation(out=v3(gu[:, blk, :], 3), in_=blk_view(i, 342), func=AF.Gelu_apprx_tanh),
                                deps=[tok] + s0_done)
                set_blk_free(i, [gu_ready])
            ring_free[s].append(tok)
            maybe_mod()

        sg_ready = None
        for c in range(NCH):
            for q in range(4):
                b, bdeps = alloc_tm()
                P.op("pe", lambda e, b=b, q=q: e.matmul(bank(b), lhsT=ones2[0:2, :], rhs=bs2[0:2, q * 512:(q + 1) * 512], start=True, stop=False),
                     deps=[ld2_all, vn_ready[c], t_wm] + bdeps, signal=False)
                tok = None
                for f in range(4):
                    fb = q * 4 + f
                    h = fb // 2
                    tok = P.op("pe", lambda e, b=b, f=f, fb=fb, c=c, h=h: e.matmul(bank(b)[:, f * 128:(f + 1) * 128], lhsT=gv[:, c, fb * 128:(fb + 1) * 128],
                                                                                  rhs=WmT[:, h, :], start=False, stop=(f == 3)), signal=(f == 3))
                pv = bank(b).rearrange("p (k n) -> p k n", k=4)
                if c == 0:
                    o = gu[:, q * 4:q * 4 + 4, 0:2]
                    i0 = pv[:, :, 126:128]
                else:
                    o = gu[:, q * 4:q * 4 + 4, 2 + (c - 1) * 128:2 + c * 128]
                    i0 = pv
                sg_ready = P.op("dve", lambda e, o=o, i0=i0: e.tensor_tensor(out=o, in0=i0, in1=o, op=ALU.mult), deps=[tok, gu_ready])
                bank_free[b] = [sg_ready]

        dump("dbg_sg", R1[:, 0:16 * 1026], [sg_ready])
        def evac_copy(i, out3, deps):
            k = evac_rr[0] % 2
            evac_rr[0] += 1
            if k == 0:
                t = P.op("act", lambda e: e.activation(out=out3, in_=blk_view(i, 342), func=AF.Copy), deps=deps)
            else:
                t = P.op("dve", lambda e: e.tensor_copy(out=out3, in_=blk_view(i, 342)), deps=deps)
            set_blk_free(i, [t])
            return t

        ya_ready = None
        for g in range(4):
            s, wt, t_w = load_piece(w_ba, 0, KC, g * 512, 512)
            for bl in range(4):
                blk = g * 4 + bl
                i, tok = fm_block([wt[:, kc, bl * 128:(bl + 1) * 128] for kc in range(KC)],
                                  lambda kc, c0, n: gu[:, kc, c0:c0 + n], PCS_E, [t_w, sg_ready])
                ya_ready = evac_copy(i, v3(ya[:, blk, :], 3), [tok, sg_ready])
            ring_free[s].append(tok)
            maybe_mod()
        s4_last_pe = tok
        ya_all = all_ad()

        sgt = [SC[:, i * 1026:(i + 1) * 1026] for i in range(2)]
        tmpm = [SC[:, 2052 + i * 1026:2052 + (i + 1) * 1026] for i in range(2)]
        sgt_free = [[], []]
        tmp_free = [[], []]
        n9 = [0]
        st9 = {"merged": None, "last_pe": None, "yb_all": []}

        def gate_piece(stage, g):
            s, wt, t_w = load_piece(w_in, 0, KC, 5120 + stage * 2048 + g * 512, 512)
            for bl in range(4):
                blk = g * 4 + bl
                i, tok = fm_block([wt[:, kc, bl * 128:(bl + 1) * 128] for kc in range(KC)],
                                  lambda kc, c0, n: hT[:, kc, c0:c0 + n], PCS_H, [t_w])
                k = n9[0] % 2
                n9[0] += 1
                t_s = P.op("act", lambda e, i=i, k=k: e.activation(out=v3(sgt[k], 3), in_=blk_view(i, 342), func=AF.Sigmoid), deps=[tok, t_q] + sgt_free[k])
                set_blk_free(i, [t_s])
                if stage == 0:
                    st9["merged"] = P.op("dve", lambda e, blk=blk, k=k: e.tensor_tensor(out=ya[:, blk, :], in0=ya[:, blk, :], in1=sgt[k], op=ALU.mult),
                                         deps=[t_s] + ya_all)
                    sgt_free[k] = [st9["merged"]]
                else:
                    t_m = P.op("dve", lambda e, blk=blk, k=k: e.tensor_tensor(out=tmpm[k], in0=yb[:, blk, :], in1=sgt[k], op=ALU.mult),
                               deps=[t_s] + st9["yb_all"] + tmp_free[k])
                    sgt_free[k] = [t_m]
                    st9["merged"] = P.op("dve", lambda e, blk=blk, k=k: e.tensor_tensor(out=ya[:, blk, :], in0=ya[:, blk, :], in1=tmpm[k], op=ALU.add),
                                         deps=[t_m])
                    tmp_free[k] = [st9["merged"]]
                st9["last_pe"] = tok
            ring_free[s].append(tok)
            maybe_mod()

        for g in range(2):
            s, wt, t_w = load_piece(w_in, 0, KC, 4096 + g * 512, 512)
            for c in range(NCH):
                b, bdeps = alloc_tm()
                tok = None
                for kc in range(KC):
                    tok = P.op("pe", lambda e, b=b, kc=kc, c=c, wt=wt: e.matmul(bank(b), lhsT=hT[:, kc, c * 128:(c + 1) * 128], rhs=wt[:, kc, :],
                                                                                start=(kc == 0), stop=(kc == KC - 1)),
                               deps=[t_w, sg_ready] + bdeps if kc == 0 else (), signal=(kc == KC - 1))
                k = evac_rr[0] % 2
                evac_rr[0] += 1
                if k == 0:
                    t_p = P.op("act", lambda e, b=b, c=c, g=g: e.activation(out=p_tm[:, c, g * 512:(g + 1) * 512], in_=bank(b), func=AF.Copy), deps=[tok, sg_ready])
                else:
                    t_p = P.op("dve", lambda e, b=b, c=c, g=g: e.tensor_copy(out=p_tm[:, c, g * 512:(g + 1) * 512], in_=bank(b)), deps=[tok, sg_ready])
                bank_free[b] = [t_p]
            ring_free[s].append(tok)
            maybe_mod()
        p_all = all_ad()
        P.dma("pool", lambda e: e.dma_start(out=poolA.rearrange("p k n -> p (k n)"), in_=poolA_d[:, :]), "ld", deps=[s4_last_pe])
        P.dma("pool", lambda e: e.dma_start(out=wpool.rearrange("p g k e -> p (g k e)"), in_=wpool_d[:, :]), "ld", deps=[s4_last_pe])
        ld_s6 = ("ld", P.cnt["ld"])
        gate_piece(0, 0)
        pooled_ready = None
        for c in range(NCH):
            kind = 0 if c == 1 else 2
            for q in range(2):
                b, bdeps = alloc_tm()
                tok = None
                for f in range(4):
                    fb = q * 4 + f
                    gi = fb // 2
                    two = c >= 1
                    tok = P.op("pe", lambda e, b=b, f=f, fb=fb, c=c, gi=gi, kind=kind, two=two:
                               e.matmul(bank(b)[:, f * 128:(f + 1) * 128], lhsT=p_tm[:, c, fb * 128:(fb + 1) * 128],
                                        rhs=poolA[:, kind * 4 + gi, :], start=True, stop=(not two)),
                               deps=p_all + [ld_s6] + bdeps if f == 0 else (), signal=(f == 3 and not two))
                    if two:
                        tok = P.op("pe", lambda e, b=b, f=f, fb=fb, c=c, gi=gi, kind=kind:
                                   e.matmul(bank(b)[:, f * 128:(f + 1) * 128], lhsT=p_tm[:, c - 1, fb * 128:(fb + 1) * 128],
                                            rhs=poolA[:, (kind + 1) * 4 + gi, :], start=False, stop=True), signal=(f == 3))
                pv = bank(b).rearrange("p (k n) -> p k n", k=4)
                if c == 0:
                    o = pooledT[:, q * 4:q * 4 + 4, 0:2]
                    i0 = pv[:, :, 126:128]
                else:
                    o = pooledT[:, q * 4:q * 4 + 4, 2 + (c - 1) * 128:2 + c * 128]
                    i0 = pv
                pooled_ready = P.op("act", lambda e, o=o, i0=i0: e.activation(out=o, in_=i0, func=AF.Copy), deps=[tok])
                bank_free[b] = [pooled_ready]
        gate_piece(0, 1)
        q_ready = None
        for gi in range(4):
            for eb in range(2):
                i, tok = fm_block([wpool[:, gi, k2, eb * 128:(eb + 1) * 128] for k2 in range(2)],
                                  lambda k2, c0, n, gi=gi: pooledT[:, gi * 2 + k2, c0:c0 + n], PCS_E, [pooled_ready, ld_s6])
                col = V_PS + gi * 2 + eb
                q_ready = P.op("act", lambda e, i=i, gi=gi, eb=eb, col=col: e.activation(out=v3(qT[:, gi * 2 + eb, :], 3), in_=blk_view(i, 342), func=AF.Identity,
                                                                                         scale=vec[:, col:col + 1]), deps=[tok, ldc_all, pooled_ready])
                set_blk_free(i, [q_ready])
        dump("dbg_q", R1[:, 16416:16416 + 8 * 1026], [q_ready])
        gate_piece(0, 2)
        for g in range(4):
            s, wt, t_w = load_piece(w_bb, 0, 8, g * 512, 512)
            for bl in range(4):
                blk = g * 4 + bl
                i, tok = fm_block([wt[:, kc, bl * 128:(bl + 1) * 128] for kc in range(8)],
                                  lambda kc, c0, n: qT[:, kc, c0:c0 + n], PCS_E, [t_w, q_ready])
                evac_copy(i, v3(yb[:, blk, :], 3), [tok, q_ready])
            ring_free[s].append(tok)
            maybe_mod()
        st9["yb_all"] = all_ad()
        gate_piece(0, 3)
        for g in range(4):
            gate_piece(1, g)
        while mod_queue and mod_queue[0] < 20:
            mod_piece(mod_queue.pop(0))
        merged_ready = st9["merged"]
        dump("dbg_mg", RC[:, 0:16 * 1026], [merged_ready])
        hT_dead = st9["last_pe"]

        sq_all = v3(bview(RD, 0, 32832, BF16), 16)
        rstd_e = SC[:, 0:1026]
        s10_done = all_ad()
        def square_to(eng, dst, src, deps):
            if eng == "act":
                return P.op("act", lambda e: e.activation(out=dst, in_=src, func=AF.Square), deps=deps)
            return P.op(eng, lambda e: e.tensor_tensor(out=dst, in0=src, in1=src, op=ALU.mult), deps=deps)

        sq_tok = [None] * KC
        for g in range(4):
            s, wt, t_w = load_piece(w_out, 0, KC, g * 512, 512)
            for bl in range(4):
                blk = g * 4 + bl
                i, tok = fm_block([wt[:, kc, bl * 128:(bl + 1) * 128] for kc in range(KC)],
                                  lambda kc, c0, n: ya[:, kc, c0:c0 + n], PCS_E, [t_w, merged_ready])
                eng = "act" if evac_rr[0] % 2 == 0 else "dve"
                t_e = evac_copy(i, v3(mixT[:, blk, :], 3), [tok, merged_ready])
                sq_tok[blk] = square_to(eng, sq_all[:, blk, :], mixT[:, blk, :], [t_e, hT_dead])
            ring_free[s].append(tok)
        merged_dead = tok

        def ffn_up_specs(k, half):
            col_base = half * FFN + k * GB * 128
            return [(w_up, 0, KC, col_base + pc0, pn) for (pc0, pn) in ((0, 512), (512, 512), (1024, 384))]
        for sp_ in ffn_up_specs(0, 0):
            prefetch(*sp_)

        def stats_matmuls(n, pieces, dep_list):
            i, bdeps = alloc_blk()
            tok = None
            for kc in range(KC):
                for pi, (c0, m) in enumerate(pieces):
                    tok = P.op("pe", lambda e, pi=pi, c0=c0, m=m, kc=kc: e.matmul(bank(3 * i + pi)[:, 0:m], lhsT=ones_bf[:], rhs=sq_all[:, kc, c0:c0 + m],
                                                                                  start=(kc == 0), stop=(kc == KC - 1)),
                               deps=[dep_list[kc]] + (bdeps if kc == 0 else []), signal=(kc == KC - 1 and pi == len(pieces) - 1))
            m = pieces[0][1]
            np_ = len(pieces)
            src = ps[:, 3 * i * 512:(3 * i + np_) * 512].rearrange("p (k n) -> p k n", k=np_)[:, :, 0:m]
            t_r = P.op("act", lambda e: e.activation(out=v3(rstd_e[:, 0:n], np_), in_=src, func=AF.Sqrt, bias=epst[:], scale=1.0 / D), deps=[tok] + s10_done)
            set_blk_free(i, [t_r])
            return P.op("dve", lambda e: e.reciprocal(out=rstd_e[:, 0:n], in_=rstd_e[:, 0:n]), deps=[t_r])

        t_rm = stats_matmuls(NE, PCS_E, sq_tok)
        t_ggm = P.op("dve", lambda e: e.tensor_tensor(out=ggm, in0=mod_fm[:, 32:48], in1=vec[:, V_GQM:V_GQM + 16], op=ALU.mult),
                     deps=[mod_tok[11], ldc_all])
        xmid_tok = [None] * KC
        sq2_tok = [None] * KC
        for blk in range(KC):
            b = blk % 4
            t_a = P.op("dve", lambda e, blk=blk: e.scalar_tensor_tensor(out=mixT[:, blk, :], in0=mixT[:, blk, :], scalar=ggm[:, blk:blk + 1], in1=rstd_e,
                                                                        op0=ALU.mult, op1=ALU.mult), deps=[t_rm, t_ggm])
            prev = [xmid_tok[blk - 4]] if blk >= 4 else []
            xmid_tok[blk] = P.dma("pool", lambda e, blk=blk: e.dma_start(out=mixT[:, blk, :], in_=xT[blk * 128:(blk + 1) * 128, 126:TM], accum_op=ALU.add),
                                  "cx%d" % b, deps=[t_a] + prev)
            sq2_tok[blk] = square_to("act", sq_all[:, blk, :], mixT[:, blk, :], [xmid_tok[blk], t_rm])
        xmid_ready = xmid_tok[KC - 1]
        spill = None
        for blk in range(KC):
            spill = P.dma("sp", lambda e, blk=blk: e.dma_start(out=outT[blk * 128:(blk + 1) * 128, :], in_=mixT[:, blk, 2:NE]), "st", deps=[xmid_tok[blk]])
        spill_all = ("st", P.cnt["st"])
        t_r2 = stats_matmuls(NE, PCS_E, sq2_tok)
        t_gsf = P.op("dve", lambda e: e.scalar_tensor_tensor(out=gsf, in0=mod_fm[:, 64:80], scalar=1.0, in1=vec[:, V_GPF:V_GPF + 16],
                                                             op0=ALU.add, op1=ALU.mult), deps=[mod_tok[19], ldc_all])
        tf = [bview(RC, i * 4104, 4104, F32) for i in range(4)]
        tf_free = [[], [], [], []]
        h2_tok = None
        for blk in range(KC):
            b = blk % 4
            eng = "dve"
            if eng == "dve":
                t_a = P.op("dve", lambda e, blk=blk, b=b: e.scalar_tensor_tensor(out=tf[b], in0=mixT[:, blk, :], scalar=gsf[:, blk:blk + 1], in1=rstd_e,
                                                                                 op0=ALU.mult, op1=ALU.mult), deps=[t_r2, t_gsf, merged_dead] + tf_free[b])
                h2_tok = P.op("act", lambda e, blk=blk, b=b: e.activation(out=h2T[:, blk, :], in_=tf[b], func=AF.Identity,
                                                                          bias=mod_fm[:, 48 + blk:49 + blk], scale=1.0),
                              deps=[t_a, mod_tok[15], t_r2])
            else:
                t_a = P.op("pool", lambda e, blk=blk, b=b: e.tensor_tensor(out=tf[b], in0=mixT[:, blk, :], in1=rstd_e, op=ALU.mult),
                           deps=[t_r2, merged_dead] + tf_free[b])
                h2_tok = P.op("act", lambda e, blk=blk, b=b: e.activation(out=h2T[:, blk, :], in_=tf[b], func=AF.Identity,
                                                                          bias=mod_fm[:, 48 + blk:49 + blk], scale=gsf[:, blk:blk + 1]),
                              deps=[t_a, mod_tok[15], t_r2, t_gsf])
            tf_free[b] = [h2_tok]
        h2_ready = P.op("dve", lambda e: e.tensor_scalar(out=h2T[:, :, 0:2], in0=h2T[:, :, 0:2], scalar1=hmask[:, 0:1], scalar2=None, op0=ALU.mult),
                        deps=[h2_tok, ldc_all])
        mid_done = all_ad()

        stg = [SC[:, i * 1026:(i + 1) * 1026] for i in range(2)]
        cb = [SC[:, 2052 + i * 1024:2052 + (i + 1) * 1024] for i in range(2)]
        sqy = [SC[:, i * 512:(i + 1) * 512].bitcast(BF16) for i in range(2)]
        stg_free = [[], []]
        cb_free = [[], []]
        sqy_free = [[], []]
        nblk = [0]
        y_ready = [None] * KC
        act_free = []
        conv_last = [None]
        ystat_tok = [None]
        for k in range(NG):
            act_ready = [None] * GB
            for half in (0, 1):
                done = 0
                for sp_ in ffn_up_specs(k, half):
                    s, wt, t_w = load_piece(*sp_)
                    pn = sp_[4]
                    for bl in range(pn // 128):
                        m = done
                        done += 1
                        mg = k * GB + m
                        i, tok = fm_block([wt[:, kc, bl * 128:(bl + 1) * 128] for kc in range(KC)],
                                          lambda kc, c0, n: h2T[:, kc, c0:c0 + n], PCS_E, [t_w, h2_ready])
                        b = nblk[0] % 2
                        nblk[0] += 1
                        cwc = V_CW + (half * 44 + mg) * 3
                        cbc = V_CB + half * 44 + mg
                        t_st = P.op("act", lambda e, i=i, b=b: e.activation(out=v3(stg[b], 3), in_=blk_view(i, 342), func=AF.Copy),
                                    deps=[tok] + stg_free[b] + mid_done)
                        set_blk_free(i, [t_st])
                        t_c0 = P.op("act", lambda e, b=b, cwc=cwc, cbc=cbc: e.activation(out=cb[b], in_=stg[b][:, 2:NE], func=AF.Identity,
                                                                                        bias=vec[:, cbc:cbc + 1], scale=vec[:, cwc + 2:cwc + 3]),
                                    deps=[t_st, ldc_all] + cb_free[b])
                        t_c1 = P.op("dve", lambda e, b=b, cwc=cwc: e.scalar_tensor_tensor(out=cb[b], in0=stg[b][:, 1:NE - 1], scalar=vec[:, cwc + 1:cwc + 2], in1=cb[b],
                                                                                          op0=ALU.mult, op1=ALU.add), deps=[t_c0])
                        t_c2 = P.op("dve", lambda e, b=b, cwc=cwc: e.scalar_tensor_tensor(out=cb[b], in0=stg[b][:, 0:NE - 2], scalar=vec[:, cwc:cwc + 1], in1=cb[b],
                                                                                          op0=ALU.mult, op1=ALU.add), deps=[t_c1])
                        stg_free[b] = [t_c2]
                        if half == 0:
                            t_o = P.op("act", lambda e, b=b, m=m: e.activation(out=act[:, m, :], in_=cb[b], func=AF.Gelu_apprx_tanh),
                                       deps=[t_c2] + act_free + mid_done)
                        else:
                            t_o = P.op("dve", lambda e, b=b, m=m: e.tensor_tensor(out=act[:, m, :], in0=act[:, m, :], in1=cb[b], op=ALU.mult),
                                       deps=[t_c2, act_ready[m]])
                        cb_free[b] = [t_o]
                        act_ready[m] = t_o
                        conv_last[0] = t_o
                    ring_free[s].append(tok)
            act_all = list(act_ready)
            last_pe = None
            last_group = (k == NG - 1)
            if last_group:
                ys_first = [True]
            for q in range(4):
                s, wt, t_w = load_piece(w_down, k * GB, GB, q * 512, 512)
                for ob in range(4):
                    o = q * 4 + ob
                    i, tok = fm_block([wt[:, kc, ob * 128:(ob + 1) * 128] for kc in range(GB)],
                                      lambda kc, c0, n: act[:, kc, c0:c0 + n], PCS_T, [t_w], kdeps=act_all)
                    src = ps[:, 3 * i * 512:(3 * i + 2) * 512]
                    if k == 0:
                        if o % 2:
                            t_y = P.op("dve", lambda e, o=o, src=src: e.tensor_copy(out=yT[:, o, :], in_=src), deps=[tok, spill_all] + mid_done)
                        else:
                            t_y = P.op("act", lambda e, o=o, src=src: e.activation(out=yT[:, o, :], in_=src, func=AF.Copy), deps=[tok, spill_all] + mid_done)
                    else:
                        t_y = P.op("dve", lambda e, o=o, src=src: e.tensor_tensor(out=yT[:, o, :], in0=src, in1=yT[:, o, :], op=ALU.add),
                                   deps=[tok, y_ready[o]])
                    set_blk_free(i, [t_y])
                    y_ready[o] = t_y
                    last_pe = tok
                    if last_group:
                        b2 = o % 2
                        t_sq = P.op("act", lambda e, o=o, b2=b2: e.activation(out=sqy[b2], in_=yT[:, o, :], func=AF.Square),
                                    deps=[t_y, conv_last[0]] + sqy_free[b2])

                        def hook(o=o, b2=b2, t_sq=t_sq):
                            tk = None
                            for pi in range(2):
                                tk = P.op("pe", lambda e, pi=pi, b2=b2, o=o: e.matmul(bank(6 + pi), lhsT=ones_bf[:], rhs=sqy[b2][:, pi * 512:(pi + 1) * 512],
                                                                                      start=(o == 0), stop=(o == KC - 1)),
                                          deps=[t_sq] + (bank_free[6] + bank_free[7] if o == 0 else []), signal=(pi == 1))
                            sqy_free[b2] = [tk]
                            ystat_tok[0] = tk
                        pe_hooks.append(hook)
                ring_free[s].append(tok)
                if mod_queue and ((q == 0 and k < 3) or (k == 0 and q == 2)):
                    mod_piece(mod_queue.pop(0))
            act_free = [last_pe]
        for h in list(pe_hooks):
            h()
        del pe_hooks[:]

        rstd_y = SC[:, 2052:2052 + 1024]
        t_r3 = P.op("act", lambda e: e.activation(out=rstd_y, in_=ps[:, 3072:4096], func=AF.Sqrt, bias=epst[:], scale=1.0 / D),
                    deps=[ystat_tok[0], conv_last[0]])
        t_r3 = P.op("dve", lambda e: e.reciprocal(out=rstd_y, in_=rstd_y), deps=[t_r3])
        t_ggf = P.op("dve", lambda e: e.tensor_tensor(out=ggf, in0=mod_fm[:, 80:96], in1=vec[:, V_GQF:V_GQF + 16], op=ALU.mult),
                     deps=[mod_tok[23], ldc_all])
        fin = None
        for blk in range(KC):
            t_a = P.op("dve", lambda e, blk=blk: e.scalar_tensor_tensor(out=yT[:, blk, :], in0=yT[:, blk, :], scalar=ggf[:, blk:blk + 1], in1=rstd_y,
                                                                        op0=ALU.mult, op1=ALU.mult), deps=[t_r3, t_ggf])
            fin = P.dma("pool", lambda e, blk=blk: e.dma_start(out=outT[blk * 128:(blk + 1) * 128, :], in_=yT[:, blk, :], accum_op=ALU.add), "fin",
                        deps=[t_a, spill_all])
        P.wait("sp", [("fin", P.cnt["fin"])])
        if DEBUG:
            P.wait("sp", [("dbg", P.cnt["dbg"])])
        P.emit(block, sems)
    return nc


_NC_CACHE = {}


def _pool_mats(first_core):
    A = np.zeros((4, 4, 128, 128), np.float32)
    for gi, win in enumerate(POOL_WINDOWS):
        for t in range(128):
            for j in range(t - win + 1, t + 1):
                if j >= 0:
                    A[2, gi, j, t] += 1.0 / win
                else:
                    A[3, gi, 128 + j, t] += 1.0 / win
            A[2, gi, t, t] -= 1.0
            cnt = min(t + 1, win)
            for j in range(max(0, t - win + 1), t + 1):
                A[0, gi, j, t] += 1.0 / cnt
            A[0, gi, t, t] -= 1.0
    if not first_core:
        A[0] = A[2]
        A[1] = A[3]
    return np.ascontiguousarray(A.transpose(2, 0, 1, 3).reshape(128, 16 * 128))


def kernel(x, c, w_ada, b_ada, g_pre_mix, g_post_mix, w_in, ln_v_g, ln_v_b, w_spatial, b_spatial,
           w_pool, pool_scale, w_branch_a, w_branch_b, w_out, g_pre_ffn, g_post_ffn, w_up, conv_w,
           conv_b, w_down):
    f = np.float32
    x = np.asarray(x, f)
    S = x.shape[1]
    assert S == NCORE * TOK

    def fm(v, nblk):
        return np.asarray(v, f).reshape(nblk, 128).T

    xs = x[0]
    xpad = np.concatenate([np.zeros((128, D), f), xs], axis=0)
    vec_fm = np.concatenate([
        fm(g_pre_mix[0], 16), fm(g_post_mix[0], 16), fm(g_pre_ffn[0], 16), fm(g_post_ffn[0], 16),
        fm(pool_scale[0], 8),
        np.asarray(conv_w[0], f).T.reshape(88, 128, 3).transpose(1, 0, 2).reshape(128, 264),
        fm(conv_b[0], 88)], axis=1)
    vec_fm = np.ascontiguousarray(vec_fm, f)
    assert vec_fm.shape == (128, NV)
    common = {
        "c_col": np.ascontiguousarray(fm(c[0], 16)),
        "w_ada": np.ascontiguousarray(w_ada[0], f),
        "b_ada_fm": np.ascontiguousarray(fm(b_ada[0], 96)),
        "vec_fm": vec_fm,
        "ln_rows": np.ascontiguousarray(np.stack([ln_v_g[0], ln_v_b[0]]), f),
        "bs_exp": np.ascontiguousarray(np.repeat(np.asarray(b_spatial[0], f), 2, axis=0).reshape(1, 2048)),
        "wsT": np.ascontiguousarray(np.asarray(w_spatial[0], f).transpose(2, 0, 1).reshape(128, 1024)),
        "tri": np.ascontiguousarray(np.triu(np.ones((128, 128), f))),
        "wpool": np.ascontiguousarray(np.asarray(w_pool[0], f).reshape(4, 2, 128, 256).transpose(2, 0, 1, 3).reshape(128, 2048)),
        "w_in": np.ascontiguousarray(w_in[0], f),
        "w_branch_a": np.ascontiguousarray(w_branch_a[0], f),
        "w_branch_b": np.ascontiguousarray(w_branch_b[0], f),
        "w_out": np.ascontiguousarray(w_out[0], f),
        "w_up": np.ascontiguousarray(w_up[0], f),
        "w_down": np.ascontiguousarray(w_down[0], f),
    }
    in_maps = []
    for core in range(NCORE):
        m = dict(common)
        m["xT"] = np.ascontiguousarray(xpad[core * TOK:core * TOK + TM].T)
        m["poolA"] = _pool_mats(core == 0)
        m["hmask"] = np.full((128, 1), 0.0 if core == 0 else 1.0, f)
        in_maps.append(m)
    if "nc" not in _NC_CACHE:
        _NC_CACHE["nc"] = build_program()
    res = run_bass_kernel_spmd(_NC_CACHE["nc"], in_maps, core_ids=list(range(NCORE)))
    if DEBUG:
        _NC_CACHE["dbg"] = [{k: np.asarray(v) for k, v in r.items() if k.startswith("dbg_")} for r in res.results]
    outs = [np.asarray(r["outT"], f) for r in res.results]
    full = np.concatenate(outs, axis=1).T
    return np.ascontiguousarray(full[None], f)
```

```python
import contextlib
import numpy as np
import concourse.bass as bass
import concourse.mybir as mybir
from concourse.bass_utils import run_bass_kernel_spmd

F32 = mybir.dt.float32
BF16 = mybir.dt.bfloat16
AF = mybir.ActivationFunctionType
ALU = mybir.AluOpType
AX = mybir.AxisListType

NCORE = 8
D = 2048
KC = 16
TOK = 1024
TM = 1152
NCH = 9
NE = 1026
FFN = 5632
NG = 4
GB = 11
EPS = 1e-6
POOL_WINDOWS = (2, 4, 8, 16)

V_GPM, V_GQM, V_GPF, V_GQF, V_PS, V_CW, V_CB = 0, 16, 32, 48, 64, 72, 72 + 264
NV = 72 + 264 + 88


class Plan:
    ENGS = ("pe", "act", "dve", "pool", "sp")

    def __init__(self):
        self.q = {e: [] for e in self.ENGS}
        self.cnt = {e: 0 for e in self.ENGS}
        self.waited = {e: {} for e in self.ENGS}
        self.semnames = list(self.ENGS)

    def new_sem(self, name):
        self.cnt[name] = 0
        self.semnames.append(name)
        return name

    def _waits(self, eng, deps):
        for d in deps:
            if d is None:
                continue
            s, v = d
            if v <= 0 or self.waited[eng].get(s, 0) >= v:
                continue
            self.waited[eng][s] = v
            self.q[eng].append(("w", s, v))

    def op(self, eng, fn, deps=(), signal=True):
        self._waits(eng, deps)
        if signal:
            self.cnt[eng] += 1
            self.q[eng].append(("o", fn, eng, 1))
            return (eng, self.cnt[eng])
        self.q[eng].append(("o", fn, None, 0))
        return None

    def dma(self, eng, fn, sem, deps=()):
        self._waits(eng, deps)
        self.cnt[sem] += 16
        self.q[eng].append(("o", fn, sem, 16))
        return (sem, self.cnt[sem])

    def wait(self, eng, deps):
        self._waits(eng, deps)

    def emit(self, block, sems):
        def run(engname, e):
            for it in self.q[engname]:
                if it[0] == "w":
                    e.wait_ge(sems[it[1]], it[2])
                else:
                    ins = it[1](e)
                    if it[2] is not None:
                        ins.then_inc(sems[it[2]], it[3])
        if self.q["pe"]:
            block.tensor(lambda e: run("pe", e))
        if self.q["act"]:
            block.scalar(lambda e: run("act", e))
        if self.q["dve"]:
            block.vector(lambda e: run("dve", e))
        if self.q["pool"]:
            block.gpsimd(lambda e: run("pool", e))
        if self.q["sp"]:
            block.sync(lambda e: run("sp", e))


DEBUG = False


def build_program():
    nc = bass.Bass("TRN2", target_bir_lowering=False)
    dbg = {}
    if DEBUG:
        for nm, n in (("dbg_h", 16 * 1152), ("dbg_vn", 9 * 2048), ("dbg_sg", 16 * 1026), ("dbg_mg", 16 * 1026), ("dbg_q", 8 * 1026)):
            dbg[nm] = nc.dram_tensor(nm, [128, n], BF16, kind="ExternalOutput").ap()

    def din(name, shape, dt=F32):
        return nc.dram_tensor(name, list(shape), dt, kind="ExternalInput").ap()

    xT = din("xT", [D, TM])
    c_col = din("c_col", [128, KC])
    w_ada = din("w_ada", [D, 6 * D])
    b_ada_fm = din("b_ada_fm", [128, 96])
    vec_fm_d = din("vec_fm", [128, NV])
    ln_rows = din("ln_rows", [2, D])
    bs_exp_d = din("bs_exp", [1, 2048])
    wsT_d = din("wsT", [128, 8 * 128])
    tri_d = din("tri", [128, 128])
    poolA_d = din("poolA", [128, 16 * 128])
    wpool_d = din("wpool", [128, 4 * 2 * 256])
    hmask_d = din("hmask", [128, 1])
    w_in = din("w_in", [D, 9216])
    w_ba = din("w_branch_a", [D, D])
    w_bb = din("w_branch_b", [1024, D])
    w_out = din("w_out", [D, D])
    w_up = din("w_up", [D, 2 * FFN])
    w_down = din("w_down", [FFN, D])
    outT = nc.dram_tensor("outT", [D, TOK], F32, kind="ExternalOutput").ap()

    P = Plan()
    P.new_sem("dbg")

    def dump(nm, src2d, deps):
        if DEBUG:
            P.dma("sp", lambda e: e.dma_start(out=dbg[nm][:, :], in_=src2d), "dbg", deps=deps)
    for s in ("ring0", "ring1", "ring2", "ld", "ldp", "ld2s", "ld2h", "ldc", "xin", "st", "fin", "cx0", "cx1", "cx2", "cx3", "fx0", "fx1", "fx2"):
        P.new_sem(s)

    with contextlib.ExitStack() as es:
        def sb(name, shape, dt):
            return es.enter_context(nc.sbuf_tensor(name, list(shape), dt))

        R1 = sb("R1", [128, 34848], BF16)
        RC = sb("RC", [128, 16416], BF16)
        RD = sb("RD", [128, 18432], BF16)
        RING = sb("RING", [128, 3 * 8192], BF16)
        SC = sb("SC", [128, 4104], F32)
        ps = es.enter_context(nc.psum_tensor("ps", [128, 4096], F32))

        def bview(reg, off_b, nbytes, dt):
            a = reg[:, off_b // 2:(off_b + nbytes) // 2]
            return a if dt == BF16 else a.bitcast(F32)

        def v3(ap2, k):
            return ap2.rearrange("p (k n) -> p k n", k=k)

        gu = v3(bview(R1, 0, 32832, BF16), 16)
        gv = v3(bview(R1, 32832, 36864, BF16), 9)
        yb = gu
        poolA = v3(bview(R1, 0, 4096, BF16), 16)
        wpool = bview(R1, 4096, 4096, BF16).rearrange("p (g k e) -> p g k e", g=4, k=2)
        p_tm = v3(bview(R1, 32832, 18432, BF16), 9)
        qT = v3(bview(R1, 32832, 16416, BF16), 8)
        pooledT = v3(bview(R1, 51264, 16416, BF16), 8)
        mixT = v3(bview(R1, 0, 65664, F32), 16)
        yT = v3(bview(R1, 0, 65536, F32), 16)
        ya = v3(bview(RC, 0, 32832, BF16), 16)
        ln_g_bc = bview(RC, 0, 8192, F32)
        ln_b_bc = bview(RC, 8192, 8192, F32)
        WmT = v3(bview(RC, 16384, 2048, BF16), 8)
        bs2 = bview(RC, 18432, 4096, BF16)
        wsT_f = v3(bview(RC, 22528, 4096, F32), 8)
        tri = bview(RC, 26624, 512, F32)
        act = v3(bview(RC, 0, 22528, BF16), 11)
        cxin = [bview(RC, i * 4104, 4104, F32) for i in range(2)]
        hT = v3(bview(RD, 0, 36864, BF16), 16)
        h2T = v3(bview(RD, 0, 32832, BF16), 16)

        ring = [v3(RING[:, i * 8192:(i + 1) * 8192], 16) for i in range(3)]
        ring_flat = [RING[:, i * 8192:(i + 1) * 8192] for i in range(3)]

        mod_fm = sb("mod_fm", [128, 96], F32)
        bada = sb("bada", [128, 96], F32)
        vec = sb("vec", [128, NV], F32)
        derived = sb("derived", [128, 64], F32)
        gs_m, ggm, gsf, ggf = (derived[:, 0:16], derived[:, 16:32], derived[:, 32:48], derived[:, 48:64])
        ccol = sb("ccol", [128, KC], F32)
        scb = sb("scb", [128, KC], BF16)
        epst = sb("epst", [128, 1], F32)
        hmask = sb("hmask_sb", [128, 1], F32)
        ones_bf = sb("ones_bf", [128, 128], BF16)
        onesrow = sb("onesrow", [1, 128], F32)
        ones2 = sb("ones2", [2, 128], BF16)
        modrow = sb("modrow", [1, 512], F32)
        vsum = sb("vsum", [128, 36], F32)
        vstat = sb("vstat", [128, 5 * 9], F32)
        vsq, vmean, vmsq, vvar, vrstd = (vstat[:, i * 9:(i + 1) * 9] for i in range(5))

        def scv(off, n, dt=F32):
            a = SC[:, off:off + n]
            return a if dt == F32 else a.bitcast(BF16)

        sems = {n: es.enter_context(nc.semaphore(n)) for n in P.semnames}
        block = es.enter_context(nc.Block())

        ring_free = [[], [], []]
        ring_next = [0]
        prefetched = []

        def _load(spec):
            w_ap, r0, nk, c0, ncols = spec
            s = ring_next[0] % 3
            ring_next[0] += 1
            dst = ring_flat[s][:, 0:nk * ncols].rearrange("p (k n) -> p k n", k=nk)
            src = w_ap[r0 * 128:(r0 + nk) * 128, c0:c0 + ncols].rearrange("(k p) n -> p k n", p=128)
            tok = P.dma("pool", lambda e: e.dma_start(out=dst, in_=src), "ring%d" % s, deps=ring_free[s])
            ring_free[s] = []
            return s, dst, tok

        def prefetch(w_ap, r0, nk, c0, ncols):
            spec = (id(w_ap.tensor) if hasattr(w_ap, "tensor") else id(w_ap), r0, nk, c0, ncols)
            prefetched.append((spec, _load((w_ap, r0, nk, c0, ncols))))

        def load_piece(w_ap, r0, nk, c0, ncols):
            spec = (id(w_ap.tensor) if hasattr(w_ap, "tensor") else id(w_ap), r0, nk, c0, ncols)
            if prefetched and prefetched[0][0] == spec:
                return prefetched.pop(0)[1]
            return _load((w_ap, r0, nk, c0, ncols))

        bank_free = [[] for _ in range(8)]
        blk_next = [0]
        tm_next = [0]

        def bank(b):
            return ps[:, b * 512:(b + 1) * 512]

        def alloc_blk():
            i = blk_next[0] % 2
            blk_next[0] += 1
            deps = bank_free[3 * i] + bank_free[3 * i + 1] + bank_free[3 * i + 2]
            return i, deps

        def blk_view(i, n):
            return ps[:, 3 * i * 512:(3 * i + 3) * 512].rearrange("p (k n) -> p k n", k=3)[:, :, 0:n]

        def set_blk_free(i, toks):
            for b in range(3):
                bank_free[3 * i + b] = list(toks)

        def alloc_tm():
            b = tm_next[0] % 6
            tm_next[0] += 1
            return b, bank_free[b]

        pe_hooks = []

        def fm_block(lhsTs, rhs_fn, pieces, deps, kdeps=None):
            i, bdeps = alloc_blk()
            nk = len(lhsTs)
            tok = None
            first = True
            for kc in range(nk):
                for pi, (c0, n) in enumerate(pieces):
                    out_ap = bank(3 * i + pi)[:, 0:n]
                    last = (kc == nk - 1) and (pi == len(pieces) - 1)
                    tok = P.op("pe", lambda e, o=out_ap, l=lhsTs[kc], r=rhs_fn(kc, c0, n), st=(kc == 0), sp=(kc == nk - 1):
                               e.matmul(o, lhsT=l, rhs=r, start=st, stop=sp),
                               deps=((list(deps) + bdeps) if first else []) + ([kdeps[kc]] if (kdeps and pi == 0) else []), signal=last)
                    first = False
            hooks = list(pe_hooks)
            del pe_hooks[:]
            for h in hooks:
                h()
            return i, tok

        PCS_H = [(126, 342), (468, 342), (810, 342)]
        PCS_E = [(0, 342), (342, 342), (684, 342)]
        PCS_T = [(0, 512), (512, 512)]

        def all_ad():
            return [("act", P.cnt["act"]), ("dve", P.cnt["dve"]), ("pool", P.cnt["pool"])]

        t_c = P.dma("sp", lambda e: e.dma_start(out=ccol[:], in_=c_col[:, :]), "ldc")
        P.dma("sp", lambda e: e.dma_start(out=bada[:], in_=b_ada_fm[:, :]), "ldc")
        P.dma("sp", lambda e: e.dma_start(out=vec[:], in_=vec_fm_d[:, :]), "ldc")
        P.dma("sp", lambda e: e.dma_start(out=hmask[:], in_=hmask_d[:, :]), "ldc")
        ldc_all = ("ldc", P.cnt["ldc"])
        t_ms = P.op("dve", lambda e: e.memset(epst[:], EPS))
        P.op("dve", lambda e: e.memset(ones_bf[:], 1.0))
        P.op("dve", lambda e: e.memset(ones2[:], 1.0))
        t_ones = P.op("dve", lambda e: e.memset(onesrow[:], 1.0))
        P.op("dve", lambda e: e.memset(vsum[:], 0.0))
        t_vz = P.op("dve", lambda e: e.memset(vstat[:], 0.0))
        t_init = t_vz
        t_sc = P.op("act", lambda e: e.activation(out=scb[:], in_=ccol[:], func=AF.Silu), deps=[ldc_all, t_init])

        xres = [bview(R1, kc * 4608, 4608, F32) for kc in range(15)] + [bview(RC, 0, 4608, F32)]
        rstd0 = bview(RC, 4608, 4608, F32)
        sqb = [SC[:, i * 576:(i + 1) * 576].bitcast(BF16) for i in range(2)]
        tfx = [SC[:, 1152 + i * 1152:1152 + (i + 1) * 1152] for i in range(2)]
        for kc in range(KC):
            P.dma("sp", lambda e, kc=kc: e.dma_start(out=xres[kc], in_=xT[kc * 128:(kc + 1) * 128, :]), "xin")
        t_xl = [("xin", P.cnt["xin"])] * KC
        S0_PCS = [(0, 384), (384, 384), (768, 384)]
        st_i, st_deps = alloc_blk()
        stat_tok = None
        sq_free = [[], []]
        for kc in range(KC):
            b = kc % 2
            t_sq = P.op("act", lambda e, b=b, kc=kc: e.activation(out=sqb[b], in_=xres[kc], func=AF.Square),
                        deps=[t_xl[kc]] + sq_free[b])
            for pi, (c0, n) in enumerate(S0_PCS):
                stat_tok = P.op("pe", lambda e, b=b, pi=pi, c0=c0, n=n, kc=kc:
                                e.matmul(bank(3 * st_i + pi)[:, 0:n], lhsT=ones_bf[:], rhs=sqb[b][:, c0:c0 + n],
                                         start=(kc == 0), stop=(kc == KC - 1)),
                                deps=[t_sq, t_init] + (st_deps if kc == 0 else []), signal=(pi == 2))
            sq_free[b] = [stat_tok]
        t_r0 = P.op("act", lambda e: e.activation(out=v3(rstd0, 3), in_=blk_view(st_i, 384), func=AF.Sqrt,
                                                  bias=epst[:], scale=1.0 / D), deps=[stat_tok, t_ms])
        set_blk_free(st_i, [t_r0])
        t_r0 = P.op("dve", lambda e: e.reciprocal(out=rstd0, in_=rstd0), deps=[t_r0])

        mod_tok = [None] * 24
        modrow_free = [[]]
        pm_free = [[]]

        def mod_piece(j):
            s, wt, t_w = load_piece(w_ada, 0, KC, j * 512, 512)
            tok = None
            for kc in range(KC):
                tok = P.op("pe", lambda e, kc=kc, wt=wt: e.matmul(ps[0:1, 3584:4096], lhsT=scb[:, kc:kc + 1], rhs=wt[:, kc, :],
                                                                  start=(kc == 0), stop=(kc == KC - 1)),
                           deps=[t_w, t_sc] + pm_free[0] + bank_free[7] if kc == 0 else (), signal=(kc == KC - 1))
            ring_free[s].append(tok)
            mr = modrow[0:1, 0:512]
            t_cp = P.op("dve", lambda e, mr=mr: e.tensor_copy(out=mr, in_=ps[0:1, 3584:4096]), deps=[tok] + modrow_free[0])
            pm_free[0] = [t_cp]
            bank_free[7] = [t_cp]
            t4 = None
            for q in range(4):
                t4 = P.op("pe", lambda e, q=q, mr=mr: e.matmul(ps[:, 3072 + q:3072 + q + 1], lhsT=mr[0:1, q * 128:(q + 1) * 128],
                                                               rhs=onesrow[0:1, 0:1], start=True, stop=True),
                          deps=[t_cp, t_ones] + bank_free[6] if q == 0 else (), signal=(q == 3))
            modrow_free[0] = [t4]
            t_m = P.op("dve", lambda e, j=j: e.tensor_tensor(out=mod_fm[:, 4 * j:4 * j + 4], in0=ps[:, 3072:3076],
                                                            in1=bada[:, 4 * j:4 * j + 4], op=ALU.add), deps=[t4, ldc_all])
            bank_free[6] = [t_m]
            mod_tok[j] = t_m

        for j in range(8):
            mod_piece(j)
        t_gsm = P.op("dve", lambda e: e.scalar_tensor_tensor(out=gs_m, in0=mod_fm[:, 16:32], scalar=1.0, in1=vec[:, V_GPM:V_GPM + 16],
                                                             op0=ALU.add, op1=ALU.mult), deps=[mod_tok[7], ldc_all])
        hT_ready = None
        tfx_free = [[], []]
        for kc in range(KC):
            b = kc % 2
            t_a = P.op("dve", lambda e, kc=kc, b=b: e.scalar_tensor_tensor(out=tfx[b], in0=xres[kc], scalar=gs_m[:, kc:kc + 1], in1=rstd0,
                                                                           op0=ALU.mult, op1=ALU.mult), deps=[t_xl[kc], t_gsm, t_r0] + tfx_free[b])
            hT_ready = P.op("act", lambda e, kc=kc, b=b: e.activation(out=hT[:, kc, :], in_=tfx[b], func=AF.Identity,
                                                                      bias=mod_fm[:, kc:kc + 1], scale=1.0), deps=[t_a, mod_tok[3]])
            tfx_free[b] = [hT_ready]
        s0_done = all_ad()
        dump("dbg_h", RD[:, 0:16 * 1152], [hT_ready])

        P.dma("sp", lambda e: e.dma_start(out=ln_g_bc, in_=ln_rows[0:1, :].broadcast_to([128, D])), "ld", deps=s0_done)
        P.dma("sp", lambda e: e.dma_start(out=ln_b_bc, in_=ln_rows[1:2, :].broadcast_to([128, D])), "ld")
        P.dma("sp", lambda e: e.dma_start(out=wsT_f.rearrange("p k n -> p (k n)"), in_=wsT_d[:, :]), "ld")
        P.dma("sp", lambda e: e.dma_start(out=tri, in_=tri_d[:, :]), "ld")
        bsf = SC[0:1, 0:2048]
        bslo = SC[0:1, 2048:3072].bitcast(BF16)
        P.dma("sp", lambda e: e.dma_start(out=bsf, in_=bs_exp_d[:, :]), "ld")
        ld_s3 = ("ld", P.cnt["ld"])
        t_wm = None
        for h in range(8):
            t_wm = P.op("dve", lambda e, h=h: e.tensor_tensor(out=WmT[:, h, :], in0=wsT_f[:, h, :], in1=tri, op=ALU.mult), deps=[ld_s3])

        mod_queue = list(range(8, 24))
        mod_ctr = [0]

        def maybe_mod(every=2, limit=20):
            mod_ctr[0] += 1
            if mod_queue and mod_queue[0] < limit and mod_ctr[0] % every == 0:
                mod_piece(mod_queue.pop(0))

        gv_ready = None
        for g in range(4):
            s, wt, t_w = load_piece(w_in, 0, KC, 2048 + g * 512, 512)
            if g == 0:
                t_bh = P.dma("pool", lambda e: e.dma_start(out=bs2[0:1, :], in_=bs_exp_d[:, :]), "ld2s", deps=s0_done)
            for c in range(NCH):
                b, bdeps = alloc_tm()
                tok = None
                for kc in range(KC):
                    tok = P.op("pe", lambda e, b=b, kc=kc, c=c, wt=wt: e.matmul(bank(b), lhsT=hT[:, kc, c * 128:(c + 1) * 128], rhs=wt[:, kc, :],
                                                                                start=(kc == 0), stop=(kc == KC - 1)),
                               deps=[t_w, hT_ready] + bdeps if kc == 0 else (), signal=(kc == KC - 1))
                gv_ready = P.op("act", lambda e, b=b, c=c, g=g: e.activation(out=gv[:, c, g * 512:(g + 1) * 512], in_=bank(b), func=AF.Gelu_apprx_tanh,
                                                                             accum_out=vsum[:, c * 4 + g:c * 4 + g + 1]), deps=[tok, t_vz] + s0_done)
                bank_free[b] = [gv_ready]
            ring_free[s].append(tok)
            maybe_mod()
        t_lo = P.op("dve", lambda e: e.tensor_tensor(out=bslo, in0=bsf, in1=bs2[0:1, :], op=ALU.subtract), deps=[ld_s3, t_bh])
        t_bl = P.dma("sp", lambda e: e.dma_start(out=bs2[1:2, :], in_=bslo), "ld2h", deps=[t_lo])

        junk = SC[:, 1024:2048].bitcast(BF16)
        t_q = None
        for c in range(NCH):
            t_q = P.op("act", lambda e, c=c: e.activation(out=junk, in_=gv[:, c, :], func=AF.Square, accum_out=vsq[:, c:c + 1]),
                       deps=[gv_ready, hT_ready, t_lo, t_bl, t_q])
        t = P.op("dve", lambda e: e.tensor_reduce(out=vmean, in_=vsum[:].rearrange("p (c g) -> p c g", g=4), axis=AX.X, op=ALU.add), deps=[gv_ready])
        t = P.op("dve", lambda e: e.tensor_scalar(out=vmean, in0=vmean, scalar1=1.0 / D, scalar2=None, op0=ALU.mult), deps=[t])
        t = P.op("dve", lambda e: e.tensor_tensor(out=vmsq, in0=vmean, in1=vmean, op=ALU.mult), deps=[t])
        t = P.op("dve", lambda e: e.scalar_tensor_tensor(out=vvar, in0=vsq, scalar=1.0 / D, in1=vmsq, op0=ALU.mult, op1=ALU.subtract), deps=[t, t_q])
        t = P.op("act", lambda e: e.activation(out=vvar, in_=vvar, func=AF.Sqrt, bias=epst[:], scale=1.0), deps=[t])
        t_vr = P.op("dve", lambda e: e.reciprocal(out=vrstd, in_=vvar), deps=[t])
        vn_ready = [None] * NCH
        for c in range(NCH):
            t1 = P.op("dve", lambda e, c=c: e.tensor_scalar(out=gv[:, c, :], in0=gv[:, c, :], scalar1=vmean[:, c:c + 1], scalar2=vrstd[:, c:c + 1],
                                                            op0=ALU.subtract, op1=ALU.mult), deps=[t_vr, t_q])
            t2 = P.op("dve", lambda e, c=c: e.tensor_tensor(out=gv[:, c, :], in0=gv[:, c, :], in1=ln_g_bc, op=ALU.mult), deps=[t1, ld_s3])
            vn_ready[c] = P.op("dve", lambda e, c=c: e.tensor_tensor(out=gv[:, c, :], in0=gv[:, c, :], in1=ln_b_bc, op=ALU.add), deps=[t2])

        dump("dbg_vn", R1[:, 16416:16416 + 9 * 2048], [vn_ready[8]])
        evac_rr = [0]
        gu_ready = None
        for g in range(4):
            s, wt, t_w = load_piece(w_in, 0, KC, g * 512, 512)
            for bl in range(4):
                blk = g * 4 + bl
                i, tok = fm_block([wt[:, kc, bl * 128:(bl + 1) * 128] for kc in range(KC)],
                                  lambda kc, c0, n: hT[:, kc, c0:c0 + n], PCS_H, [t_w, hT_ready])
                gu_ready = P.op("act", lambda e, i=i, blk=blk: e.activation(out=v3(gu[:, blk, :], 3), in_=blk_view(i, 342), func=AF.Gelu_apprx_tanh),
                                deps=[tok] + s0_done)
                set_blk_free(i, [gu_ready])
            ring_free[s].append(tok)
            maybe_mod()

        sg_ready = None
        for c in range(NCH):
            for q in range(4):
                b, bdeps = alloc_tm()
                P.op("pe", lambda e, b=b, q=q: e.matmul(bank(b), lhsT=ones2[0:2, :], rhs=bs2[0:2, q * 512:(q + 1) * 512], start=True, stop=False),
                     deps=[t_bh, t_bl, vn_ready[c], t_wm] + bdeps, signal=False)
                tok = None
                for f in range(4):
                    fb = q * 4 + f
                    h = fb // 2
                    tok = P.op("pe", lambda e, b=b, f=f, fb=fb, c=c, h=h: e.matmul(bank(b)[:, f * 128:(f + 1) * 128], lhsT=gv[:, c, fb * 128:(fb + 1) * 128],
                                                                                  rhs=WmT[:, h, :], start=False, stop=(f == 3)), signal=(f == 3))
                pv = bank(b).rearrange("p (k n) -> p k n", k=4)
                if c == 0:
                    o = gu[:, q * 4:q * 4 + 4, 0:2]
                    i0 = pv[:, :, 126:128]
                else:
                    o = gu[:, q * 4:q * 4 + 4, 2 + (c - 1) * 128:2 + c * 128]
                    i0 = pv
                sg_ready = P.op("dve", lambda e, o=o, i0=i0: e.tensor_tensor(out=o, in0=i0, in1=o, op=ALU.mult), deps=[tok, gu_ready])
                bank_free[b] = [sg_ready]

        dump("dbg_sg", R1[:, 0:16 * 1026], [sg_ready])
        def evac_copy(i, out3, deps):
            k = evac_rr[0] % 2
            evac_rr[0] += 1
            if k == 0:
                t = P.op("act", lambda e: e.activation(out=out3, in_=blk_view(i, 342), func=AF.Copy), deps=deps)
            else:
                t = P.op("dve", lambda e: e.tensor_copy(out=out3, in_=blk_view(i, 342)), deps=deps)
            set_blk_free(i, [t])
            return t

        ya_ready = None
        for g in range(4):
            s, wt, t_w = load_piece(w_ba, 0, KC, g * 512, 512)
            for bl in range(4):
                blk = g * 4 + bl
                i, tok = fm_block([wt[:, kc, bl * 128:(bl + 1) * 128] for kc in range(KC)],
                                  lambda kc, c0, n: gu[:, kc, c0:c0 + n], PCS_E, [t_w, sg_ready])
                ya_ready = evac_copy(i, v3(ya[:, blk, :], 3), [tok, sg_ready])
            ring_free[s].append(tok)
            maybe_mod()
        s4_last_pe = tok
        ya_all = all_ad()

        sgt = [SC[:, i * 1026:(i + 1) * 1026] for i in range(2)]
        tmpm = [SC[:, 2052 + i * 1026:2052 + (i + 1) * 1026] for i in range(2)]
        sgt_free = [[], []]
        tmp_free = [[], []]
        n9 = [0]
        st9 = {"merged": None, "last_pe": None, "yb_all": []}

        def gate_piece(stage, g):
            s, wt, t_w = load_piece(w_in, 0, KC, 5120 + stage * 2048 + g * 512, 512)
            for bl in range(4):
                blk = g * 4 + bl
                i, tok = fm_block([wt[:, kc, bl * 128:(bl + 1) * 128] for kc in range(KC)],
                                  lambda kc, c0, n: hT[:, kc, c0:c0 + n], PCS_H, [t_w])
                k = n9[0] % 2
                n9[0] += 1
                t_s = P.op("act", lambda e, i=i, k=k: e.activation(out=v3(sgt[k], 3), in_=blk_view(i, 342), func=AF.Sigmoid), deps=[tok, t_q] + sgt_free[k])
                set_blk_free(i, [t_s])
                if stage == 0:
                    st9["merged"] = P.op("dve", lambda e, blk=blk, k=k: e.tensor_tensor(out=ya[:, blk, :], in0=ya[:, blk, :], in1=sgt[k], op=ALU.mult),
                                         deps=[t_s] + ya_all)
                    sgt_free[k] = [st9["merged"]]
                else:
                    t_m = P.op("dve", lambda e, blk=blk, k=k: e.tensor_tensor(out=tmpm[k], in0=yb[:, blk, :], in1=sgt[k], op=ALU.mult),
                               deps=[t_s] + st9["yb_all"] + tmp_free[k])
                    sgt_free[k] = [t_m]
                    st9["merged"] = P.op("dve", lambda e, blk=blk, k=k: e.tensor_tensor(out=ya[:, blk, :], in0=ya[:, blk, :], in1=tmpm[k], op=ALU.add),
                                         deps=[t_m])
                    tmp_free[k] = [st9["merged"]]
                st9["last_pe"] = tok
            ring_free[s].append(tok)
            maybe_mod()

        for g in range(2):
            s, wt, t_w = load_piece(w_in, 0, KC, 4096 + g * 512, 512)
            for c in range(NCH):
                b, bdeps = alloc_tm()
                tok = None
                for kc in range(KC):
                    tok = P.op("pe", lambda e, b=b, kc=kc, c=c, wt=wt: e.matmul(bank(b), lhsT=hT[:, kc, c * 128:(c + 1) * 128], rhs=wt[:, kc, :],
                                                                                start=(kc == 0), stop=(kc == KC - 1)),
                               deps=[t_w, sg_ready] + bdeps if kc == 0 else (), signal=(kc == KC - 1))
                k = evac_rr[0] % 2
                evac_rr[0] += 1
                if k == 0:
                    t_p = P.op("act", lambda e, b=b, c=c, g=g: e.activation(out=p_tm[:, c, g * 512:(g + 1) * 512], in_=bank(b), func=AF.Copy), deps=[tok, sg_ready])
                else:
                    t_p = P.op("dve", lambda e, b=b, c=c, g=g: e.tensor_copy(out=p_tm[:, c, g * 512:(g + 1) * 512], in_=bank(b)), deps=[tok, sg_ready])
                bank_free[b] = [t_p]
            ring_free[s].append(tok)
            maybe_mod()
        p_all = all_ad()
        P.dma("pool", lambda e: e.dma_start(out=poolA.rearrange("p k n -> p (k n)"), in_=poolA_d[:, :]), "ldp", deps=[s4_last_pe])
        P.dma("pool", lambda e: e.dma_start(out=wpool.rearrange("p g k e -> p (g k e)"), in_=wpool_d[:, :]), "ldp", deps=[s4_last_pe])
        ld_s6 = ("ldp", P.cnt["ldp"])
        gate_piece(0, 0)
        pooled_ready = None
        for c in range(NCH):
            kind = 0 if c == 1 else 2
            for q in range(2):
                b, bdeps = alloc_tm()
                tok = None
                for f in range(4):
                    fb = q * 4 + f
                    gi = fb // 2
                    two = c >= 1
                    tok = P.op("pe", lambda e, b=b, f=f, fb=fb, c=c, gi=gi, kind=kind, two=two:
                               e.matmul(bank(b)[:, f * 128:(f + 1) * 128], lhsT=p_tm[:, c, fb * 128:(fb + 1) * 128],
                                        rhs=poolA[:, kind * 4 + gi, :], start=True, stop=(not two)),
                               deps=p_all + [ld_s6] + bdeps if f == 0 else (), signal=(f == 3 and not two))
                    if two:
                        tok = P.op("pe", lambda e, b=b, f=f, fb=fb, c=c, gi=gi, kind=kind:
                                   e.matmul(bank(b)[:, f * 128:(f + 1) * 128], lhsT=p_tm[:, c - 1, fb * 128:(fb + 1) * 128],
                                            rhs=poolA[:, (kind + 1) * 4 + gi, :], start=False, stop=True), signal=(f == 3))
                pv = bank(b).rearrange("p (k n) -> p k n", k=4)
                if c == 0:
                    o = pooledT[:, q * 4:q * 4 + 4, 0:2]
                    i0 = pv[:, :, 126:128]
                else:
                    o = pooledT[:, q * 4:q * 4 + 4, 2 + (c - 1) * 128:2 + c * 128]
                    i0 = pv
                pooled_ready = P.op("act", lambda e, o=o, i0=i0: e.activation(out=o, in_=i0, func=AF.Copy), deps=[tok])
                bank_free[b] = [pooled_ready]
        gate_piece(0, 1)
        q_ready = None
        for gi in range(4):
            for eb in range(2):
                i, tok = fm_block([wpool[:, gi, k2, eb * 128:(eb + 1) * 128] for k2 in range(2)],
                                  lambda k2, c0, n, gi=gi: pooledT[:, gi * 2 + k2, c0:c0 + n], PCS_E, [pooled_ready, ld_s6])
                col = V_PS + gi * 2 + eb
                q_ready = P.op("act", lambda e, i=i, gi=gi, eb=eb, col=col: e.activation(out=v3(qT[:, gi * 2 + eb, :], 3), in_=blk_view(i, 342), func=AF.Identity,
                                                                                         scale=vec[:, col:col + 1]), deps=[tok, ldc_all, pooled_ready])
                set_blk_free(i, [q_ready])
        dump("dbg_q", R1[:, 16416:16416 + 8 * 1026], [q_ready])
        gate_piece(0, 2)
        for g in range(4):
            s, wt, t_w = load_piece(w_bb, 0, 8, g * 512, 512)
            for bl in range(4):
                blk = g * 4 + bl
                i, tok = fm_block([wt[:, kc, bl * 128:(bl + 1) * 128] for kc in range(8)],
                                  lambda kc, c0, n: qT[:, kc, c0:c0 + n], PCS_E, [t_w, q_ready])
                evac_copy(i, v3(yb[:, blk, :], 3), [tok, q_ready])
            ring_free[s].append(tok)
            maybe_mod()
        st9["yb_all"] = all_ad()
        gate_piece(0, 3)
        for g in range(4):
            gate_piece(1, g)
        while mod_queue and mod_queue[0] < 20:
            mod_piece(mod_queue.pop(0))
        merged_ready = st9["merged"]
        dump("dbg_mg", RC[:, 0:16 * 1026], [merged_ready])
        hT_dead = st9["last_pe"]

        sq_all = v3(bview(RD, 0, 32832, BF16), 16)
        rstd_e = SC[:, 0:1026]
        s10_done = all_ad()
        def square_to(eng, dst, src, deps):
            if eng == "act":
                return P.op("act", lambda e: e.activation(out=dst, in_=src, func=AF.Square), deps=deps)
            return P.op(eng, lambda e: e.tensor_tensor(out=dst, in0=src, in1=src, op=ALU.mult), deps=deps)

        sq_tok = [None] * KC
        for g in range(4):
            s, wt, t_w = load_piece(w_out, 0, KC, g * 512, 512)
            for bl in range(4):
                blk = g * 4 + bl
                i, tok = fm_block([wt[:, kc, bl * 128:(bl + 1) * 128] for kc in range(KC)],
                                  lambda kc, c0, n: ya[:, kc, c0:c0 + n], PCS_E, [t_w, merged_ready])
                eng = "act" if evac_rr[0] % 2 == 0 else "dve"
                t_e = evac_copy(i, v3(mixT[:, blk, :], 3), [tok, merged_ready])
                sq_tok[blk] = square_to(eng, sq_all[:, blk, :], mixT[:, blk, :], [t_e, hT_dead])
            ring_free[s].append(tok)
        merged_dead = tok

        def ffn_up_specs(k, half):
            col_base = half * FFN + k * GB * 128
            return [(w_up, 0, KC, col_base + pc0, pn) for (pc0, pn) in ((0, 512), (512, 512), (1024, 384))]

        def stats_matmuls(n, pieces, dep_list):
            i, bdeps = alloc_blk()
            tok = None
            for kc in range(KC):
                for pi, (c0, m) in enumerate(pieces):
                    tok = P.op("pe", lambda e, pi=pi, c0=c0, m=m, kc=kc: e.matmul(bank(3 * i + pi)[:, 0:m], lhsT=ones_bf[:], rhs=sq_all[:, kc, c0:c0 + m],
                                                                                  start=(kc == 0), stop=(kc == KC - 1)),
                               deps=[dep_list[kc]] + (bdeps if kc == 0 else []), signal=(kc == KC - 1 and pi == len(pieces) - 1))
            m = pieces[0][1]
            np_ = len(pieces)
            src = ps[:, 3 * i * 512:(3 * i + np_) * 512].rearrange("p (k n) -> p k n", k=np_)[:, :, 0:m]
            t_r = P.op("act", lambda e: e.activation(out=v3(rstd_e[:, 0:n], np_), in_=src, func=AF.Sqrt, bias=epst[:], scale=1.0 / D), deps=[tok] + s10_done)
            set_blk_free(i, [t_r])
            return P.op("dve", lambda e: e.reciprocal(out=rstd_e[:, 0:n], in_=rstd_e[:, 0:n]), deps=[t_r])

        t_rm = stats_matmuls(NE, PCS_E, sq_tok)
        t_ggm = P.op("dve", lambda e: e.tensor_tensor(out=ggm, in0=mod_fm[:, 32:48], in1=vec[:, V_GQM:V_GQM + 16], op=ALU.mult),
                     deps=[mod_tok[11], ldc_all])
        xmid_tok = [None] * KC
        sq2_tok = [None] * KC
        for blk in range(KC):
            b = blk % 4
            t_a = P.op("dve", lambda e, blk=blk: e.scalar_tensor_tensor(out=mixT[:, blk, :], in0=mixT[:, blk, :], scalar=ggm[:, blk:blk + 1], in1=rstd_e,
                                                                        op0=ALU.mult, op1=ALU.mult), deps=[t_rm, t_ggm])
            prev = [xmid_tok[blk - 4]] if blk >= 4 else []
            xmid_tok[blk] = P.dma("pool", lambda e, blk=blk: e.dma_start(out=mixT[:, blk, :], in_=xT[blk * 128:(blk + 1) * 128, 126:TM], accum_op=ALU.add),
                                  "cx%d" % b, deps=[t_a] + prev)
            sq2_tok[blk] = square_to("act", sq_all[:, blk, :], mixT[:, blk, :], [xmid_tok[blk], t_rm])
        xmid_ready = xmid_tok[KC - 1]
        for sp_ in ffn_up_specs(0, 0):
            prefetch(*sp_)
        t_r2 = stats_matmuls(NE, PCS_E, sq2_tok)
        spill = None
        for blk in range(KC):
            spill = P.dma("sp", lambda e, blk=blk: e.dma_start(out=outT[blk * 128:(blk + 1) * 128, :], in_=mixT[:, blk, 2:NE]), "st", deps=[xmid_tok[blk], t_r2])
        spill_all = ("st", P.cnt["st"])
        t_gsf = P.op("dve", lambda e: e.scalar_tensor_tensor(out=gsf, in0=mod_fm[:, 64:80], scalar=1.0, in1=vec[:, V_GPF:V_GPF + 16],
                                                             op0=ALU.add, op1=ALU.mult), deps=[mod_tok[19], ldc_all])
        tf = [bview(RC, i * 4104, 4104, F32) for i in range(4)]
        tf_free = [[], [], [], []]
        h2_tok = None
        for blk in range(KC):
            b = blk % 4
            eng = "dve"
            if eng == "dve":
                t_a = P.op("dve", lambda e, blk=blk, b=b: e.scalar_tensor_tensor(out=tf[b], in0=mixT[:, blk, :], scalar=gsf[:, blk:blk + 1], in1=rstd_e,
                                                                                 op0=ALU.mult, op1=ALU.mult), deps=[t_r2, t_gsf, merged_dead] + tf_free[b])
                h2_tok = P.op("act", lambda e, blk=blk, b=b: e.activation(out=h2T[:, blk, :], in_=tf[b], func=AF.Identity,
                                                                          bias=mod_fm[:, 48 + blk:49 + blk], scale=1.0),
                              deps=[t_a, mod_tok[15], t_r2])
            else:
                t_a = P.op("pool", lambda e, blk=blk, b=b: e.tensor_tensor(out=tf[b], in0=mixT[:, blk, :], in1=rstd_e, op=ALU.mult),
                           deps=[t_r2, merged_dead] + tf_free[b])
                h2_tok = P.op("act", lambda e, blk=blk, b=b: e.activation(out=h2T[:, blk, :], in_=tf[b], func=AF.Identity,
                                                                          bias=mod_fm[:, 48 + blk:49 + blk], scale=gsf[:, blk:blk + 1]),
                              deps=[t_a, mod_tok[15], t_r2, t_gsf])
            tf_free[b] = [h2_tok]
        h2_ready = P.op("dve", lambda e: e.tensor_scalar(out=h2T[:, :, 0:2], in0=h2T[:, :, 0:2], scalar1=hmask[:, 0:1], scalar2=None, op0=ALU.mult),
                        deps=[h2_tok, ldc_all])
        mid_done = all_ad()

        stg = [SC[:, i * 1026:(i + 1) * 1026] for i in range(2)]
        cb = [SC[:, 2052 + i * 1024:2052 + (i + 1) * 1024] for i in range(2)]
        sqy = [SC[:, i * 512:(i + 1) * 512].bitcast(BF16) for i in range(2)]
        stg_free = [[], []]
        cb_free = [[], []]
        sqy_free = [[], []]
        nblk = [0]
        y_ready = [None] * KC
        act_free = []
        conv_last = [None]
        ystat_tok = [None]
        for k in range(NG):
            act_ready = [None] * GB
            for half in (0, 1):
                done = 0
                for sp_ in ffn_up_specs(k, half):
                    s, wt, t_w = load_piece(*sp_)
                    pn = sp_[4]
                    for bl in range(pn // 128):
                        m = done
                        done += 1
                        mg = k * GB + m
                        i, tok = fm_block([wt[:, kc, bl * 128:(bl + 1) * 128] for kc in range(KC)],
                                          lambda kc, c0, n: h2T[:, kc, c0:c0 + n], PCS_E, [t_w, h2_ready])
                        b = nblk[0] % 2
                        nblk[0] += 1
                        cwc = V_CW + (half * 44 + mg) * 3
                        cbc = V_CB + half * 44 + mg
                        t_st = P.op("act", lambda e, i=i, b=b: e.activation(out=v3(stg[b], 3), in_=blk_view(i, 342), func=AF.Copy),
                                    deps=[tok] + stg_free[b] + mid_done)
                        set_blk_free(i, [t_st])
                        t_c0 = P.op("act", lambda e, b=b, cwc=cwc, cbc=cbc: e.activation(out=cb[b], in_=stg[b][:, 2:NE], func=AF.Identity,
                                                                                        bias=vec[:, cbc:cbc + 1], scale=vec[:, cwc + 2:cwc + 3]),
                                    deps=[t_st, ldc_all] + cb_free[b])
                        t_c1 = P.op("dve", lambda e, b=b, cwc=cwc: e.scalar_tensor_tensor(out=cb[b], in0=stg[b][:, 1:NE - 1], scalar=vec[:, cwc + 1:cwc + 2], in1=cb[b],
                                                                                          op0=ALU.mult, op1=ALU.add), deps=[t_c0])
                        t_c2 = P.op("dve", lambda e, b=b, cwc=cwc: e.scalar_tensor_tensor(out=cb[b], in0=stg[b][:, 0:NE - 2], scalar=vec[:, cwc:cwc + 1], in1=cb[b],
                                                                                          op0=ALU.mult, op1=ALU.add), deps=[t_c1])
                        stg_free[b] = [t_c2]
                        if half == 0:
                            t_o = P.op("act", lambda e, b=b, m=m: e.activation(out=act[:, m, :], in_=cb[b], func=AF.Gelu_apprx_tanh),
                                       deps=[t_c2] + act_free + mid_done)
                        else:
                            t_o = P.op("dve", lambda e, b=b, m=m: e.tensor_tensor(out=act[:, m, :], in0=act[:, m, :], in1=cb[b], op=ALU.mult),
                                       deps=[t_c2, act_ready[m]])
                        cb_free[b] = [t_o]
                        act_ready[m] = t_o
                        conv_last[0] = t_o
                    ring_free[s].append(tok)
            act_all = list(act_ready)
            last_up_pe = tok
            last_pe = None
            last_group = (k == NG - 1)
            if last_group:
                xm_bufs = [bview(RD, i * 4096, 4096, F32) for i in range(9)] + [bview(RC, 22528 + i * 4096, 4096, F32) for i in range(2)] \
                    + [SC[:, 1026:2050], SC[:, 3080:4104]]
                NPF = len(xm_bufs)
                xm_free = [None] * NPF
                for blk in range(NPF):
                    P.dma("sp", lambda e, blk=blk: e.dma_start(out=xm_bufs[blk], in_=outT[blk * 128:(blk + 1) * 128, :]), "xin",
                          deps=[last_up_pe, conv_last[0], spill_all])
                xm_pref = ("xin", P.cnt["xin"])
            for q in range(4):
                s, wt, t_w = load_piece(w_down, k * GB, GB, q * 512, 512)
                for ob in range(4):
                    o = q * 4 + ob
                    i, tok = fm_block([wt[:, kc, ob * 128:(ob + 1) * 128] for kc in range(GB)],
                                      lambda kc, c0, n: act[:, kc, c0:c0 + n], PCS_T, [t_w], kdeps=act_all)
                    src = ps[:, 3 * i * 512:(3 * i + 2) * 512]
                    if k == 0:
                        if o % 2:
                            t_y = P.op("dve", lambda e, o=o, src=src: e.tensor_copy(out=yT[:, o, :], in_=src), deps=[tok, spill_all] + mid_done)
                        else:
                            t_y = P.op("act", lambda e, o=o, src=src: e.activation(out=yT[:, o, :], in_=src, func=AF.Copy), deps=[tok, spill_all] + mid_done)
                    else:
                        t_y = P.op("dve", lambda e, o=o, src=src: e.tensor_tensor(out=yT[:, o, :], in0=src, in1=yT[:, o, :], op=ALU.add),
                                   deps=[tok, y_ready[o]])
                    set_blk_free(i, [t_y])
                    y_ready[o] = t_y
                    last_pe = tok
                    if last_group:
                        b2 = o % 2
                        t_sq = P.op("act", lambda e, o=o, b2=b2: e.activation(out=sqy[b2], in_=yT[:, o, :], func=AF.Square),
                                    deps=[t_y, conv_last[0]] + sqy_free[b2])

                        def hook(o=o, b2=b2, t_sq=t_sq):
                            tk = None
                            for pi in range(2):
                                tk = P.op("pe", lambda e, pi=pi, b2=b2, o=o: e.matmul(bank(6 + pi), lhsT=ones_bf[:], rhs=sqy[b2][:, pi * 512:(pi + 1) * 512],
                                                                                      start=(o == 0), stop=(o == KC - 1)),
                                          deps=[t_sq] + (bank_free[6] + bank_free[7] if o == 0 else []), signal=(pi == 1))
                            sqy_free[b2] = [tk]
                            ystat_tok[0] = tk
                        pe_hooks.append(hook)
                ring_free[s].append(tok)
                if mod_queue and ((q == 0 and k < 3) or (k == 0 and q == 2)):
                    mod_piece(mod_queue.pop(0))
            act_free = [last_pe]
        for h in list(pe_hooks):
            h()
        del pe_hooks[:]

        rstd_y = SC[:, 2052:2052 + 1024]
        t_r3 = P.op("act", lambda e: e.activation(out=rstd_y, in_=ps[:, 3072:4096], func=AF.Sqrt, bias=epst[:], scale=1.0 / D),
                    deps=[ystat_tok[0], conv_last[0]])
        t_r3 = P.op("dve", lambda e: e.reciprocal(out=rstd_y, in_=rstd_y), deps=[t_r3])
        t_ggf = P.op("dve", lambda e: e.tensor_tensor(out=ggf, in0=mod_fm[:, 80:96], in1=vec[:, V_GQF:V_GQF + 16], op=ALU.mult),
                     deps=[mod_tok[23], ldc_all])
        fin = None
        for blk in range(KC):
            t_a = P.op("dve", lambda e, blk=blk: e.scalar_tensor_tensor(out=yT[:, blk, :], in0=yT[:, blk, :], scalar=ggf[:, blk:blk + 1], in1=rstd_y,
                                                                        op0=ALU.mult, op1=ALU.mult), deps=[t_r3, t_ggf])
            if blk < NPF:
                xb, t_x = xm_bufs[blk], xm_pref
            else:
                xb = xm_bufs[blk - NPF]
                t_x = P.dma("sp", lambda e, blk=blk, xb=xb: e.dma_start(out=xb, in_=outT[blk * 128:(blk + 1) * 128, :]), "fx%d" % (blk - NPF),
                            deps=[xm_free[blk - NPF]])
            t_o = P.op("dve", lambda e, blk=blk, xb=xb: e.tensor_tensor(out=yT[:, blk, :], in0=yT[:, blk, :], in1=xb, op=ALU.add), deps=[t_a, t_x])
            if blk < KC - NPF:
                xm_free[blk] = t_o
            fin = P.dma("sp" if blk % 2 == 0 else "act", lambda e, blk=blk: e.dma_start(out=outT[blk * 128:(blk + 1) * 128, :], in_=yT[:, blk, :]), "fin",
                        deps=[t_o, xm_pref])
        P.wait("sp", [("fin", P.cnt["fin"])])
        if DEBUG:
            P.wait("sp", [("dbg", P.cnt["dbg"])])
        P.emit(block, sems)
    return nc


_NC_CACHE = {}


def _pool_mats(first_core):
    A = np.zeros((4, 4, 128, 128), np.float32)
    for gi, win in enumerate(POOL_WINDOWS):
        for t in range(128):
            for j in range(t - win + 1, t + 1):
                if j >= 0:
                    A[2, gi, j, t] += 1.0 / win
                else:
                    A[3, gi, 128 + j, t] += 1.0 / win
            A[2, gi, t, t] -= 1.0
            cnt = min(t + 1, win)
            for j in range(max(0, t - win + 1), t + 1):
                A[0, gi, j, t] += 1.0 / cnt
            A[0, gi, t, t] -= 1.0
    if not first_core:
        A[0] = A[2]
        A[1] = A[3]
    return np.ascontiguousarray(A.transpose(2, 0, 1, 3).reshape(128, 16 * 128))


def kernel(x, c, w_ada, b_ada, g_pre_mix, g_post_mix, w_in, ln_v_g, ln_v_b, w_spatial, b_spatial,
           w_pool, pool_scale, w_branch_a, w_branch_b, w_out, g_pre_ffn, g_post_ffn, w_up, conv_w,
           conv_b, w_down):
    f = np.float32
    x = np.asarray(x, f)
    S = x.shape[1]
    assert S == NCORE * TOK

    def fm(v, nblk):
        return np.asarray(v, f).reshape(nblk, 128).T

    xs = x[0]
    xpad = np.concatenate([np.zeros((128, D), f), xs], axis=0)
    vec_fm = np.concatenate([
        fm(g_pre_mix[0], 16), fm(g_post_mix[0], 16), fm(g_pre_ffn[0], 16), fm(g_post_ffn[0], 16),
        fm(pool_scale[0], 8),
        np.asarray(conv_w[0], f).T.reshape(88, 128, 3).transpose(1, 0, 2).reshape(128, 264),
        fm(conv_b[0], 88)], axis=1)
    vec_fm = np.ascontiguousarray(vec_fm, f)
    assert vec_fm.shape == (128, NV)
    common = {
        "c_col": np.ascontiguousarray(fm(c[0], 16)),
        "w_ada": np.ascontiguousarray(w_ada[0], f),
        "b_ada_fm": np.ascontiguousarray(fm(b_ada[0], 96)),
        "vec_fm": vec_fm,
        "ln_rows": np.ascontiguousarray(np.stack([ln_v_g[0], ln_v_b[0]]), f),
        "bs_exp": np.ascontiguousarray(np.repeat(np.asarray(b_spatial[0], f), 2, axis=0).reshape(1, 2048)),
        "wsT": np.ascontiguousarray(np.asarray(w_spatial[0], f).transpose(2, 0, 1).reshape(128, 1024)),
        "tri": np.ascontiguousarray(np.triu(np.ones((128, 128), f))),
        "wpool": np.ascontiguousarray(np.asarray(w_pool[0], f).reshape(4, 2, 128, 256).transpose(2, 0, 1, 3).reshape(128, 2048)),
        "w_in": np.ascontiguousarray(w_in[0], f),
        "w_branch_a": np.ascontiguousarray(w_branch_a[0], f),
        "w_branch_b": np.ascontiguousarray(w_branch_b[0], f),
        "w_out": np.ascontiguousarray(w_out[0], f),
        "w_up": np.ascontiguousarray(w_up[0], f),
        "w_down": np.ascontiguousarray(w_down[0], f),
    }
    in_maps = []
    for core in range(NCORE):
        m = dict(common)
        m["xT"] = np.ascontiguousarray(xpad[core * TOK:core * TOK + TM].T)
        m["poolA"] = _pool_mats(core == 0)
        m["hmask"] = np.full((128, 1), 0.0 if core == 0 else 1.0, f)
        in_maps.append(m)
    if "nc" not in _NC_CACHE:
        _NC_CACHE["nc"] = build_program()
    res = run_bass_kernel_spmd(_NC_CACHE["nc"], in_maps, core_ids=list(range(NCORE)))
    if DEBUG:
        _NC_CACHE["dbg"] = [{k: np.asarray(v) for k, v in r.items() if k.startswith("dbg_")} for r in res.results]
    outs = [np.asarray(r["outT"], f) for r in res.results]
    full = np.concatenate(outs, axis=1).T
    return np.ascontiguousarray(full[None], f)
```
